# Optimizing a Trainium2 kernel written in Bass

```python
import math
import jax, jax.numpy as jnp
from jax import lax
import numpy as np

D_MODEL = 2048
BATCH = 2
SEQ = 8192
DEPTH = 1
DEC_BATCH = 16
DEC_SEQ = 64
PAST_LEN = 2048

CHUNK = 64
HEAD_DIM = 64
CONV_WIDTH = D_MODEL // 2
CONV_K = 3
N_HEADS = (D_MODEL - CONV_WIDTH) // HEAD_DIM
N_KV_HEADS = 4
GROUP = N_HEADS // N_KV_HEADS
ATTN_WIDTH = N_HEADS * HEAD_DIM
KV_WIDTH = N_KV_HEADS * HEAD_DIM
MIX_WIDTH = CONV_WIDTH + ATTN_WIDTH
IN_COLS = 3 * CONV_WIDTH + ATTN_WIDTH + 2 * KV_WIDTH
WINDOW = 128
WINDOW_CHUNKS = WINDOW // CHUNK
BAND = (WINDOW_CHUNKS + 1) * CHUNK
NUM_BUCKETS = 32
MAX_DISTANCE = 128
D_FF = -(-8 * D_MODEL // (3 * 256)) * 256
EPS = 1e-6
NEG = -1e30

kernel_name = "hybrid_conv_swa_streaming_step"


def rmsnorm(x, g):
    xf = x.astype(jnp.float32)
    y = xf * lax.rsqrt(jnp.mean(xf * xf, axis=-1, keepdims=True) + EPS) * g.astype(jnp.float32)
    return y.astype(x.dtype)


def t5_bucket(rel):
    half = NUM_BUCKETS // 2
    ret = jnp.where(rel > 0, half, 0)
    n = jnp.abs(rel)
    max_exact = half // 2
    nf = jnp.maximum(n, 1).astype(jnp.float32)
    large = max_exact + (jnp.log(nf / max_exact) / math.log(MAX_DISTANCE / max_exact)
                         * (half - max_exact)).astype(jnp.int32)
    large = jnp.minimum(large, half - 1)
    return ret + jnp.where(n < max_exact, n, large)


def rel_bias(table, qpos, kpos):
    b = t5_bucket(kpos[None, :] - qpos[:, None])
    bias = jnp.transpose(table[b].astype(jnp.float32), (2, 0, 1))
    return bias.reshape(N_KV_HEADS, GROUP, qpos.shape[0], kpos.shape[0])


def split_proj(z):
    offs = np.cumsum([CONV_WIDTH, CONV_WIDTH, CONV_WIDTH, ATTN_WIDTH, KV_WIDTH]).tolist()
    return jnp.split(z, offs, axis=-1)


def causal_conv(gp, conv_w):
    L = gp.shape[1] - (CONV_K - 1)
    out = conv_w[0] * gp[:, 0:L]
    for i in range(1, CONV_K):
        out = out + conv_w[i] * gp[:, i:i + L]
    return out


def sink_attention(q, k, v, bias, key_valid, sinks):
    s = jnp.einsum('bnqkgd,bnskd->bnkgqs', q, k).astype(jnp.float32) * (HEAD_DIM ** -0.5)
    s = s + bias[None, None]
    s = jnp.where(key_valid[None, :, None, None, None, :], s, NEG)
    sink = sinks.astype(jnp.float32).reshape(N_KV_HEADS, GROUP)[None, None, :, :, None, None]
    m = jnp.maximum(jnp.max(s, axis=-1, keepdims=True), sink)
    p = jnp.exp(s - m)
    p = p / (jnp.sum(p, axis=-1, keepdims=True) + jnp.exp(sink - m))
    return jnp.einsum('bnkgqs,bnskd->bnqkgd', p.astype(v.dtype), v)


def mix_prompt(h, w_in, conv_w, sinks, rel_table):
    b, L, _ = h.shape
    cb, cc, cu, q, k, v = split_proj(h @ w_in)
    g = cc * cu
    gp = jnp.pad(g, ((0, 0), (CONV_K - 1, 0), (0, 0)))
    conv_out = cb * causal_conv(gp, conv_w)
    nc = L // CHUNK
    pad = WINDOW_CHUNKS * CHUNK
    q = q.reshape(b, nc, CHUNK, N_KV_HEADS, GROUP, HEAD_DIM)
    k = k.reshape(b, L, N_KV_HEADS, HEAD_DIM)
    v = v.reshape(b, L, N_KV_HEADS, HEAD_DIM)
    kc = jnp.pad(k, ((0, 0), (pad, 0), (0, 0), (0, 0))).reshape(b, nc + WINDOW_CHUNKS, CHUNK, N_KV_HEADS, HEAD_DIM)
    vc = jnp.pad(v, ((0, 0), (pad, 0), (0, 0), (0, 0))).reshape(b, nc + WINDOW_CHUNKS, CHUNK, N_KV_HEADS, HEAD_DIM)
    kb = jnp.concatenate([kc[:, i:i + nc] for i in range(WINDOW_CHUNKS + 1)], axis=2)
    vb = jnp.concatenate([vc[:, i:i + nc] for i in range(WINDOW_CHUNKS + 1)], axis=2)
    qpos = jnp.arange(CHUNK)
    kpos = jnp.arange(BAND) - pad
    bias = rel_bias(rel_table, qpos, kpos)
    valid = (jnp.arange(nc)[:, None] * CHUNK + kpos[None, :]) >= 0
    o = sink_attention(q, kb, vb, bias, valid, sinks)
    attn_out = o.reshape(b, L, ATTN_WIDTH)
    mixed = jnp.concatenate([conv_out, attn_out], axis=-1)
    return mixed, g[:, L - (CONV_K - 1):], k[:, L - WINDOW:], v[:, L - WINDOW:]


def mix_sample(h, conv_state, k_cache, v_cache, w_in, conv_w, sinks, rel_table):
    b, L, _ = h.shape
    cb, cc, cu, q, k, v = split_proj(h @ w_in)
    g = cc * cu
    gp = jnp.concatenate([conv_state.astype(g.dtype), g], axis=1)
    conv_out = cb * causal_conv(gp, conv_w)
    lc = k_cache.shape[1]
    k = k.reshape(b, L, N_KV_HEADS, HEAD_DIM)
    v = v.reshape(b, L, N_KV_HEADS, HEAD_DIM)
    kf = jnp.concatenate([k_cache.astype(k.dtype), k], axis=1)
    vf = jnp.concatenate([v_cache.astype(v.dtype), v], axis=1)
    qpos = PAST_LEN + jnp.arange(L)
    kpos = PAST_LEN - lc + jnp.arange(lc + L)
    bias = rel_bias(rel_table, qpos, kpos)
    valid = (kpos >= 0)[None, :]
    o = sink_attention(q.reshape(b, 1, L, N_KV_HEADS, GROUP, HEAD_DIM), kf[:, None], vf[:, None],
                       bias, valid, sinks)
    attn_out = o.reshape(b, L, ATTN_WIDTH)
    mixed = jnp.concatenate([conv_out, attn_out], axis=-1)
    n_all = lc + L
    return mixed, gp[:, L:], kf[:, n_all - lc:], vf[:, n_all - lc:]


def ffn(x, w_gate, w_up, w_down):
    return (jax.nn.silu(x @ w_gate) * (x @ w_up)) @ w_down


def setup_inputs(seed: int = 0) -> dict:
    key = jax.random.key(seed)
    ks = jax.random.split(key, 20)
    f32 = jnp.float32
    lc = min(WINDOW, PAST_LEN)

    def nrm(k, shape, scale):
        return jax.random.normal(k, shape, f32) * scale

    def gain(k):
        return 1.0 + 0.05 * jax.random.normal(k, (DEPTH, D_MODEL), f32)

    return {
        "x_prompt": nrm(ks[0], (BATCH, SEQ, D_MODEL), 1.0),
        "x_sample": nrm(ks[1], (DEC_BATCH, DEC_SEQ, D_MODEL), 1.0),
        "state_conv": nrm(ks[2], (DEPTH, DEC_BATCH, CONV_K - 1, CONV_WIDTH), 1.0),
        "cache_k": nrm(ks[3], (DEPTH, DEC_BATCH, lc, N_KV_HEADS, HEAD_DIM), 1.0),
        "cache_v": nrm(ks[4], (DEPTH, DEC_BATCH, lc, N_KV_HEADS, HEAD_DIM), 1.0),
        "rel_table": nrm(ks[5], (NUM_BUCKETS, N_HEADS), 0.5),
        "g_pre_mix": gain(ks[6]),
        "w_in": nrm(ks[7], (DEPTH, D_MODEL, IN_COLS), D_MODEL ** -0.5),
        "conv_w": nrm(ks[8], (DEPTH, CONV_K, CONV_WIDTH), CONV_K ** -0.5),
        "attn_sinks": nrm(ks[9], (DEPTH, N_HEADS), 0.5),
        "w_out": nrm(ks[10], (DEPTH, MIX_WIDTH, D_MODEL), MIX_WIDTH ** -0.5),
        "g_post_mix": gain(ks[11]),
        "g_pre_ffn": gain(ks[12]),
        "w_gate": nrm(ks[13], (DEPTH, D_MODEL, D_FF), D_MODEL ** -0.5),
        "w_up": nrm(ks[14], (DEPTH, D_MODEL, D_FF), D_MODEL ** -0.5),
        "w_down": nrm(ks[15], (DEPTH, D_FF, D_MODEL), D_FF ** -0.5),
        "g_post_ffn": gain(ks[16]),
    }


def reference(x_prompt, x_sample, state_conv, cache_k, cache_v, rel_table, g_pre_mix, w_in,
              conv_w, attn_sinks, w_out, g_post_mix, g_pre_ffn, w_gate, w_up, w_down, g_post_ffn):
    yp, ys = x_prompt, x_sample
    pc, pk, pv, sc, sk, sv = [], [], [], [], [], []
    for l in range(DEPTH):
        m, c_new, k_new, v_new = mix_prompt(rmsnorm(yp, g_pre_mix[l]), w_in[l], conv_w[l],
                                            attn_sinks[l], rel_table)
        yp = yp + rmsnorm(m @ w_out[l], g_post_mix[l])
        yp = yp + rmsnorm(ffn(rmsnorm(yp, g_pre_ffn[l]), w_gate[l], w_up[l], w_down[l]), g_post_ffn[l])
        pc.append(c_new); pk.append(k_new); pv.append(v_new)
        m, c_new, k_new, v_new = mix_sample(rmsnorm(ys, g_pre_mix[l]), state_conv[l], cache_k[l],
                                            cache_v[l], w_in[l], conv_w[l], attn_sinks[l], rel_table)
        ys = ys + rmsnorm(m @ w_out[l], g_post_mix[l])
        ys = ys + rmsnorm(ffn(rmsnorm(ys, g_pre_ffn[l]), w_gate[l], w_up[l], w_down[l]), g_post_ffn[l])
        sc.append(c_new); sk.append(k_new); sv.append(v_new)
    new_conv_prompt = jnp.stack(pc)
    new_k_prompt = jnp.stack(pk)
    new_v_prompt = jnp.stack(pv)
    new_conv_sample = jnp.stack(sc)
    new_k_sample = jnp.stack(sk)
    new_v_sample = jnp.stack(sv)
    return (yp, ys, new_conv_prompt, new_k_prompt, new_v_prompt, new_conv_sample, new_k_sample, new_v_sample)
```

```python
import math
import contextlib
import numpy as np
import concourse.bass as bass
import concourse.mybir as mybir
from concourse.bass_utils import run_bass_kernel_spmd

F32 = mybir.dt.float32
BF16 = mybir.dt.bfloat16
AF = mybir.ActivationFunctionType
ALU = mybir.AluOpType

D = 2048
KC = 16
CW = 1024
NH = 16
NKV = 4
HD = 64
DFF = 5632
FC = 44
IN_COLS = 4608
NCORES = 8
PTOK = 2048
HALO = 128
STOK = 128
NTOK = HALO + PTOK + STOK
NOUT = PTOK + STOK
CH = 64
BLOCKS = [6, 7, 7, 7, 7]
TMAX = 448
EPS = 1e-6
NS = 5
NBUCK = 32

T_K = 0
T_V = 2
T_Q = 4
T_CONV = 12
T_OUT = 36
T_GU = 52
T_DN = 140
NT = 188
DN_PIECES = [(0, 16), (16, 16), (32, 12)]

ENGS = ("pe", "act", "dve", "pool", "sp")
LATE_RSTD = True
SINK_MM = True
VEC2 = "dve"
POOL_H = True
POOL_RES = True
POOL_EB = True
POOL_TAPS = True
ACT_RECIP = True
SCRATCH = False
IOQ = "pool" if SCRATCH else "sp"
PREFETCH = True
import os
DBG = set(os.environ.get("KDBG", "").split(","))


class _Ins:
    __slots__ = ("eng", "idx", "fn", "deps", "dma_key", "signals", "signum")

    def __init__(self, eng, idx, fn, deps, dma_key):
        self.eng = eng
        self.idx = idx
        self.fn = fn
        self.deps = deps
        self.dma_key = dma_key
        self.signals = False
        self.signum = None


class Prog:
    def __init__(self, group_keys=()):
        self.streams = {e: [] for e in ENGS}
        self.last_writer = {}
        self.readers = {}
        self.dma_keys = []
        self.group_keys = set(group_keys)

    def op(self, eng, fn, reads=(), writes=(), dma_key=None):
        deps = set()
        for r in reads:
            lw = self.last_writer.get(r)
            if lw is not None:
                deps.add(lw)
        for w in writes:
            lw = self.last_writer.get(w)
            if lw is not None:
                deps.add(lw)
            for rd in self.readers.get(w, {}).values():
                deps.add(rd)
        idx = len(self.streams[eng])
        me = (eng, idx)
        deps.discard(me)
        if eng == "pe":
            deps = {d for d in deps if d[0] != "pe"}
        ins = _Ins(eng, idx, fn, deps, dma_key)
        self.streams[eng].append(ins)
        if dma_key is not None and dma_key not in self.dma_keys:
            self.dma_keys.append(dma_key)
        for r in reads:
            rk = eng if eng in ("pe", "act", "dve") else me
            self.readers.setdefault(r, {})[rk] = me
        for w in writes:
            self.last_writer[w] = me
            self.readers[w] = {}
        return ins

    def emit(self, nc, final_wait_eng="sp"):
        streams = self.streams
        fin_deps = set()
        for e in ENGS:
            for ins in streams[e]:
                if ins.dma_key is not None:
                    fin_deps.add((e, ins.idx))
            if streams[e] and e != final_wait_eng:
                fin_deps.add((e, len(streams[e]) - 1))
        for e in ENGS:
            for ins in streams[e]:
                for (de, di) in ins.deps:
                    streams[de][di].signals = True
        for (de, di) in fin_deps:
            streams[de][di].signals = True
        dma_count = {}
        for e in ENGS:
            n = 0
            for ins in streams[e]:
                if ins.dma_key is not None:
                    c = dma_count.get(ins.dma_key, 0) + 1
                    dma_count[ins.dma_key] = c
                    ins.signum = 16 * c
                elif ins.signals:
                    n += 1
                    ins.signum = n
        self.sig_counts = {e: max([i.signum or 0 for i in streams[e] if i.dma_key is None] + [0]) for e in ENGS}
        self.sig_counts.update({str(k): 16 * v for k, v in dma_count.items()})
        for e in ENGS:
            for ins in streams[e]:
                if ins.dma_key in self.group_keys:
                    ins.signum = 16 * dma_count[ins.dma_key]
        with contextlib.ExitStack() as st:
            sem_eng = {e: st.enter_context(nc.semaphore("s_" + e)) for e in ENGS}
            sem_dma = {k: st.enter_context(nc.semaphore("d_%d" % i)) for i, k in enumerate(self.dma_keys)}
            block = st.enter_context(nc.Block())

            def sem_of(ins):
                return sem_dma[ins.dma_key] if ins.dma_key is not None else sem_eng[ins.eng]

            def waits_for(deps, known, engobj):
                need = {}
                for (de, di) in deps:
                    d = streams[de][di]
                    s = sem_of(d)
                    key = id(s)
                    if known.get(key, 0) >= d.signum:
                        continue
                    if key not in need or need[key][1] < d.signum:
                        need[key] = (s, d.signum)
                for key, (s, v) in need.items():
                    engobj.wait_ge(s, v)
                    known[key] = v

            def run_stream(e, engobj):
                known = {}
                for ins in streams[e]:
                    waits_for(ins.deps, known, engobj)
                    bi = ins.fn(engobj)
                    if ins.dma_key is not None:
                        bi.then_inc(sem_dma[ins.dma_key], 16)
                    elif ins.signals:
                        bi.then_inc(sem_eng[e], 1)
                if e == final_wait_eng:
                    waits_for(fin_deps, known, engobj)

            @block.tensor
            def _(eng):
                run_stream("pe", eng)

            @block.scalar
            def _(eng):
                run_stream("act", eng)

            @block.vector
            def _(eng):
                run_stream("dve", eng)

            @block.gpsimd
            def _(eng):
                run_stream("pool", eng)

            @block.sync
            def _(eng):
                run_stream("sp", eng)


def build_program(n_blocks=len(BLOCKS)):
    nc = bass.Bass("TRN2", target_bir_lowering=False)

    def din(name, shape, dt=F32):
        return nc.dram_tensor(name, list(shape), dt, kind="ExternalInput").ap()

    def dout(name, shape, dt=F32):
        return nc.dram_tensor(name, list(shape), dt, kind="ExternalOutput").ap()

    xT = din("xT", [D, NTOK])
    wst = din("wst", [NT, 128, 16, 128])
    cst = din("cst", [128, 112])
    tab = din("tab", [NBUCK, NH])
    ohd = din("oh", [128, 256])
    ckT = din("ckT", [128, 2, 2, 128])
    ckn = din("ckn", [2, 128, 256])
    cvn = din("cvn", [2, 128, 256])
    scv = din("scv", [128, 2, 8, 2])
    wbf = nc.dram_tensor("wbf", [NT, 128, 16, 128], BF16, kind="Internal").ap()

    yT = dout("yT", [D, NOUT])
    g_out = dout("g_out", [128, 8, 3, 2])
    ko_p = dout("ko_p", [128, 2, 128])
    vo_p = dout("vo_p", [64, 2, 256])
    ko_s = dout("ko_s", [128, 2, 2, 64])
    kc_copy = dout("kc_copy", [2, 64, 256])
    vs_out = dout("vs_out", [2, 128, 256])

    xT_v = xT.rearrange("(kc p) t -> p kc t", p=128)
    yT_v = yT.rearrange("(kc p) t -> p kc t", p=128)

    P = Prog(group_keys=("consts", "consts_sw", "outs"))

    with contextlib.ExitStack() as st:
        def sb(name, shape, dt):
            return st.enter_context(nc.sbuf_tensor(name, list(shape), dt))

        PS = st.enter_context(nc.psum_tensor("PS", [128, 4096], F32))
        xs = sb("xs", [128, KC, TMAX], F32)
        hT = sb("hT", [128, KC, 512], BF16)
        u = sb("u", [128, KC, TMAX], F32)
        act = sb("act", [128, FC, TMAX], BF16)
        kT = sb("kT", [128, 2, 16 * CH], BF16)
        Vr = sb("Vr", [64, 16, 256], BF16)
        kcb = sb("kcb", [128, 2, 2, 128], BF16)
        vcb = sb("vcb", [64, 2, 2, 256], BF16)
        wsl = [sb("w%d" % i, [128, 16, 128], BF16) for i in range(NS)]
        EB = sb("EB", [64, NKV, 3, 256], F32)
        cs = sb("cs", [128, 112], F32)
        sinkx = sb("sinkx", [64, NH], F32)
        sinkb = sb("sinkb", [1, NH], BF16)
        bd = sb("bd", [128, 64], F32)
        ohs = sb("ohs", [128, 256], F32)
        ones = sb("ones", [128, 128], BF16)
        gst = sb("gst", [128, 8, 2], F32)
        scs = sb("scs", [128, 2, 8, 2], F32)
        NTMP = 6
        tmp = [sb("tmp%d" % i, [128, 520], F32) for i in range(NTMP)]
        NSQ = 3
        sq = [sb("sq%d" % i, [128, 512], BF16) for i in range(NSQ)]
        Ebuf = [sb("E%d" % i, [64, 768], F32) for i in range(2)]
        PT = [sb("PT%d" % i, [64, 768], BF16) for i in range(2)]
        rec = [sb("rec%d" % i, [64, 256], F32) for i in range(2)]
        rstd = [sb("rstd%d" % i, [128, 512], F32) for i in range(2)]
        stg = [sb("stg%d" % i, [128, TMAX], F32) for i in range(4)]
        gout_s = sb("gout_s", [128, 8, 3, 2], F32)
        kop_s = sb("kop_s", [128, 2, 128], F32)
        kos_s = sb("kos_s", [128, 2, 2, 64], F32)
        vop_s = sb("vop_s", [64, 2, 256], F32)
        vos_s = sb("vos_s", [64, 2, 256], F32)

        cnt = {"tmp": 0, "sq": 0, "rstd": 0, "bank": 0}

        def next_tmp():
            i = cnt["tmp"] % NTMP
            cnt["tmp"] += 1
            return tmp[i], ("tmp", i)

        def next_sq():
            i = cnt["sq"] % NSQ
            cnt["sq"] += 1
            return sq[i], ("sq", i)

        def next_rstd():
            i = cnt["rstd"] % 2
            cnt["rstd"] += 1
            return rstd[i], ("rstd", i)

        def psk(c0, c1):
            return [("ps", h) for h in range(c0 // 512, (c1 + 511) // 512)]

        BANK_SSUM = 6
        BANK_V = 7

        def next_bank():
            b = cnt["bank"] % 6
            cnt["bank"] += 1
            return b

        ws = {"issued": 0, "req": 0}
        total_tiles = NT * n_blocks

        def tile_kcn(t):
            if t >= T_DN and (t - T_DN) % 3 == 2:
                return 12
            return 16

        def ws_ensure(upto):
            while ws["issued"] < min(upto, total_tiles):
                g = ws["issued"]
                pas, t = divmod(g, NT)
                slot = g % NS
                kcn = tile_kcn(t)
                dst = wsl[slot][:, 0:kcn, :]
                if pas == 0 or not SCRATCH:
                    first_reads = ([("xs", kc) for kc in range(KC)] + [("u", kc) for kc in range(KC)]) if g < NS else []
                    P.op("pool", lambda e, dst=dst, src=wst[t, :, 0:kcn, :]: e.dma_start(out=dst, in_=src),
                         reads=first_reads, writes=[("w", slot)], dma_key=("wl_sw", slot))
                    if n_blocks > 1 and SCRATCH:
                        P.op("sp", lambda e, dst=wbf[t, :, 0:kcn, :], src=dst: e.dma_start(out=dst, in_=src),
                             reads=[("w", slot)], writes=[("wbf", t)], dma_key=("wo", slot))
                else:
                    P.op("sp", lambda e, dst=dst, src=wbf[t, :, 0:kcn, :]: e.dma_start(out=dst, in_=src),
                         reads=[("wbf", t)], writes=[("w", slot)], dma_key=("wl", slot))
                ws["issued"] += 1

        def ws_next(expect_t):
            g = ws["req"]
            ws["req"] += 1
            assert g % NT == expect_t, (g % NT, expect_t)
            ws_ensure(g + NS)
            return g % NS

        P.op("sp", lambda e: e.dma_start(out=cs[:], in_=cst), writes=["cs"], dma_key="consts")
        P.op("dve", lambda e: e.memset(bd[:], 0.0), writes=[("bd", i) for i in range(4)])
        for ql in range(4):
            P.op("sp", lambda e, ql=ql: e.dma_start(out=bd[ql * 32:(ql + 1) * 32, ql * 16:(ql + 1) * 16], in_=tab),
                 writes=[("bd", ql)], dma_key="consts")
        P.op("sp", lambda e: e.dma_start(out=ohs[:], in_=ohd), writes=["ohs"], dma_key="consts")
        P.op("sp", lambda e: e.dma_start(out=scs[:], in_=scv), writes=["scs"], dma_key="consts")
        P.op("pool", lambda e: e.dma_start(out=kcb[:], in_=ckT), writes=["kcb"], dma_key="consts_sw")
        P.op("pool", lambda e: e.dma_start(out=vcb[:], in_=cvn.rearrange("s (j p) f -> p s j f", p=64)),
             writes=["vcb"], dma_key="consts_sw")
        P.op(IOQ, lambda e: e.dma_start(out=kc_copy, in_=ckn[:, 64:128, :]), dma_key="outs")
        P.op(IOQ, lambda e: e.dma_start(out=vs_out[:, 0:64, :], in_=cvn[:, 64:128, :]), dma_key="outs")
        P.op("dve", lambda e: e.memset(ones[:], 1.0), writes=["ones"])
        for i in range(NTMP):
            P.op("dve", lambda e, i=i: e.memset(tmp[i][:], 0.0), writes=[("tmp", i)])
        P.op("dve", lambda e: e.memset(gst[:], 0.0), writes=["gst"])
        P.op("act", lambda e: e.activation(out=sinkx[:], in_=cs[0:64, 89:105], func=AF.Exp),
             reads=["cs"], writes=["sinkx"])
        if SINK_MM:
            P.op("act", lambda e: e.activation(out=sinkb[0:1, :], in_=sinkx[0:1, :], func=AF.Copy),
                 reads=["sinkx"], writes=["sinkb"])
        for j in range(3):
            for qh in range(2):
                bank = next_bank()
                c0 = bank * 512
                for c2 in range(8):
                    X = j * 64 + 60 - 4 * (qh * 8 + c2)
                    P.op("pe", lambda e, c=c0 + c2 * 64, X=X: e.matmul(
                        PS[0:64, c:c + 64], ohs[:, X:X + 64], bd[:, :], start=True, stop=True),
                        reads=["ohs"] + [("bd", i) for i in range(4)], writes=psk(c0, c0 + 512))
                out_ap = EB[:, :, j, :].rearrange("p kh (g q) -> p kh g q", g=4)[:, :, :, qh * 32:(qh + 1) * 32] \
                    .rearrange("p kh g q -> p q kh g")
                in_ap = PS[0:64, c0:c0 + 512].rearrange("p (q kh g) -> p q kh g", q=32, kh=4)
                P.op("act", lambda e, o=out_ap, i=in_ap: e.activation(out=o, in_=i, func=AF.Exp),
                     reads=psk(c0, c0 + 512), writes=["EB"])

        def gain_ap(gi, kc):
            return cs[:, gi * 16 + kc: gi * 16 + kc + 1]

        def convw_ap(tap, j):
            return cs[:, 64 + tap * 8 + j: 64 + tap * 8 + j + 1]

        def ssum_mm(sq_t, sq_key, n, first, last, bank=BANK_SSUM):
            P.op("pe", lambda e, r=sq_t[:, 0:n]: e.matmul(PS[:, bank * 512: bank * 512 + n], ones[:, :], r,
                                                           start=first, stop=last),
                 reads=["ones", sq_key], writes=psk(bank * 512, bank * 512 + 512))

        def finish_rstd(n, bank=BANK_SSUM):
            r_t, r_key = next_rstd()
            if ACT_RECIP:
                P.op("act", lambda e: e.activation(out=r_t[:, 0:n], in_=PS[:, bank * 512: bank * 512 + n],
                                                   func=AF.Ln, scale=1.0 / D, bias=EPS),
                     reads=psk(bank * 512, bank * 512 + 512), writes=[r_key])
                P.op("act", lambda e: e.activation(out=r_t[:, 0:n], in_=r_t[:, 0:n], func=AF.Exp, scale=-0.5),
                     reads=[r_key], writes=[r_key])
            else:
                P.op("act", lambda e: e.activation(out=r_t[:, 0:n], in_=PS[:, bank * 512: bank * 512 + n],
                                                   func=AF.Sqrt, scale=1.0 / D, bias=EPS),
                     reads=psk(bank * 512, bank * 512 + 512), writes=[r_key])
                P.op("dve", lambda e: e.reciprocal(out=r_t[:, 0:n], in_=r_t[:, 0:n]), reads=[r_key], writes=[r_key])
            return r_t, r_key

        def apply_h(out_ap, out_key, x_ap, x_key, g_ap, r_ap, r_key, n, use_pool):
            if not use_pool:
                P.op("dve", lambda e: e.scalar_tensor_tensor(out=out_ap, in0=x_ap, scalar=g_ap, in1=r_ap,
                                                             op0=ALU.mult, op1=ALU.mult),
                     reads=[x_key, r_key, "cs"], writes=[out_key])
            else:
                t_t, t_key = next_tmp()
                P.op("act", lambda e: e.mul(out=t_t[:, 0:n], in_=x_ap, mul=g_ap), reads=[x_key, "cs"], writes=[t_key])
                P.op(VEC2, lambda e: e.tensor_tensor(out=out_ap, in0=t_t[:, 0:n], in1=r_ap, op=ALU.mult),
                     reads=[t_key, r_key], writes=[out_key])

        def norm_to_h(src, src_key_fn, n, gi, h_off):
            for kc in range(KC):
                s_t, s_key = next_sq()
                P.op("act", lambda e, kc=kc, s_t=s_t: e.activation(out=s_t[:, 0:n], in_=src[:, kc, 0:n], func=AF.Square),
                     reads=[src_key_fn(kc)], writes=[s_key])
                ssum_mm(s_t, s_key, n, kc == 0, kc == KC - 1)
            r_t, r_key = finish_rstd(n)
            for kc in range(KC):
                apply_h(hT[:, kc, h_off:h_off + n], ("hT", kc), src[:, kc, 0:n], src_key_fn(kc), gain_ap(gi, kc),
                        r_t[:, 0:n], r_key, n, POOL_H)

        def mm_group(out_ap, out_keys, slot, rhs_fn, rhs_keys_fn, nk, first=True, last=True, k_off=0):
            for kk in range(nk):
                rhs = rhs_fn(k_off + kk)
                P.op("pe", lambda e, kk=kk, rhs=rhs: e.matmul(out_ap, wsl[slot][:, kk, :], rhs,
                                                              start=(first and kk == 0), stop=(last and kk == nk - 1)),
                     reads=[("w", slot), rhs_keys_fn(k_off + kk)], writes=out_keys)

        def residual_phase(n, gi, src_is_u=True):
            r_t, r_key = finish_rstd(n)
            for oc in range(KC):
                t_t, t_key = next_tmp()
                P.op(VEC2 if (POOL_RES and oc % 2 == 1) else "dve", lambda e, oc=oc, t_t=t_t: e.tensor_tensor(
                    out=t_t[:, 0:n], in0=u[:, oc, 0:n], in1=r_t[:, 0:n], op=ALU.mult),
                    reads=[("u", oc), r_key], writes=[t_key])
                P.op("dve", lambda e, oc=oc, t_t=t_t: e.tensor_tensor(
                    out=xs[:, oc, 0:n], in0=xs[:, oc, 0:n], in1=t_t[:, 0:n], op=ALU.add),
                    reads=[("xs", oc), t_key], writes=[("xs", oc)])

        def out_proj_phase(n, tile_base, gi, rhs_fn, rhs_keys_fn, pieces, hook=None):
            pending = None
            for oc in range(KC):
                bank = next_bank()
                c0 = bank * 512
                o_ap = PS[:, c0:c0 + n]
                for pi, (k0, kn) in enumerate(pieces):
                    slot = ws_next(tile_base + oc * len(pieces) + pi)
                    mm_group(o_ap, psk(c0, c0 + 512), slot, rhs_fn, rhs_keys_fn, kn,
                             first=(pi == 0), last=(pi == len(pieces) - 1), k_off=k0)
                if pending is not None:
                    ssum_mm(*pending)
                    pending = None
                if hook is not None:
                    hook(oc)
                P.op("act", lambda e, oc=oc, o_ap=o_ap: e.mul(out=u[:, oc, 0:n], in_=o_ap, mul=gain_ap(gi, oc)),
                     reads=psk(c0, c0 + 512) + ["cs"], writes=[("u", oc)])
                s_t, s_key = next_sq()
                P.op("act", lambda e, s_t=s_t, o_ap=o_ap: e.activation(out=s_t[:, 0:n], in_=o_ap, func=AF.Square),
                     reads=psk(c0, c0 + 512), writes=[s_key])
                pending = (s_t, s_key, n, oc == 0, oc == KC - 1)
            ssum_mm(*pending)

        def do_block(b, chunk_base):
            nch = BLOCKS[b]
            T = nch * CH
            pre = HALO if b == 0 else 0
            tok0 = chunk_base * CH
            last_block = (b == len(BLOCKS) - 1) and "nolast" not in DBG
            if b == 0:
                P.op("sp", lambda e: e.dma_start(out=u[:, :, 0:HALO], in_=xT_v[:, :, 0:HALO]),
                     writes=[("u", kc) for kc in range(KC)], dma_key="halo")
            for qd in range(4):
                P.op(IOQ, lambda e, qd=qd: e.dma_start(out=xs[:, qd * 4:(qd + 1) * 4, 0:T],
                                                        in_=xT_v[:, qd * 4:(qd + 1) * 4, HALO + tok0: HALO + tok0 + T]),
                     writes=[("xs", kc) for kc in range(qd * 4, qd * 4 + 4)], dma_key=("xl", qd))
            if b == 0:
                norm_to_h(u, lambda kc: ("u", kc), HALO, 0, 0)
            if not (PREFETCH and b > 0):
                norm_to_h(xs, lambda kc: ("xs", kc), T, 0, pre)

            have_next = PREFETCH and (b + 1 < n_blocks)
            if have_next:
                Tn = BLOCKS[b + 1] * CH
                xoff_n = HALO + tok0 + T
            pf = {"rstd": None}

            NPF = 2 * KC

            def pf_issue(i):
                kc, si = i % KC, i % 4
                P.op(IOQ, lambda e: e.dma_start(out=stg[si][:, 0:Tn], in_=xT_v[:, kc, xoff_n:xoff_n + Tn]),
                     writes=[("stg", si)], dma_key=("stgl", si))

            def pf_process(i):
                kc, si = i % KC, i % 4
                if i < KC:
                    s_t, s_key = next_sq()
                    P.op("act", lambda e, s_t=s_t: e.activation(out=s_t[:, 0:Tn], in_=stg[si][:, 0:Tn], func=AF.Square),
                         reads=[("stg", si)], writes=[s_key])
                    ssum_mm(s_t, s_key, Tn, kc == 0, kc == KC - 1, bank=7)
                    if kc == KC - 1:
                        pf["rstd"] = finish_rstd(Tn, bank=7)
                else:
                    r_t, r_key = pf["rstd"]
                    apply_h(hT[:, kc, 0:Tn], ("hT", kc), stg[si][:, 0:Tn], ("stg", si), gain_ap(0, kc),
                            r_t[:, 0:Tn], r_key, Tn, POOL_H)

            pf_state = {"done": 0}

            def pf_hook(oc):
                if not have_next:
                    return
                if oc == 0:
                    for i in range(4):
                        pf_issue(i)
                    return
                target = min(NPF, -(-oc * NPF // 12))
                while pf_state["done"] < target:
                    i = pf_state["done"]
                    pf_process(i)
                    if i + 4 < NPF:
                        pf_issue(i + 4)
                    pf_state["done"] += 1

            NW = pre + T

            def h_ctx(kc):
                return hT[:, kc, 0:NW]

            def h_main(kc):
                return hT[:, kc, pre:pre + T]

            def h_key(kc):
                return ("hT", kc)

            ctx_chunks = []
            if b == 0:
                ctx_chunks += [(0, 0), (1, 64)]
            for ci in range(nch):
                ctx_chunks.append(((chunk_base + ci + 2) % 16, pre + ci * CH))

            for kp in range(2):
                slot = ws_next(T_K + kp)
                bank = next_bank()
                c0 = bank * 512
                mm_group(PS[:, c0:c0 + NW], psk(c0, c0 + 512), slot, h_ctx, h_key, KC)
                for (cslot, col) in ctx_chunks:
                    P.op("act", lambda e, kp=kp, cslot=cslot, col=col, c0=c0: e.activation(
                        out=kT[:, kp, cslot * CH:(cslot + 1) * CH], in_=PS[:, c0 + col:c0 + col + CH], func=AF.Copy),
                        reads=psk(c0, c0 + 512), writes=[("kT", kp, cslot)])
                if last_block:
                    pc = (nch - 4) * CH
                    P.op("act", lambda e, kp=kp, c0=c0, pc=pc: e.activation(out=kop_s[:, kp, :], in_=PS[:, c0 + pc:c0 + pc + 128], func=AF.Copy),
                         reads=psk(c0, c0 + 512), writes=["kop_s"])
                    sc = (nch - 2) * CH
                    P.op("act", lambda e, kp=kp, c0=c0, sc=sc: e.activation(
                        out=kos_s[:, kp, :, :], in_=PS[:, c0 + sc:c0 + sc + 128].rearrange("p (s t) -> p s t", s=2), func=AF.Copy),
                        reads=psk(c0, c0 + 512), writes=["kos_s"])
            for vt in range(2):
                slot = ws_next(T_V + vt)
                for k_i, (cslot, col) in enumerate(ctx_chunks):
                    vc0 = (6 + k_i % 2) * 512
                    for kc in range(KC):
                        P.op("pe", lambda e, kc=kc, col=col, vc0=vc0, slot=slot: e.matmul(
                            PS[0:64, vc0:vc0 + 128], hT[:, kc, col:col + CH], wsl[slot][:, kc, :],
                            start=(kc == 0), stop=(kc == KC - 1)),
                            reads=[("w", slot), ("hT", kc)], writes=psk(vc0, vc0 + 128))
                    P.op("act", lambda e, cslot=cslot, vt=vt, vc0=vc0: e.activation(
                        out=Vr[:, cslot, vt * 128:(vt + 1) * 128], in_=PS[0:64, vc0:vc0 + 128], func=AF.Copy),
                        reads=psk(vc0, vc0 + 128), writes=[("Vr", cslot)])
                    if last_block:
                        ci = k_i
                        if nch - 4 <= ci < nch - 2:
                            P.op("act", lambda e, ci=ci, vt=vt, vc0=vc0: e.activation(
                                out=vop_s[:, ci - (nch - 4), vt * 128:(vt + 1) * 128], in_=PS[0:64, vc0:vc0 + 128], func=AF.Copy),
                                reads=psk(vc0, vc0 + 128), writes=["vop_s"])
                        if ci >= nch - 2:
                            P.op("act", lambda e, ci=ci, vt=vt, vc0=vc0: e.activation(
                                out=vos_s[:, ci - (nch - 2), vt * 128:(vt + 1) * 128], in_=PS[0:64, vc0:vc0 + 128], func=AF.Copy),
                                reads=psk(vc0, vc0 + 128), writes=["vos_s"])
            for jq in range(8):
                slot = ws_next(T_Q + jq)
                bank = next_bank()
                c0 = bank * 512
                mm_group(PS[:, c0:c0 + T], psk(c0, c0 + 512), slot, h_main, h_key, KC)
                P.op("act", lambda e, jq=jq, c0=c0: e.activation(out=act[:, 16 + jq, 0:T], in_=PS[:, c0:c0 + T], func=AF.Copy),
                     reads=psk(c0, c0 + 512), writes=[("act", 16 + jq)])

            units = []
            for ci in range(nch):
                for kh in range(NKV):
                    units.append((ci, kh))

            def unit_S(n):
                ci, kh = units[n]
                side, kp = kh % 2, kh // 2
                par = n % 2
                sc0 = par * 768
                gci = chunk_base + ci
                is_sample = last_block and ci >= nch - 2 and "nosample" not in DBG
                p0 = side * 64
                q_ap = act[p0:p0 + 64, 16 + 4 * kp:16 + 4 * kp + 4, ci * CH:(ci + 1) * CH]
                for j in range(3):
                    if is_sample and j < 2:
                        s_i = ci - (nch - 2)
                        k_ap = kcb[p0:p0 + 64, s_i, kp, j * 64:(j + 1) * 64]
                        k_key = "kcb"
                    else:
                        cslot = (gci + j) % 16
                        k_ap = kT[p0:p0 + 64, kp, cslot * CH:(cslot + 1) * CH]
                        k_key = ("kT", kp, cslot)
                    P.op("pe", lambda e, j=j, k_ap=k_ap: e.matmul(PS[0:64, sc0 + j * 256: sc0 + (j + 1) * 256], k_ap, q_ap,
                                                                 start=True, stop=True),
                         reads=[k_key] + [("act", 16 + 4 * kp + g) for g in range(4)],
                         writes=psk(sc0, sc0 + 768))
                P.op("act", lambda e: e.activation(out=Ebuf[par][:, :], in_=PS[0:64, sc0:sc0 + 768], func=AF.Exp, scale=HD ** -0.5),
                     reads=psk(sc0, sc0 + 768), writes=[("E", par)])
                P.op(VEC2 if POOL_EB else "dve", lambda e: e.tensor_tensor(out=PT[par][:, :], in0=Ebuf[par][:, :],
                                                      in1=EB[:, kh, :, :].rearrange("p j c -> p (j c)"), op=ALU.mult),
                     reads=[("E", par), "EB"], writes=[("PT", par)])
                if b == 0 and ci < 2:
                    w_m = 512 if ci == 0 else 256
                    P.op("dve", lambda e: e.tensor_scalar(out=PT[par][:, 0:w_m], in0=PT[par][:, 0:w_m],
                                                          scalar1=cs[0:64, 88:89], scalar2=0.0, op0=ALU.mult, op1=ALU.add),
                         reads=[("PT", par), "cs"], writes=[("PT", par)])

            def unit_PV(n):
                ci, kh = units[n]
                side, kp = kh % 2, kh // 2
                par = n % 2
                oc0 = (3 + par) * 512
                gci = chunk_base + ci
                is_sample = last_block and ci >= nch - 2 and "nosample" not in DBG
                p0 = side * 64
                for j in range(3):
                    if is_sample and j < 2:
                        s_i = ci - (nch - 2)
                        v_ap = vcb[:, s_i, j, kh * 64:(kh + 1) * 64]
                        v_key = "vcb"
                    else:
                        cslot = (gci + j) % 16
                        v_ap = Vr[:, cslot, kh * 64:(kh + 1) * 64]
                        v_key = ("Vr", cslot)
                    P.op("pe", lambda e, j=j, v_ap=v_ap: e.matmul(PS[0:64, oc0:oc0 + 256], v_ap, PT[par][:, j * 256:(j + 1) * 256],
                                                                 start=(j == 0), stop=(j == 2)),
                         reads=[v_key, ("PT", par)], writes=psk(oc0, oc0 + 256))
                for j in range(3):
                    P.op("pe", lambda e, j=j: e.matmul(PS[0:64, oc0 + 256:oc0 + 512], ones[0:64, 0:64], PT[par][:, j * 256:(j + 1) * 256],
                                                       start=(j == 0), stop=(j == 2 and not SINK_MM)),
                         reads=["ones", ("PT", par)], writes=psk(oc0 + 256, oc0 + 512))
                if SINK_MM:
                    P.op("pe", lambda e: e.matmul(PS[0:64, oc0 + 256:oc0 + 512], ones[0:1, 0:64], bass.AP(sinkb, kh * 4, [[NH, 1], [1, 4], [0, 64]]),
                                                  start=False, stop=True),
                         reads=["ones", "sinkb"], writes=psk(oc0 + 256, oc0 + 512))
                    P.op("act", lambda e: e.activation(out=rec[par][:, :], in_=PS[0:64, oc0 + 256:oc0 + 512], func=AF.Ln),
                         reads=psk(oc0 + 256, oc0 + 512), writes=[("rec", par)])
                else:
                    sk = bass.AP(sinkx, kh * 4, [[NH, 64], [1, 4], [0, 64]])
                    P.op("dve", lambda e: e.tensor_tensor(out=rec[par][:, :].rearrange("p (g q) -> p g q", g=4),
                                                          in0=PS[0:64, oc0 + 256:oc0 + 512].rearrange("p (g q) -> p g q", g=4),
                                                          in1=sk, op=ALU.add),
                         reads=psk(oc0 + 256, oc0 + 512) + ["sinkx"], writes=[("rec", par)])
                if ACT_RECIP and not SINK_MM:
                    P.op("act", lambda e: e.activation(out=rec[par][:, :], in_=rec[par][:, :], func=AF.Ln),
                         reads=[("rec", par)], writes=[("rec", par)])
                if ACT_RECIP:
                    P.op("act", lambda e: e.activation(out=rec[par][:, :], in_=rec[par][:, :], func=AF.Exp, scale=-1.0),
                         reads=[("rec", par)], writes=[("rec", par)])
                if not ACT_RECIP:
                    P.op("dve", lambda e: e.reciprocal(out=rec[par][:, :], in_=rec[par][:, :]),
                         reads=[("rec", par)], writes=[("rec", par)])
                mc0 = 8 + 4 * kp
                P.op("dve", lambda e: e.tensor_tensor(
                    out=act[p0:p0 + 64, mc0:mc0 + 4, ci * CH:(ci + 1) * CH],
                    in0=rec[par][:, :].rearrange("p (g q) -> p g q", g=4),
                    in1=PS[0:64, oc0:oc0 + 256].rearrange("p (g q) -> p g q", g=4), op=ALU.mult),
                    reads=[("rec", par)] + psk(oc0, oc0 + 256), writes=[("act", mc0 + g) for g in range(4)])

            n_units = len(units)
            n_slots = 24
            s_done = 0
            pv_done = 0

            def attn_step(target_s):
                nonlocal s_done, pv_done
                while pv_done < s_done - 1:
                    unit_PV(pv_done)
                    pv_done += 1
                while s_done < target_s:
                    while pv_done < s_done - 1:
                        unit_PV(pv_done)
                        pv_done += 1
                    unit_S(s_done)
                    s_done += 1

            if last_block and "nosegs" not in DBG:
                segs = [(0, (nch - 2) * CH, "prev"), ((nch - 2) * CH, CH, ("s", 0)), ((nch - 1) * CH, CH, ("s", 1))]
            else:
                segs = [(0, T, "prev")]
            slot_i = 0
            for j in range(8):
                slot = ws_next(T_CONV + 3 * j)
                c_cc = 5 * 512
                mm_group(PS[:, c_cc:c_cc + NW], psk(c_cc, c_cc + 512), slot, h_ctx, h_key, KC)
                attn_step(math.ceil((slot_i + 1) * n_units / n_slots)); slot_i += 1
                ccs_t, ccs_key = next_tmp()
                P.op("act", lambda e, ccs_t=ccs_t: e.activation(out=ccs_t[:, 0:NW], in_=PS[:, c_cc:c_cc + NW], func=AF.Copy),
                     reads=psk(c_cc, c_cc + 512), writes=[ccs_key])
                slot = ws_next(T_CONV + 3 * j + 1)
                c_cu = 6 * 512
                mm_group(PS[:, c_cu:c_cu + NW], psk(c_cu, c_cu + 512), slot, h_ctx, h_key, KC)
                attn_step(math.ceil((slot_i + 1) * n_units / n_slots)); slot_i += 1
                G_t, G_key = next_tmp()
                goff = 0
                seg_info = []
                for (toff, L, src) in segs:
                    Lc = L + (pre if toff == 0 else 0)
                    coff = toff + (pre if toff != 0 else 0)
                    seg_info.append((goff, coff, Lc, toff, L, src))
                    goff += 2 + Lc
                GW = goff
                for (go, coff, Lc, toff, L, src) in seg_info:
                    if src == "prev":
                        if b > 0:
                            P.op("dve", lambda e, G_t=G_t, go=go, j=j: e.tensor_copy(out=G_t[:, go:go + 2], in_=gst[:, j, :]),
                                 reads=["gst"], writes=[G_key])
                    else:
                        P.op("dve", lambda e, G_t=G_t, go=go, j=j, si=src[1]: e.tensor_copy(out=G_t[:, go:go + 2], in_=scs[:, si, j, :]),
                             reads=["scs"], writes=[G_key])
                    P.op("dve", lambda e, G_t=G_t, go=go, coff=coff, Lc=Lc, ccs_t=ccs_t: e.tensor_tensor(
                        out=G_t[:, go + 2:go + 2 + Lc], in0=ccs_t[:, coff:coff + Lc], in1=PS[:, c_cu + coff:c_cu + coff + Lc], op=ALU.mult),
                        reads=[ccs_key] + psk(c_cu, c_cu + 512), writes=[G_key])
                go0, _, Lc0, _, _, _ = seg_info[0]
                if not last_block:
                    P.op("dve", lambda e, G_t=G_t, a=go0 + Lc0, j=j: e.tensor_copy(out=gst[:, j, :], in_=G_t[:, a:a + 2]),
                         reads=[G_key], writes=["gst"])
                else:
                    for si_, (go, coff, Lc, toff, L, src) in enumerate(seg_info):
                        P.op("dve", lambda e, G_t=G_t, a=go + Lc, j=j, si_=si_: e.tensor_copy(out=gout_s[:, j, si_, :], in_=G_t[:, a:a + 2]),
                             reads=[G_key], writes=["gout_s"])
                A_t, A_key = next_tmp()
                WA = GW - 2
                P.op("act", lambda e, A_t=A_t, G_t=G_t, j=j: e.mul(out=A_t[:, 0:WA], in_=G_t[:, 0:WA], mul=convw_ap(0, j)),
                     reads=[G_key, "cs"], writes=[A_key])
                for tap in (1, 2):
                    if POOL_TAPS:
                        B_t, B_key = next_tmp()
                        P.op("act", lambda e, B_t=B_t, G_t=G_t, j=j, tap=tap: e.mul(out=B_t[:, 0:WA], in_=G_t[:, tap:tap + WA],
                                                                                    mul=convw_ap(tap, j)),
                             reads=[G_key, "cs"], writes=[B_key])
                        P.op(VEC2, lambda e, A_t=A_t, B_t=B_t: e.tensor_tensor(out=A_t[:, 0:WA], in0=A_t[:, 0:WA],
                                                                                 in1=B_t[:, 0:WA], op=ALU.add),
                             reads=[A_key, B_key], writes=[A_key])
                    else:
                        P.op("dve", lambda e, A_t=A_t, G_t=G_t, j=j, tap=tap: e.scalar_tensor_tensor(
                            out=A_t[:, 0:WA], in0=G_t[:, tap:tap + WA], scalar=convw_ap(tap, j), in1=A_t[:, 0:WA],
                            op0=ALU.mult, op1=ALU.add),
                            reads=[G_key, A_key, "cs"], writes=[A_key])
                slot = ws_next(T_CONV + 3 * j + 2)
                c_cb = 7 * 512
                mm_group(PS[:, c_cb:c_cb + T], psk(c_cb, c_cb + 512), slot, h_main, h_key, KC)
                attn_step(math.ceil((slot_i + 1) * n_units / n_slots)); slot_i += 1
                for (go, coff, Lc, toff, L, src) in seg_info:
                    a0 = go + (Lc - L)
                    P.op("dve", lambda e, A_t=A_t, a0=a0, toff=toff, L=L, j=j: e.tensor_tensor(
                        out=act[:, j, toff:toff + L], in0=A_t[:, a0:a0 + L], in1=PS[:, c_cb + toff:c_cb + toff + L], op=ALU.mult),
                        reads=[A_key] + psk(c_cb, c_cb + 512), writes=[("act", j)])
            attn_step(n_units)
            while pv_done < s_done:
                unit_PV(pv_done)
                pv_done += 1

            out_proj_phase(T, T_OUT, 1, lambda kc: act[:, kc, 0:T], lambda kc: ("act", kc), [(0, 16)])
            residual_phase(T, 1)
            if LATE_RSTD:
                for kc in range(KC):
                    P.op("act", lambda e, kc=kc: e.mul(out=hT[:, kc, 0:T], in_=xs[:, kc, 0:T], mul=gain_ap(2, kc)),
                         reads=[("xs", kc), "cs"], writes=[("hT", kc)])
                    s_t, s_key = next_sq()
                    P.op("act", lambda e, kc=kc, s_t=s_t: e.activation(out=s_t[:, 0:T], in_=xs[:, kc, 0:T], func=AF.Square),
                         reads=[("xs", kc)], writes=[s_key])
                    ssum_mm(s_t, s_key, T, kc == 0, kc == KC - 1)
                r2_t, r2_key = finish_rstd(T)
            else:
                norm_to_h(xs, lambda kc: ("xs", kc), T, 2, 0)

            def h2(kc):
                return hT[:, kc, 0:T]

            KMAJ = 2 if (LATE_RSTD and NS >= 5) else 0
            pre_banks = {}
            if KMAJ:
                g0 = ws["req"]
                assert g0 % NT == T_GU and ws["issued"] >= g0 + 2 * KMAJ
                kslots = [(g0 + k) % NS for k in range(2 * KMAJ)]
                kcols = [next_bank() * 512 for k in range(2 * KMAJ)]
                for kc in range(KC):
                    for k in range(2 * KMAJ):
                        P.op("pe", lambda e, kc=kc, k=k: e.matmul(PS[:, kcols[k]:kcols[k] + T], wsl[kslots[k]][:, kc, :], h2(kc),
                                                                   start=(kc == 0), stop=(kc == KC - 1)),
                             reads=[("w", kslots[k]), ("hT", kc)], writes=psk(kcols[k], kcols[k] + 512))
                ws["req"] = g0 + 2 * KMAJ
                for f in range(KMAJ):
                    pre_banks[f] = (kcols[2 * f], kcols[2 * f + 1])
            for fc in range(FC):
                if fc in pre_banks:
                    cg, cu_ = pre_banks[fc]
                else:
                    slot = ws_next(T_GU + 2 * fc)
                    bg = next_bank()
                    cg = bg * 512
                    mm_group(PS[:, cg:cg + T], psk(cg, cg + 512), slot, h2, h_key, KC)
                    slot = ws_next(T_GU + 2 * fc + 1)
                    bu = next_bank()
                    cu_ = bu * 512
                    mm_group(PS[:, cu_:cu_ + T], psk(cu_, cu_ + 512), slot, h2, h_key, KC)
                s_t, s_key = next_tmp()
                if LATE_RSTD:
                    w_t, w_key = next_tmp()
                    P.op("dve", lambda e, s_t=s_t, cg=cg: e.tensor_tensor(out=s_t[:, 0:T], in0=r2_t[:, 0:T], in1=PS[:, cg:cg + T], op=ALU.mult),
                         reads=[r2_key] + psk(cg, cg + 512), writes=[s_key])
                    P.op("act", lambda e, s_t=s_t: e.activation(out=s_t[:, 0:T], in_=s_t[:, 0:T], func=AF.Silu),
                         reads=[s_key], writes=[s_key])
                    P.op("dve", lambda e, w_t=w_t, cu_=cu_: e.tensor_tensor(out=w_t[:, 0:T], in0=r2_t[:, 0:T], in1=PS[:, cu_:cu_ + T], op=ALU.mult),
                         reads=[r2_key] + psk(cu_, cu_ + 512), writes=[w_key])
                    P.op("dve", lambda e, s_t=s_t, w_t=w_t, fc=fc: e.tensor_tensor(
                        out=act[:, fc, 0:T], in0=s_t[:, 0:T], in1=w_t[:, 0:T], op=ALU.mult),
                        reads=[s_key, w_key], writes=[("act", fc)])
                else:
                    P.op("act", lambda e, s_t=s_t, cg=cg: e.activation(out=s_t[:, 0:T], in_=PS[:, cg:cg + T], func=AF.Silu),
                         reads=psk(cg, cg + 512), writes=[s_key])
                    P.op("dve", lambda e, s_t=s_t, cu_=cu_, fc=fc: e.tensor_tensor(
                        out=act[:, fc, 0:T], in0=s_t[:, 0:T], in1=PS[:, cu_:cu_ + T], op=ALU.mult),
                        reads=[s_key] + psk(cu_, cu_ + 512), writes=[("act", fc)])
            out_proj_phase(T, T_DN, 3, lambda kc: act[:, kc, 0:T], lambda kc: ("act", kc), DN_PIECES, hook=pf_hook)
            residual_phase(T, 3)
            for qd in range(4):
                P.op(IOQ, lambda e, qd=qd, tok0=tok0, T=T: e.dma_start(out=yT_v[:, qd * 4:(qd + 1) * 4, tok0:tok0 + T],
                                                                     in_=xs[:, qd * 4:(qd + 1) * 4, 0:T]),
                     reads=[("xs", kc) for kc in range(qd * 4, qd * 4 + 4)], dma_key=("ys", qd))
            if last_block and "noouts" not in DBG:
                P.op(IOQ, lambda e: e.dma_start(out=g_out, in_=gout_s[:]), reads=["gout_s"], dma_key="outs")
                P.op(IOQ, lambda e: e.dma_start(out=ko_p, in_=kop_s[:]), reads=["kop_s"], dma_key="outs")
                P.op(IOQ, lambda e: e.dma_start(out=ko_s, in_=kos_s[:]), reads=["kos_s"], dma_key="outs")
                P.op(IOQ, lambda e: e.dma_start(out=vo_p, in_=vop_s[:]), reads=["vop_s"], dma_key="outs")
                P.op(IOQ, lambda e: e.dma_start(out=vs_out[:, 64:128, :].rearrange("s p f -> p s f"), in_=vos_s[:]),
                     reads=["vos_s"], dma_key="outs")

        chunk_base = 0
        for b in range(n_blocks):
            do_block(b, chunk_base)
            chunk_base += BLOCKS[b]

        P.emit(nc)
        _NC_CACHE["sig_counts"] = P.sig_counts
    return nc


def _t5_bucket_np(rel):
    half = NBUCK // 2
    max_exact = half // 2
    try:
        import jax
        import jax.numpy as jnp
        with jax.default_device(jax.devices("cpu")[0]):
            r = jnp.asarray(rel, dtype=jnp.int32)
            ret = jnp.where(r > 0, half, 0)
            n = jnp.abs(r)
            nf = jnp.maximum(n, 1).astype(jnp.float32)
            large = max_exact + (jnp.log(nf / max_exact) / math.log(128 / max_exact) * (half - max_exact)).astype(jnp.int32)
            large = jnp.minimum(large, half - 1)
            return np.asarray(ret + jnp.where(n < max_exact, n, large))
    except Exception:
        rel = np.asarray(rel, dtype=np.int32)
        ret = np.where(rel > 0, half, 0)
        n = np.abs(rel)
        nf = np.maximum(n, 1).astype(np.float32)
        large = max_exact + (np.log(nf / np.float32(max_exact)) / np.float32(math.log(128 / max_exact))
                             * np.float32(half - max_exact)).astype(np.int32)
        large = np.minimum(large, half - 1)
        return ret + np.where(n < max_exact, n, large)


def _tiles(W, col_lists=None, row_perm=None):
    K = W.shape[0]
    if row_perm is not None:
        W = W[row_perm]
    out = []
    for cols in col_lists:
        blk = W[:, cols]
        out.append(blk.reshape(K // 128, 128, 128).transpose(1, 0, 2))
    return out


def _q_head_pairs():
    pairs = []
    for kp in range(2):
        for g in range(4):
            pairs.append((8 * kp + g, 8 * kp + 4 + g))
    return pairs


def _pack_weights(w_in, w_out, w_gate, w_up, w_down):
    tiles = [None] * NT
    ar = np.arange
    for j in range(8):
        cols_cb = ar(j * 128, (j + 1) * 128)
        cols_cc = CW + cols_cb
        cols_cu = 2 * CW + cols_cb
        t = _tiles(w_in, [cols_cc, cols_cu, cols_cb])
        tiles[T_CONV + 3 * j: T_CONV + 3 * j + 3] = t
    qb = 3 * CW
    kb = qb + NH * HD
    vb = kb + NKV * HD
    tiles[T_K:T_K + 2] = _tiles(w_in, [ar(kb + kp * 128, kb + (kp + 1) * 128) for kp in range(2)])
    tiles[T_V:T_V + 2] = _tiles(w_in, [ar(vb + vt * 128, vb + (vt + 1) * 128) for vt in range(2)])
    pairs = _q_head_pairs()
    tiles[T_Q:T_Q + 8] = _tiles(w_in, [np.concatenate([ar(qb + a * 64, qb + (a + 1) * 64), ar(qb + bb * 64, qb + (bb + 1) * 64)])
                                       for (a, bb) in pairs])
    rp = [ar(0, CW)]
    for (a, bb) in pairs:
        rp.append(ar(CW + a * 64, CW + (a + 1) * 64))
        rp.append(ar(CW + bb * 64, CW + (bb + 1) * 64))
    rp = np.concatenate(rp)
    tiles[T_OUT:T_OUT + 16] = _tiles(w_out, [ar(oc * 128, (oc + 1) * 128) for oc in range(16)], row_perm=rp)
    tg = _tiles(w_gate, [ar(f * 128, (f + 1) * 128) for f in range(FC)])
    tu = _tiles(w_up, [ar(f * 128, (f + 1) * 128) for f in range(FC)])
    for f in range(FC):
        tiles[T_GU + 2 * f] = tg[f]
        tiles[T_GU + 2 * f + 1] = tu[f]
    td = _tiles(w_down, [ar(oc * 128, (oc + 1) * 128) for oc in range(16)])
    for oc in range(16):
        for pi, (k0, kn) in enumerate(DN_PIECES):
            t = np.zeros((128, 16, 128), np.float32)
            t[:, 0:kn, :] = td[oc][:, k0:k0 + kn, :]
            tiles[T_DN + 3 * oc + pi] = t
    return np.ascontiguousarray(np.stack(tiles, 0).astype(np.float32))


_NC_CACHE = {}


def _prepare(x_prompt, x_sample, state_conv, cache_k, cache_v, rel_table, g_pre_mix, w_in, conv_w, attn_sinks,
             w_out, g_post_mix, g_pre_ffn, w_gate, w_up, w_down, g_post_ffn):
    f = np.float32
    x_prompt = np.asarray(x_prompt, f)
    x_sample = np.asarray(x_sample, f)
    wst = _pack_weights(np.asarray(w_in, f)[0], np.asarray(w_out, f)[0], np.asarray(w_gate, f)[0],
                        np.asarray(w_up, f)[0], np.asarray(w_down, f)[0])
    cst0 = np.zeros((128, 112), f)
    for gi, gv in enumerate((g_pre_mix, g_post_mix, g_pre_ffn, g_post_ffn)):
        cst0[:, gi * 16:(gi + 1) * 16] = np.asarray(gv, f)[0].reshape(16, 128).T
    cw = np.asarray(conv_w, f)[0]
    for tap in range(3):
        cst0[:, 64 + tap * 8: 64 + tap * 8 + 8] = cw[tap].reshape(8, 128).T
    cst0[:, 89:105] = np.asarray(attn_sinks, f)[0][None, :]
    tab = np.ascontiguousarray(np.asarray(rel_table, f))
    rel = np.arange(256) - 191
    bk = _t5_bucket_np(rel)
    oh1 = np.zeros((NBUCK, 260), f)
    oh1[bk[:255], np.arange(255)] = 1.0
    oh = np.zeros((128, 256), f)
    for ql in range(4):
        oh[ql * 32:(ql + 1) * 32, :] = oh1[:, 3 - ql: 3 - ql + 256]
    sc = np.asarray(state_conv, f)[0]
    ck = np.asarray(cache_k, f)[0].reshape(16, 128, 256)
    cv = np.asarray(cache_v, f)[0].reshape(16, 128, 256)

    in_maps = []
    for c in range(NCORES):
        bi, qt = c // 4, c % 4
        xT = np.zeros((D, NTOK), f)
        if qt > 0:
            xT[:, 0:HALO] = x_prompt[bi, qt * PTOK - HALO: qt * PTOK].T
        xT[:, HALO:HALO + PTOK] = x_prompt[bi, qt * PTOK:(qt + 1) * PTOK].T
        xT[:, HALO + PTOK:] = x_sample[2 * c:2 * c + 2].reshape(STOK, D).T
        cst = cst0.copy()
        cst[:, 88] = 1.0 if qt > 0 else 0.0
        ckT = ck[2 * c:2 * c + 2].transpose(2, 0, 1).reshape(2, 128, 2, 128).transpose(1, 2, 0, 3)
        scv = sc[2 * c:2 * c + 2].transpose(2, 0, 1).reshape(8, 128, 2, 2).transpose(1, 2, 0, 3)
        in_maps.append({
            "xT": np.ascontiguousarray(xT), "wst": wst, "cst": cst, "tab": tab, "oh": oh,
            "ckT": np.ascontiguousarray(ckT), "ckn": np.ascontiguousarray(ck[2 * c:2 * c + 2]),
            "cvn": np.ascontiguousarray(cv[2 * c:2 * c + 2]), "scv": np.ascontiguousarray(scv),
        })
    return in_maps


def _assemble(R):
    f = np.float32
    y_prompt = np.zeros((2, 8192, D), f)
    y_sample = np.zeros((16, 64, D), f)
    ncp = np.zeros((1, 2, 2, CW), f)
    nkp = np.zeros((1, 2, 128, NKV, HD), f)
    nvp = np.zeros((1, 2, 128, NKV, HD), f)
    ncs = np.zeros((1, 16, 2, CW), f)
    nks = np.zeros((1, 16, 128, NKV, HD), f)
    nvs = np.zeros((1, 16, 128, NKV, HD), f)
    for c in range(NCORES):
        bi, qt = c // 4, c % 4
        r = R[c]
        yT = np.asarray(r["yT"])
        y_prompt[bi, qt * PTOK:(qt + 1) * PTOK] = yT[:, 0:PTOK].T
        y_sample[2 * c:2 * c + 2] = yT[:, PTOK:].T.reshape(2, 64, D)
        go = np.asarray(r["g_out"])
        for s in range(2):
            ncs[0, 2 * c + s] = go[:, :, 1 + s, :].transpose(2, 1, 0).reshape(2, CW)
            kn = np.asarray(r["ko_s"])[:, :, s, :]
            nks[0, 2 * c + s, 0:64] = np.asarray(r["kc_copy"])[s].reshape(64, NKV, HD)
            nks[0, 2 * c + s, 64:128] = kn.transpose(2, 1, 0).reshape(64, NKV, HD)
            nvs[0, 2 * c + s] = np.asarray(r["vs_out"])[s].reshape(128, NKV, HD)
        if qt == 3:
            ncp[0, bi] = go[:, :, 0, :].transpose(2, 1, 0).reshape(2, CW)
            nkp[0, bi] = np.asarray(r["ko_p"]).transpose(2, 1, 0).reshape(128, NKV, HD)
            nvp[0, bi] = np.asarray(r["vo_p"]).transpose(1, 0, 2).reshape(128, NKV, HD)
    return (y_prompt, y_sample, ncp, nkp, nvp, ncs, nks, nvs)


def kernel(**inputs):
    in_maps = _prepare(**inputs)
    if "nc" not in _NC_CACHE:
        _NC_CACHE["nc"] = build_program()
    nc = _NC_CACHE["nc"]
    res = run_bass_kernel_spmd(nc, in_maps, core_ids=list(range(NCORES)))
    return _assemble(res.results)
```

```python
import math
import contextlib
import numpy as np
import concourse.bass as bass
import concourse.mybir as mybir
from concourse.bass_utils import run_bass_kernel_spmd

F32 = mybir.dt.float32
BF16 = mybir.dt.bfloat16
AF = mybir.ActivationFunctionType
ALU = mybir.AluOpType

D = 2048
KC = 16
CW = 1024
NH = 16
NKV = 4
HD = 64
DFF = 5632
FC = 44
IN_COLS = 4608
NCORES = 8
PTOK = 2048
HALO = 128
STOK = 128
NTOK = HALO + PTOK + STOK
NOUT = PTOK + STOK
CH = 64
BLOCKS = [6, 7, 7, 7, 7]
TMAX = 448
EPS = 1e-6
NS = 5
NBUCK = 32

T_K = 0
T_V = 2
T_Q = 4
T_CONV = 12
T_OUT = 36
T_GU = 52
T_DN = 140
NT = 188
DN_PIECES = [(0, 16), (16, 16), (32, 12)]

ENGS = ("pe", "act", "dve", "pool", "sp")
LATE_RSTD = True
SINK_MM = True
VEC2 = "dve"
POOL_H = True
POOL_RES = True
POOL_EB = True
POOL_TAPS = True
ACT_RECIP = True
SCRATCH = False
IOQ = "pool" if SCRATCH else "sp"
PREFETCH = True
import os
DBG = set(os.environ.get("KDBG", "").split(","))


class _Ins:
    __slots__ = ("eng", "idx", "fn", "deps", "dma_key", "signals", "signum")

    def __init__(self, eng, idx, fn, deps, dma_key):
        self.eng = eng
        self.idx = idx
        self.fn = fn
        self.deps = deps
        self.dma_key = dma_key
        self.signals = False
        self.signum = None


class Prog:
    def __init__(self, group_keys=()):
        self.streams = {e: [] for e in ENGS}
        self.last_writer = {}
        self.readers = {}
        self.dma_keys = []
        self.group_keys = set(group_keys)

    def op(self, eng, fn, reads=(), writes=(), dma_key=None):
        deps = set()
        for r in reads:
            lw = self.last_writer.get(r)
            if lw is not None:
                deps.add(lw)
        for w in writes:
            lw = self.last_writer.get(w)
            if lw is not None:
                deps.add(lw)
            for rd in self.readers.get(w, {}).values():
                deps.add(rd)
        idx = len(self.streams[eng])
        me = (eng, idx)
        deps.discard(me)
        if eng == "pe":
            deps = {d for d in deps if d[0] != "pe"}
        ins = _Ins(eng, idx, fn, deps, dma_key)
        self.streams[eng].append(ins)
        if dma_key is not None and dma_key not in self.dma_keys:
            self.dma_keys.append(dma_key)
        for r in reads:
            rk = eng if eng in ("pe", "act", "dve") else me
            self.readers.setdefault(r, {})[rk] = me
        for w in writes:
            self.last_writer[w] = me
            self.readers[w] = {}
        return ins

    def emit(self, nc, final_wait_eng="sp"):
        streams = self.streams
        fin_deps = set()
        for e in ENGS:
            for ins in streams[e]:
                if ins.dma_key is not None:
                    fin_deps.add((e, ins.idx))
            if streams[e] and e != final_wait_eng:
                fin_deps.add((e, len(streams[e]) - 1))
        for e in ENGS:
            for ins in streams[e]:
                for (de, di) in ins.deps:
                    streams[de][di].signals = True
        for (de, di) in fin_deps:
            streams[de][di].signals = True
        dma_count = {}
        for e in ENGS:
            n = 0
            for ins in streams[e]:
                if ins.dma_key is not None:
                    c = dma_count.get(ins.dma_key, 0) + 1
                    dma_count[ins.dma_key] = c
                    ins.signum = 16 * c
                elif ins.signals:
                    n += 1
                    ins.signum = n
        self.sig_counts = {e: max([i.signum or 0 for i in streams[e] if i.dma_key is None] + [0]) for e in ENGS}
        self.sig_counts.update({str(k): 16 * v for k, v in dma_count.items()})
        for e in ENGS:
            for ins in streams[e]:
                if ins.dma_key in self.group_keys:
                    ins.signum = 16 * dma_count[ins.dma_key]
        with contextlib.ExitStack() as st:
            sem_eng = {e: st.enter_context(nc.semaphore("s_" + e)) for e in ENGS}
            sem_dma = {k: st.enter_context(nc.semaphore("d_%d" % i)) for i, k in enumerate(self.dma_keys)}
            block = st.enter_context(nc.Block())

            def sem_of(ins):
                return sem_dma[ins.dma_key] if ins.dma_key is not None else sem_eng[ins.eng]

            def waits_for(deps, known, engobj):
                need = {}
                for (de, di) in deps:
                    d = streams[de][di]
                    s = sem_of(d)
                    key = id(s)
                    if known.get(key, 0) >= d.signum:
                        continue
                    if key not in need or need[key][1] < d.signum:
                        need[key] = (s, d.signum)
                for key, (s, v) in need.items():
                    engobj.wait_ge(s, v)
                    known[key] = v

            def run_stream(e, engobj):
                known = {}
                for ins in streams[e]:
                    waits_for(ins.deps, known, engobj)
                    bi = ins.fn(engobj)
                    if ins.dma_key is not None:
                        bi.then_inc(sem_dma[ins.dma_key], 16)
                    elif ins.signals:
                        bi.then_inc(sem_eng[e], 1)
                if e == final_wait_eng:
                    waits_for(fin_deps, known, engobj)

            @block.tensor
            def _(eng):
                run_stream("pe", eng)

            @block.scalar
            def _(eng):
                run_stream("act", eng)

            @block.vector
            def _(eng):
                run_stream("dve", eng)

            @block.gpsimd
            def _(eng):
                run_stream("pool", eng)

            @block.sync
            def _(eng):
                run_stream("sp", eng)


def build_program(n_blocks=len(BLOCKS)):
    nc = bass.Bass("TRN2", target_bir_lowering=False)

    def din(name, shape, dt=F32):
        return nc.dram_tensor(name, list(shape), dt, kind="ExternalInput").ap()

    def dout(name, shape, dt=F32):
        return nc.dram_tensor(name, list(shape), dt, kind="ExternalOutput").ap()

    xT = din("xT", [D, NTOK])
    wst = din("wst", [NT, 128, 16, 128])
    cst = din("cst", [128, 112])
    tab = din("tab", [NBUCK, NH])
    ohd = din("oh", [128, 256])
    ckT = din("ckT", [128, 2, 2, 128])
    ckn = din("ckn", [2, 128, 256])
    cvn = din("cvn", [2, 128, 256])
    scv = din("scv", [128, 2, 8, 2])
    wbf = nc.dram_tensor("wbf", [NT, 128, 16, 128], BF16, kind="Internal").ap()

    yT = dout("yT", [D, NOUT])
    g_out = dout("g_out", [128, 8, 3, 2])
    ko_p = dout("ko_p", [128, 2, 128])
    vo_p = dout("vo_p", [64, 2, 256])
    ko_s = dout("ko_s", [128, 2, 2, 64])
    kc_copy = dout("kc_copy", [2, 64, 256])
    vs_out = dout("vs_out", [2, 128, 256])

    xT_v = xT.rearrange("(kc p) t -> p kc t", p=128)
    yT_v = yT.rearrange("(kc p) t -> p kc t", p=128)

    P = Prog(group_keys=("consts", "consts_sw", "outs"))

    with contextlib.ExitStack() as st:
        def sb(name, shape, dt):
            return st.enter_context(nc.sbuf_tensor(name, list(shape), dt))

        PS = st.enter_context(nc.psum_tensor("PS", [128, 4096], F32))
        xs = sb("xs", [128, KC, TMAX], F32)
        hT = sb("hT", [128, KC, 512], BF16)
        u = sb("u", [128, KC, TMAX], F32)
        act = sb("act", [128, FC, TMAX], BF16)
        kT = sb("kT", [128, 2, 16 * CH], BF16)
        Vr = sb("Vr", [64, 16, 256], BF16)
        kcb = sb("kcb", [128, 2, 2, 128], BF16)
        vcb = sb("vcb", [64, 2, 2, 256], BF16)
        wsl = [sb("w%d" % i, [128, 16, 128], BF16) for i in range(NS)]
        EB = sb("EB", [64, NKV, 3, 256], F32)
        cs = sb("cs", [128, 112], F32)
        sinkx = sb("sinkx", [64, NH], F32)
        sinkb = sb("sinkb", [1, NH], BF16)
        bd = sb("bd", [128, 64], F32)
        ohs = sb("ohs", [128, 256], F32)
        ones = sb("ones", [128, 128], BF16)
        gst = sb("gst", [128, 8, 2], F32)
        scs = sb("scs", [128, 2, 8, 2], F32)
        NTMP = 6
        tmp = [sb("tmp%d" % i, [128, 520], F32) for i in range(NTMP)]
        NSQ = 3
        sq = [sb("sq%d" % i, [128, 512], BF16) for i in range(NSQ)]
        Ebuf = [sb("E%d" % i, [64, 768], F32) for i in range(2)]
        PT = [sb("PT%d" % i, [64, 768], BF16) for i in range(2)]
        rec = [sb("rec%d" % i, [64, 256], F32) for i in range(2)]
        rstd = [sb("rstd%d" % i, [128, 512], F32) for i in range(2)]
        stg = [sb("stg%d" % i, [128, TMAX], F32) for i in range(4)]
        gout_s = sb("gout_s", [128, 8, 3, 2], F32)
        kop_s = sb("kop_s", [128, 2, 128], F32)
        kos_s = sb("kos_s", [128, 2, 2, 64], F32)
        vop_s = sb("vop_s", [64, 2, 256], F32)
        vos_s = sb("vos_s", [64, 2, 256], F32)

        cnt = {"tmp": 0, "sq": 0, "rstd": 0, "bank": 0}

        def next_tmp():
            i = cnt["tmp"] % NTMP
            cnt["tmp"] += 1
            return tmp[i], ("tmp", i)

        def next_sq():
            i = cnt["sq"] % NSQ
            cnt["sq"] += 1
            return sq[i], ("sq", i)

        def next_rstd():
            i = cnt["rstd"] % 2
            cnt["rstd"] += 1
            return rstd[i], ("rstd", i)

        def psk(c0, c1):
            return [("ps", h) for h in range(c0 // 512, (c1 + 511) // 512)]

        BANK_SSUM = 6
        BANK_V = 7

        def next_bank():
            b = cnt["bank"] % 6
            cnt["bank"] += 1
            return b

        ws = {"issued": 0, "req": 0}
        total_tiles = NT * n_blocks

        def tile_kcn(t):
            if t >= T_DN and (t - T_DN) % 3 == 2:
                return 12
            return 16

        def ws_ensure(upto):
            while ws["issued"] < min(upto, total_tiles):
                g = ws["issued"]
                pas, t = divmod(g, NT)
                slot = g % NS
                kcn = tile_kcn(t)
                dst = wsl[slot][:, 0:kcn, :]
                if pas == 0 or not SCRATCH:
                    first_reads = ([("xs", kc) for kc in range(KC)] + [("u", kc) for kc in range(KC)]) if g < NS else []
                    P.op("pool", lambda e, dst=dst, src=wst[t, :, 0:kcn, :]: e.dma_start(out=dst, in_=src),
                         reads=first_reads, writes=[("w", slot)], dma_key=("wl_sw", slot))
                    if n_blocks > 1 and SCRATCH:
                        P.op("sp", lambda e, dst=wbf[t, :, 0:kcn, :], src=dst: e.dma_start(out=dst, in_=src),
                             reads=[("w", slot)], writes=[("wbf", t)], dma_key=("wo", slot))
                else:
                    P.op("sp", lambda e, dst=dst, src=wbf[t, :, 0:kcn, :]: e.dma_start(out=dst, in_=src),
                         reads=[("wbf", t)], writes=[("w", slot)], dma_key=("wl", slot))
                ws["issued"] += 1

        def ws_next(expect_t):
            g = ws["req"]
            ws["req"] += 1
            assert g % NT == expect_t, (g % NT, expect_t)
            ws_ensure(g + NS)
            return g % NS

        P.op("sp", lambda e: e.dma_start(out=cs[:], in_=cst), writes=["cs"], dma_key="consts")
        P.op("dve", lambda e: e.memset(bd[:], 0.0), writes=[("bd", i) for i in range(4)])
        for ql in range(4):
            P.op("sp", lambda e, ql=ql: e.dma_start(out=bd[ql * 32:(ql + 1) * 32, ql * 16:(ql + 1) * 16], in_=tab),
                 writes=[("bd", ql)], dma_key="consts")
        P.op("sp", lambda e: e.dma_start(out=ohs[:], in_=ohd), writes=["ohs"], dma_key="consts")
        P.op("sp", lambda e: e.dma_start(out=scs[:], in_=scv), writes=["scs"], dma_key="consts")
        P.op("pool", lambda e: e.dma_start(out=kcb[:], in_=ckT), writes=["kcb"], dma_key="consts_sw")
        P.op("pool", lambda e: e.dma_start(out=vcb[:], in_=cvn.rearrange("s (j p) f -> p s j f", p=64)),
             writes=["vcb"], dma_key="consts_sw")
        P.op(IOQ, lambda e: e.dma_start(out=kc_copy, in_=ckn[:, 64:128, :]), dma_key="outs")
        P.op(IOQ, lambda e: e.dma_start(out=vs_out[:, 0:64, :], in_=cvn[:, 64:128, :]), dma_key="outs")
        P.op("dve", lambda e: e.memset(ones[:], 1.0), writes=["ones"])
        for i in range(NTMP):
            P.op("dve", lambda e, i=i: e.memset(tmp[i][:], 0.0), writes=[("tmp", i)])
        P.op("dve", lambda e: e.memset(gst[:], 0.0), writes=["gst"])
        P.op("act", lambda e: e.activation(out=sinkx[:], in_=cs[0:64, 89:105], func=AF.Exp),
             reads=["cs"], writes=["sinkx"])
        if SINK_MM:
            P.op("act", lambda e: e.activation(out=sinkb[0:1, :], in_=sinkx[0:1, :], func=AF.Copy),
                 reads=["sinkx"], writes=["sinkb"])
        for j in range(3):
            for qh in range(2):
                bank = next_bank()
                c0 = bank * 512
                for c2 in range(8):
                    X = j * 64 + 60 - 4 * (qh * 8 + c2)
                    P.op("pe", lambda e, c=c0 + c2 * 64, X=X: e.matmul(
                        PS[0:64, c:c + 64], ohs[:, X:X + 64], bd[:, :], start=True, stop=True),
                        reads=["ohs"] + [("bd", i) for i in range(4)], writes=psk(c0, c0 + 512))
                out_ap = EB[:, :, j, :].rearrange("p kh (g q) -> p kh g q", g=4)[:, :, :, qh * 32:(qh + 1) * 32] \
                    .rearrange("p kh g q -> p q kh g")
                in_ap = PS[0:64, c0:c0 + 512].rearrange("p (q kh g) -> p q kh g", q=32, kh=4)
                P.op("act", lambda e, o=out_ap, i=in_ap: e.activation(out=o, in_=i, func=AF.Exp),
                     reads=psk(c0, c0 + 512), writes=["EB"])

        def gain_ap(gi, kc):
            return cs[:, gi * 16 + kc: gi * 16 + kc + 1]

        def convw_ap(tap, j):
            return cs[:, 64 + tap * 8 + j: 64 + tap * 8 + j + 1]

        def ssum_mm(sq_t, sq_key, n, first, last, bank=BANK_SSUM):
            P.op("pe", lambda e, r=sq_t[:, 0:n]: e.matmul(PS[:, bank * 512: bank * 512 + n], ones[:, :], r,
                                                           start=first, stop=last),
                 reads=["ones", sq_key], writes=psk(bank * 512, bank * 512 + 512))

        def finish_rstd(n, bank=BANK_SSUM):
            r_t, r_key = next_rstd()
            if ACT_RECIP:
                P.op("act", lambda e: e.activation(out=r_t[:, 0:n], in_=PS[:, bank * 512: bank * 512 + n],
                                                   func=AF.Ln, scale=1.0 / D, bias=EPS),
                     reads=psk(bank * 512, bank * 512 + 512), writes=[r_key])
                P.op("act", lambda e: e.activation(out=r_t[:, 0:n], in_=r_t[:, 0:n], func=AF.Exp, scale=-0.5),
                     reads=[r_key], writes=[r_key])
            else:
                P.op("act", lambda e: e.activation(out=r_t[:, 0:n], in_=PS[:, bank * 512: bank * 512 + n],
                                                   func=AF.Sqrt, scale=1.0 / D, bias=EPS),
                     reads=psk(bank * 512, bank * 512 + 512), writes=[r_key])
                P.op("dve", lambda e: e.reciprocal(out=r_t[:, 0:n], in_=r_t[:, 0:n]), reads=[r_key], writes=[r_key])
            return r_t, r_key

        def apply_h(out_ap, out_key, x_ap, x_key, g_ap, r_ap, r_key, n, use_pool):
            if not use_pool:
                P.op("dve", lambda e: e.scalar_tensor_tensor(out=out_ap, in0=x_ap, scalar=g_ap, in1=r_ap,
                                                             op0=ALU.mult, op1=ALU.mult),
                     reads=[x_key, r_key, "cs"], writes=[out_key])
            else:
                t_t, t_key = next_tmp()
                P.op("act", lambda e: e.mul(out=t_t[:, 0:n], in_=x_ap, mul=g_ap), reads=[x_key, "cs"], writes=[t_key])
                P.op(VEC2, lambda e: e.tensor_tensor(out=out_ap, in0=t_t[:, 0:n], in1=r_ap, op=ALU.mult),
                     reads=[t_key, r_key], writes=[out_key])

        def norm_to_h(src, src_key_fn, n, gi, h_off):
            for kc in range(KC):
                s_t, s_key = next_sq()
                P.op("act", lambda e, kc=kc, s_t=s_t: e.activation(out=s_t[:, 0:n], in_=src[:, kc, 0:n], func=AF.Square),
                     reads=[src_key_fn(kc)], writes=[s_key])
                ssum_mm(s_t, s_key, n, kc == 0, kc == KC - 1)
            r_t, r_key = finish_rstd(n)
            for kc in range(KC):
                apply_h(hT[:, kc, h_off:h_off + n], ("hT", kc), src[:, kc, 0:n], src_key_fn(kc), gain_ap(gi, kc),
                        r_t[:, 0:n], r_key, n, POOL_H)

        def mm_group(out_ap, out_keys, slot, rhs_fn, rhs_keys_fn, nk, first=True, last=True, k_off=0):
            for kk in range(nk):
                rhs = rhs_fn(k_off + kk)
                P.op("pe", lambda e, kk=kk, rhs=rhs: e.matmul(out_ap, wsl[slot][:, kk, :], rhs,
                                                              start=(first and kk == 0), stop=(last and kk == nk - 1)),
                     reads=[("w", slot), rhs_keys_fn(k_off + kk)], writes=out_keys)

        def residual_phase(n, gi, src_is_u=True):
            r_t, r_key = finish_rstd(n)
            for oc in range(KC):
                t_t, t_key = next_tmp()
                P.op(VEC2 if (POOL_RES and oc % 2 == 1) else "dve", lambda e, oc=oc, t_t=t_t: e.tensor_tensor(
                    out=t_t[:, 0:n], in0=u[:, oc, 0:n], in1=r_t[:, 0:n], op=ALU.mult),
                    reads=[("u", oc), r_key], writes=[t_key])
                P.op("dve", lambda e, oc=oc, t_t=t_t: e.tensor_tensor(
                    out=xs[:, oc, 0:n], in0=xs[:, oc, 0:n], in1=t_t[:, 0:n], op=ALU.add),
                    reads=[("xs", oc), t_key], writes=[("xs", oc)])

        def out_proj_phase(n, tile_base, gi, rhs_fn, rhs_keys_fn, pieces, hook=None):
            pending = None
            for oc in range(KC):
                bank = next_bank()
                c0 = bank * 512
                o_ap = PS[:, c0:c0 + n]
                for pi, (k0, kn) in enumerate(pieces):
                    slot = ws_next(tile_base + oc * len(pieces) + pi)
                    mm_group(o_ap, psk(c0, c0 + 512), slot, rhs_fn, rhs_keys_fn, kn,
                             first=(pi == 0), last=(pi == len(pieces) - 1), k_off=k0)
                if pending is not None:
                    ssum_mm(*pending)
                    pending = None
                if hook is not None:
                    hook(oc)
                P.op("act", lambda e, oc=oc, o_ap=o_ap: e.mul(out=u[:, oc, 0:n], in_=o_ap, mul=gain_ap(gi, oc)),
                     reads=psk(c0, c0 + 512) + ["cs"], writes=[("u", oc)])
                s_t, s_key = next_sq()
                P.op("act", lambda e, s_t=s_t, o_ap=o_ap: e.activation(out=s_t[:, 0:n], in_=o_ap, func=AF.Square),
                     reads=psk(c0, c0 + 512), writes=[s_key])
                pending = (s_t, s_key, n, oc == 0, oc == KC - 1)
            ssum_mm(*pending)

        def do_block(b, chunk_base):
            nch = BLOCKS[b]
            T = nch * CH
            pre = HALO if b == 0 else 0
            tok0 = chunk_base * CH
            last_block = (b == len(BLOCKS) - 1) and "nolast" not in DBG
            if b == 0:
                P.op("sp", lambda e: e.dma_start(out=u[:, :, 0:HALO], in_=xT_v[:, :, 0:HALO]),
                     writes=[("u", kc) for kc in range(KC)], dma_key="halo")
            for qd in range(4):
                P.op(IOQ, lambda e, qd=qd: e.dma_start(out=xs[:, qd * 4:(qd + 1) * 4, 0:T],
                                                        in_=xT_v[:, qd * 4:(qd + 1) * 4, HALO + tok0: HALO + tok0 + T]),
                     writes=[("xs", kc) for kc in range(qd * 4, qd * 4 + 4)], dma_key=("xl", qd))
            if b == 0:
                norm_to_h(u, lambda kc: ("u", kc), HALO, 0, 0)
            if not (PREFETCH and b > 0):
                norm_to_h(xs, lambda kc: ("xs", kc), T, 0, pre)

            have_next = PREFETCH and (b + 1 < n_blocks)
            if have_next:
                Tn = BLOCKS[b + 1] * CH
                xoff_n = HALO + tok0 + T
            pf = {"rstd": None}

            NPF = 2 * KC

            def pf_issue(i):
                kc, si = i % KC, i % 4
                P.op(IOQ, lambda e: e.dma_start(out=stg[si][:, 0:Tn], in_=xT_v[:, kc, xoff_n:xoff_n + Tn]),
                     writes=[("stg", si)], dma_key=("stgl", si))

            def pf_process(i):
                kc, si = i % KC, i % 4
                if i < KC:
                    s_t, s_key = next_sq()
                    P.op("act", lambda e, s_t=s_t: e.activation(out=s_t[:, 0:Tn], in_=stg[si][:, 0:Tn], func=AF.Square),
                         reads=[("stg", si)], writes=[s_key])
                    ssum_mm(s_t, s_key, Tn, kc == 0, kc == KC - 1, bank=7)
                    if kc == KC - 1:
                        pf["rstd"] = finish_rstd(Tn, bank=7)
                else:
                    r_t, r_key = pf["rstd"]
                    apply_h(hT[:, kc, 0:Tn], ("hT", kc), stg[si][:, 0:Tn], ("stg", si), gain_ap(0, kc),
                            r_t[:, 0:Tn], r_key, Tn, POOL_H)

            pf_state = {"done": 0}

            def pf_hook(oc):
                if not have_next:
                    return
                if oc == 0:
                    for i in range(4):
                        pf_issue(i)
                    return
                target = min(NPF, -(-oc * NPF // 12))
                while pf_state["done"] < target:
                    i = pf_state["done"]
                    pf_process(i)
                    if i + 4 < NPF:
                        pf_issue(i + 4)
                    pf_state["done"] += 1

            NW = pre + T

            def h_ctx(kc):
                return hT[:, kc, 0:NW]

            def h_main(kc):
                return hT[:, kc, pre:pre + T]

            def h_key(kc):
                return ("hT", kc)

            ctx_chunks = []
            if b == 0:
                ctx_chunks += [(0, 0), (1, 64)]
            for ci in range(nch):
                ctx_chunks.append(((chunk_base + ci + 2) % 16, pre + ci * CH))

            for kp in range(2):
                slot = ws_next(T_K + kp)
                bank = next_bank()
                c0 = bank * 512
                mm_group(PS[:, c0:c0 + NW], psk(c0, c0 + 512), slot, h_ctx, h_key, KC)
                for (cslot, col) in ctx_chunks:
                    P.op("act", lambda e, kp=kp, cslot=cslot, col=col, c0=c0: e.activation(
                        out=kT[:, kp, cslot * CH:(cslot + 1) * CH], in_=PS[:, c0 + col:c0 + col + CH], func=AF.Copy),
                        reads=psk(c0, c0 + 512), writes=[("kT", kp, cslot)])
                if last_block:
                    pc = (nch - 4) * CH
                    P.op("act", lambda e, kp=kp, c0=c0, pc=pc: e.activation(out=kop_s[:, kp, :], in_=PS[:, c0 + pc:c0 + pc + 128], func=AF.Copy),
                         reads=psk(c0, c0 + 512), writes=["kop_s"])
                    sc = (nch - 2) * CH
                    P.op("act", lambda e, kp=kp, c0=c0, sc=sc: e.activation(
                        out=kos_s[:, kp, :, :], in_=PS[:, c0 + sc:c0 + sc + 128].rearrange("p (s t) -> p s t", s=2), func=AF.Copy),
                        reads=psk(c0, c0 + 512), writes=["kos_s"])
            for vt in range(2):
                slot = ws_next(T_V + vt)
                for k_i, (cslot, col) in enumerate(ctx_chunks):
                    vc0 = (6 + k_i % 2) * 512
                    for kc in range(KC):
                        P.op("pe", lambda e, kc=kc, col=col, vc0=vc0, slot=slot: e.matmul(
                            PS[0:64, vc0:vc0 + 128], hT[:, kc, col:col + CH], wsl[slot][:, kc, :],
                            start=(kc == 0), stop=(kc == KC - 1)),
                            reads=[("w", slot), ("hT", kc)], writes=psk(vc0, vc0 + 128))
                    P.op("act", lambda e, cslot=cslot, vt=vt, vc0=vc0: e.activation(
                        out=Vr[:, cslot, vt * 128:(vt + 1) * 128], in_=PS[0:64, vc0:vc0 + 128], func=AF.Copy),
                        reads=psk(vc0, vc0 + 128), writes=[("Vr", cslot)])
                    if last_block:
                        ci = k_i
                        if nch - 4 <= ci < nch - 2:
                            P.op("act", lambda e, ci=ci, vt=vt, vc0=vc0: e.activation(
                                out=vop_s[:, ci - (nch - 4), vt * 128:(vt + 1) * 128], in_=PS[0:64, vc0:vc0 + 128], func=AF.Copy),
                                reads=psk(vc0, vc0 + 128), writes=["vop_s"])
                        if ci >= nch - 2:
                            P.op("act", lambda e, ci=ci, vt=vt, vc0=vc0: e.activation(
                                out=vos_s[:, ci - (nch - 2), vt * 128:(vt + 1) * 128], in_=PS[0:64, vc0:vc0 + 128], func=AF.Copy),
                                reads=psk(vc0, vc0 + 128), writes=["vos_s"])
            for jq in range(8):
                slot = ws_next(T_Q + jq)
                bank = next_bank()
                c0 = bank * 512
                mm_group(PS[:, c0:c0 + T], psk(c0, c0 + 512), slot, h_main, h_key, KC)
                P.op("act", lambda e, jq=jq, c0=c0: e.activation(out=act[:, 16 + jq, 0:T], in_=PS[:, c0:c0 + T], func=AF.Copy),
                     reads=psk(c0, c0 + 512), writes=[("act", 16 + jq)])

            units = []
            for ci in range(nch):
                for kh in range(NKV):
                    units.append((ci, kh))

            def unit_S(n):
                ci, kh = units[n]
                side, kp = kh % 2, kh // 2
                par = n % 2
                sc0 = par * 768
                gci = chunk_base + ci
                is_sample = last_block and ci >= nch - 2 and "nosample" not in DBG
                p0 = side * 64
                q_ap = act[p0:p0 + 64, 16 + 4 * kp:16 + 4 * kp + 4, ci * CH:(ci + 1) * CH]
                for j in range(3):
                    if is_sample and j < 2:
                        s_i = ci - (nch - 2)
                        k_ap = kcb[p0:p0 + 64, s_i, kp, j * 64:(j + 1) * 64]
                        k_key = "kcb"
                    else:
                        cslot = (gci + j) % 16
                        k_ap = kT[p0:p0 + 64, kp, cslot * CH:(cslot + 1) * CH]
                        k_key = ("kT", kp, cslot)
                    P.op("pe", lambda e, j=j, k_ap=k_ap: e.matmul(PS[0:64, sc0 + j * 256: sc0 + (j + 1) * 256], k_ap, q_ap,
                                                                 start=True, stop=True),
                         reads=[k_key] + [("act", 16 + 4 * kp + g) for g in range(4)],
                         writes=psk(sc0, sc0 + 768))
                P.op("act", lambda e: e.activation(out=Ebuf[par][:, :], in_=PS[0:64, sc0:sc0 + 768], func=AF.Exp, scale=HD ** -0.5),
                     reads=psk(sc0, sc0 + 768), writes=[("E", par)])
                P.op(VEC2 if POOL_EB else "dve", lambda e: e.tensor_tensor(out=PT[par][:, :], in0=Ebuf[par][:, :],
                                                      in1=EB[:, kh, :, :].rearrange("p j c -> p (j c)"), op=ALU.mult),
                     reads=[("E", par), "EB"], writes=[("PT", par)])
                if b == 0 and ci < 2:
                    w_m = 512 if ci == 0 else 256
                    P.op("dve", lambda e: e.tensor_scalar(out=PT[par][:, 0:w_m], in0=PT[par][:, 0:w_m],
                                                          scalar1=cs[0:64, 88:89], scalar2=0.0, op0=ALU.mult, op1=ALU.add),
                         reads=[("PT", par), "cs"], writes=[("PT", par)])

            def unit_PV(n):
                ci, kh = units[n]
                side, kp = kh % 2, kh // 2
                par = n % 2
                oc0 = (3 + par) * 512
                gci = chunk_base + ci
                is_sample = last_block and ci >= nch - 2 and "nosample" not in DBG
                p0 = side * 64
                for j in range(3):
                    if is_sample and j < 2:
                        s_i = ci - (nch - 2)
                        v_ap = vcb[:, s_i, j, kh * 64:(kh + 1) * 64]
                        v_key = "vcb"
                    else:
                        cslot = (gci + j) % 16
                        v_ap = Vr[:, cslot, kh * 64:(kh + 1) * 64]
                        v_key = ("Vr", cslot)
                    P.op("pe", lambda e, j=j, v_ap=v_ap: e.matmul(PS[0:64, oc0:oc0 + 256], v_ap, PT[par][:, j * 256:(j + 1) * 256],
                                                                 start=(j == 0), stop=(j == 2)),
                         reads=[v_key, ("PT", par)], writes=psk(oc0, oc0 + 256))
                for j in range(3):
                    P.op("pe", lambda e, j=j: e.matmul(PS[0:64, oc0 + 256:oc0 + 512], ones[0:64, 0:64], PT[par][:, j * 256:(j + 1) * 256],
                                                       start=(j == 0), stop=(j == 2 and not SINK_MM)),
                         reads=["ones", ("PT", par)], writes=psk(oc0 + 256, oc0 + 512))
                if SINK_MM:
                    P.op("pe", lambda e: e.matmul(PS[0:64, oc0 + 256:oc0 + 512], ones[0:1, 0:64], bass.AP(sinkb, kh * 4, [[NH, 1], [1, 4], [0, 64]]),
                                                  start=False, stop=True),
                         reads=["ones", "sinkb"], writes=psk(oc0 + 256, oc0 + 512))
                    P.op("act", lambda e: e.activation(out=rec[par][:, :], in_=PS[0:64, oc0 + 256:oc0 + 512], func=AF.Ln),
                         reads=psk(oc0 + 256, oc0 + 512), writes=[("rec", par)])
                else:
                    sk = bass.AP(sinkx, kh * 4, [[NH, 64], [1, 4], [0, 64]])
                    P.op("dve", lambda e: e.tensor_tensor(out=rec[par][:, :].rearrange("p (g q) -> p g q", g=4),
                                                          in0=PS[0:64, oc0 + 256:oc0 + 512].rearrange("p (g q) -> p g q", g=4),
                                                          in1=sk, op=ALU.add),
                         reads=psk(oc0 + 256, oc0 + 512) + ["sinkx"], writes=[("rec", par)])
                if ACT_RECIP and not SINK_MM:
                    P.op("act", lambda e: e.activation(out=rec[par][:, :], in_=rec[par][:, :], func=AF.Ln),
                         reads=[("rec", par)], writes=[("rec", par)])
                if ACT_RECIP:
                    P.op("act", lambda e: e.activation(out=rec[par][:, :], in_=rec[par][:, :], func=AF.Exp, scale=-1.0),
                         reads=[("rec", par)], writes=[("rec", par)])
                if not ACT_RECIP:
                    P.op("dve", lambda e: e.reciprocal(out=rec[par][:, :], in_=rec[par][:, :]),
                         reads=[("rec", par)], writes=[("rec", par)])
                mc0 = 8 + 4 * kp
                P.op("dve", lambda e: e.tensor_tensor(
                    out=act[p0:p0 + 64, mc0:mc0 + 4, ci * CH:(ci + 1) * CH],
                    in0=rec[par][:, :].rearrange("p (g q) -> p g q", g=4),
                    in1=PS[0:64, oc0:oc0 + 256].rearrange("p (g q) -> p g q", g=4), op=ALU.mult),
                    reads=[("rec", par)] + psk(oc0, oc0 + 256), writes=[("act", mc0 + g) for g in range(4)])

            n_units = len(units)
            n_slots = 24
            s_done = 0
            pv_done = 0

            def attn_step(target_s):
                nonlocal s_done, pv_done
                while pv_done < s_done - 1:
                    unit_PV(pv_done)
                    pv_done += 1
                while s_done < target_s:
                    while pv_done < s_done - 1:
                        unit_PV(pv_done)
                        pv_done += 1
                    unit_S(s_done)
                    s_done += 1

            if last_block and "nosegs" not in DBG:
                segs = [(0, (nch - 2) * CH, "prev"), ((nch - 2) * CH, CH, ("s", 0)), ((nch - 1) * CH, CH, ("s", 1))]
            else:
                segs = [(0, T, "prev")]
            slot_i = 0
            for j in range(8):
                slot = ws_next(T_CONV + 3 * j)
                c_cc = 5 * 512
                mm_group(PS[:, c_cc:c_cc + NW], psk(c_cc, c_cc + 512), slot, h_ctx, h_key, KC)
                attn_step(math.ceil((slot_i + 1) * n_units / n_slots)); slot_i += 1
                ccs_t, ccs_key = next_tmp()
                P.op("act", lambda e, ccs_t=ccs_t: e.activation(out=ccs_t[:, 0:NW], in_=PS[:, c_cc:c_cc + NW], func=AF.Copy),
                     reads=psk(c_cc, c_cc + 512), writes=[ccs_key])
                slot = ws_next(T_CONV + 3 * j + 1)
                c_cu = 6 * 512
                mm_group(PS[:, c_cu:c_cu + NW], psk(c_cu, c_cu + 512), slot, h_ctx, h_key, KC)
                attn_step(math.ceil((slot_i + 1) * n_units / n_slots)); slot_i += 1
                G_t, G_key = next_tmp()
                goff = 0
                seg_info = []
                for (toff, L, src) in segs:
                    Lc = L + (pre if toff == 0 else 0)
                    coff = toff + (pre if toff != 0 else 0)
                    seg_info.append((goff, coff, Lc, toff, L, src))
                    goff += 2 + Lc
                GW = goff
                for (go, coff, Lc, toff, L, src) in seg_info:
                    if src == "prev":
                        if b > 0:
                            P.op("dve", lambda e, G_t=G_t, go=go, j=j: e.tensor_copy(out=G_t[:, go:go + 2], in_=gst[:, j, :]),
                                 reads=["gst"], writes=[G_key])
                    else:
                        P.op("dve", lambda e, G_t=G_t, go=go, j=j, si=src[1]: e.tensor_copy(out=G_t[:, go:go + 2], in_=scs[:, si, j, :]),
                             reads=["scs"], writes=[G_key])
                    P.op("dve", lambda e, G_t=G_t, go=go, coff=coff, Lc=Lc, ccs_t=ccs_t: e.tensor_tensor(
                        out=G_t[:, go + 2:go + 2 + Lc], in0=ccs_t[:, coff:coff + Lc], in1=PS[:, c_cu + coff:c_cu + coff + Lc], op=ALU.mult),
                        reads=[ccs_key] + psk(c_cu, c_cu + 512), writes=[G_key])
                go0, _, Lc0, _, _, _ = seg_info[0]
                if not last_block:
                    P.op("dve", lambda e, G_t=G_t, a=go0 + Lc0, j=j: e.tensor_copy(out=gst[:, j, :], in_=G_t[:, a:a + 2]),
                         reads=[G_key], writes=["gst"])
                else:
                    for si_, (go, coff, Lc, toff, L, src) in enumerate(seg_info):
                        P.op("dve", lambda e, G_t=G_t, a=go + Lc, j=j, si_=si_: e.tensor_copy(out=gout_s[:, j, si_, :], in_=G_t[:, a:a + 2]),
                             reads=[G_key], writes=["gout_s"])
                A_t, A_key = next_tmp()
                WA = GW - 2
                P.op("act", lambda e, A_t=A_t, G_t=G_t, j=j: e.mul(out=A_t[:, 0:WA], in_=G_t[:, 0:WA], mul=convw_ap(0, j)),
                     reads=[G_key, "cs"], writes=[A_key])
                for tap in (1, 2):
                    if POOL_TAPS:
                        B_t, B_key = next_tmp()
                        P.op("act", lambda e, B_t=B_t, G_t=G_t, j=j, tap=tap: e.mul(out=B_t[:, 0:WA], in_=G_t[:, tap:tap + WA],
                                                                                    mul=convw_ap(tap, j)),
                             reads=[G_key, "cs"], writes=[B_key])
                        P.op(VEC2, lambda e, A_t=A_t, B_t=B_t: e.tensor_tensor(out=A_t[:, 0:WA], in0=A_t[:, 0:WA],
                                                                                 in1=B_t[:, 0:WA], op=ALU.add),
                             reads=[A_key, B_key], writes=[A_key])
                    else:
                        P.op("dve", lambda e, A_t=A_t, G_t=G_t, j=j, tap=tap: e.scalar_tensor_tensor(
                            out=A_t[:, 0:WA], in0=G_t[:, tap:tap + WA], scalar=convw_ap(tap, j), in1=A_t[:, 0:WA],
                            op0=ALU.mult, op1=ALU.add),
                            reads=[G_key, A_key, "cs"], writes=[A_key])
                slot = ws_next(T_CONV + 3 * j + 2)
                c_cb = 7 * 512
                mm_group(PS[:, c_cb:c_cb + T], psk(c_cb, c_cb + 512), slot, h_main, h_key, KC)
                attn_step(math.ceil((slot_i + 1) * n_units / n_slots)); slot_i += 1
                for (go, coff, Lc, toff, L, src) in seg_info:
                    a0 = go + (Lc - L)
                    P.op("dve", lambda e, A_t=A_t, a0=a0, toff=toff, L=L, j=j: e.tensor_tensor(
                        out=act[:, j, toff:toff + L], in0=A_t[:, a0:a0 + L], in1=PS[:, c_cb + toff:c_cb + toff + L], op=ALU.mult),
                        reads=[A_key] + psk(c_cb, c_cb + 512), writes=[("act", j)])
            attn_step(n_units)
            while pv_done < s_done:
                unit_PV(pv_done)
                pv_done += 1

            out_proj_phase(T, T_OUT, 1, lambda kc: act[:, kc, 0:T], lambda kc: ("act", kc), [(0, 16)])
            residual_phase(T, 1)
            def h2(kc):
                return hT[:, kc, 0:T]

            h2_aps = [hT[:, kc, 0:T] for kc in range(KC)]
            KMAJ = 2 if (LATE_RSTD and NS >= 5) else 0
            pre_banks = {}
            if KMAJ:
                g0 = ws["req"]
                assert g0 % NT == T_GU and ws["issued"] >= g0 + 2 * KMAJ
                kslots = [(g0 + k) % NS for k in range(2 * KMAJ)]
                kcols = [next_bank() * 512 for k in range(2 * KMAJ)]
                ws["req"] = g0 + 2 * KMAJ
                for f in range(KMAJ):
                    pre_banks[f] = (kcols[2 * f], kcols[2 * f + 1])
            if LATE_RSTD:
                for kc in range(KC):
                    P.op("act", lambda e, kc=kc: e.mul(out=hT[:, kc, 0:T], in_=xs[:, kc, 0:T], mul=gain_ap(2, kc)),
                         reads=[("xs", kc), "cs"], writes=[("hT", kc)])
                    s_t, s_key = next_sq()
                    P.op("act", lambda e, kc=kc, s_t=s_t: e.activation(out=s_t[:, 0:T], in_=xs[:, kc, 0:T], func=AF.Square),
                         reads=[("xs", kc)], writes=[s_key])
                    for k in range(2 * KMAJ):
                        P.op("pe", lambda e, kc=kc, k=k: e.matmul(PS[:, kcols[k]:kcols[k] + T], wsl[kslots[k]][:, kc, :], h2_aps[kc],
                                                                   start=(kc == 0), stop=(kc == KC - 1)),
                             reads=[("w", kslots[k]), ("hT", kc)], writes=psk(kcols[k], kcols[k] + 512))
                    ssum_mm(s_t, s_key, T, kc == 0, kc == KC - 1)
                r2_t, r2_key = finish_rstd(T)
            else:
                norm_to_h(xs, lambda kc: ("xs", kc), T, 2, 0)
            for fc in range(FC):
                if fc in pre_banks:
                    cg, cu_ = pre_banks[fc]
                else:
                    slot = ws_next(T_GU + 2 * fc)
                    bg = next_bank()
                    cg = bg * 512
                    mm_group(PS[:, cg:cg + T], psk(cg, cg + 512), slot, h2, h_key, KC)
                    slot = ws_next(T_GU + 2 * fc + 1)
                    bu = next_bank()
                    cu_ = bu * 512
                    mm_group(PS[:, cu_:cu_ + T], psk(cu_, cu_ + 512), slot, h2, h_key, KC)
                s_t, s_key = next_tmp()
                if LATE_RSTD:
                    w_t, w_key = next_tmp()
                    P.op("dve", lambda e, s_t=s_t, cg=cg: e.tensor_tensor(out=s_t[:, 0:T], in0=r2_t[:, 0:T], in1=PS[:, cg:cg + T], op=ALU.mult),
                         reads=[r2_key] + psk(cg, cg + 512), writes=[s_key])
                    P.op("act", lambda e, s_t=s_t: e.activation(out=s_t[:, 0:T], in_=s_t[:, 0:T], func=AF.Silu),
                         reads=[s_key], writes=[s_key])
                    P.op("dve", lambda e, w_t=w_t, cu_=cu_: e.tensor_tensor(out=w_t[:, 0:T], in0=r2_t[:, 0:T], in1=PS[:, cu_:cu_ + T], op=ALU.mult),
                         reads=[r2_key] + psk(cu_, cu_ + 512), writes=[w_key])
                    P.op("dve", lambda e, s_t=s_t, w_t=w_t, fc=fc: e.tensor_tensor(
                        out=act[:, fc, 0:T], in0=s_t[:, 0:T], in1=w_t[:, 0:T], op=ALU.mult),
                        reads=[s_key, w_key], writes=[("act", fc)])
                else:
                    P.op("act", lambda e, s_t=s_t, cg=cg: e.activation(out=s_t[:, 0:T], in_=PS[:, cg:cg + T], func=AF.Silu),
                         reads=psk(cg, cg + 512), writes=[s_key])
                    P.op("dve", lambda e, s_t=s_t, cu_=cu_, fc=fc: e.tensor_tensor(
                        out=act[:, fc, 0:T], in0=s_t[:, 0:T], in1=PS[:, cu_:cu_ + T], op=ALU.mult),
                        reads=[s_key] + psk(cu_, cu_ + 512), writes=[("act", fc)])
            out_proj_phase(T, T_DN, 3, lambda kc: act[:, kc, 0:T], lambda kc: ("act", kc), DN_PIECES, hook=pf_hook)
            residual_phase(T, 3)
            for qd in range(4):
                P.op(IOQ, lambda e, qd=qd, tok0=tok0, T=T: e.dma_start(out=yT_v[:, qd * 4:(qd + 1) * 4, tok0:tok0 + T],
                                                                     in_=xs[:, qd * 4:(qd + 1) * 4, 0:T]),
                     reads=[("xs", kc) for kc in range(qd * 4, qd * 4 + 4)], dma_key=("ys", qd))
            if last_block and "noouts" not in DBG:
                P.op(IOQ, lambda e: e.dma_start(out=g_out, in_=gout_s[:]), reads=["gout_s"], dma_key="outs")
                P.op(IOQ, lambda e: e.dma_start(out=ko_p, in_=kop_s[:]), reads=["kop_s"], dma_key="outs")
                P.op(IOQ, lambda e: e.dma_start(out=ko_s, in_=kos_s[:]), reads=["kos_s"], dma_key="outs")
                P.op(IOQ, lambda e: e.dma_start(out=vo_p, in_=vop_s[:]), reads=["vop_s"], dma_key="outs")
                P.op(IOQ, lambda e: e.dma_start(out=vs_out[:, 64:128, :].rearrange("s p f -> p s f"), in_=vos_s[:]),
                     reads=["vos_s"], dma_key="outs")

        chunk_base = 0
        for b in range(n_blocks):
            do_block(b, chunk_base)
            chunk_base += BLOCKS[b]

        P.emit(nc)
        _NC_CACHE["sig_counts"] = P.sig_counts
    return nc


def _t5_bucket_np(rel):
    half = NBUCK // 2
    max_exact = half // 2
    try:
        import jax
        import jax.numpy as jnp
        with jax.default_device(jax.devices("cpu")[0]):
            r = jnp.asarray(rel, dtype=jnp.int32)
            ret = jnp.where(r > 0, half, 0)
            n = jnp.abs(r)
            nf = jnp.maximum(n, 1).astype(jnp.float32)
            large = max_exact + (jnp.log(nf / max_exact) / math.log(128 / max_exact) * (half - max_exact)).astype(jnp.int32)
            large = jnp.minimum(large, half - 1)
            return np.asarray(ret + jnp.where(n < max_exact, n, large))
    except Exception:
        rel = np.asarray(rel, dtype=np.int32)
        ret = np.where(rel > 0, half, 0)
        n = np.abs(rel)
        nf = np.maximum(n, 1).astype(np.float32)
        large = max_exact + (np.log(nf / np.float32(max_exact)) / np.float32(math.log(128 / max_exact))
                             * np.float32(half - max_exact)).astype(np.int32)
        large = np.minimum(large, half - 1)
        return ret + np.where(n < max_exact, n, large)


def _tiles(W, col_lists=None, row_perm=None):
    K = W.shape[0]
    if row_perm is not None:
        W = W[row_perm]
    out = []
    for cols in col_lists:
        blk = W[:, cols]
        out.append(blk.reshape(K // 128, 128, 128).transpose(1, 0, 2))
    return out


def _q_head_pairs():
    pairs = []
    for kp in range(2):
        for g in range(4):
            pairs.append((8 * kp + g, 8 * kp + 4 + g))
    return pairs


def _pack_weights(w_in, w_out, w_gate, w_up, w_down):
    tiles = [None] * NT
    ar = np.arange
    for j in range(8):
        cols_cb = ar(j * 128, (j + 1) * 128)
        cols_cc = CW + cols_cb
        cols_cu = 2 * CW + cols_cb
        t = _tiles(w_in, [cols_cc, cols_cu, cols_cb])
        tiles[T_CONV + 3 * j: T_CONV + 3 * j + 3] = t
    qb = 3 * CW
    kb = qb + NH * HD
    vb = kb + NKV * HD
    tiles[T_K:T_K + 2] = _tiles(w_in, [ar(kb + kp * 128, kb + (kp + 1) * 128) for kp in range(2)])
    tiles[T_V:T_V + 2] = _tiles(w_in, [ar(vb + vt * 128, vb + (vt + 1) * 128) for vt in range(2)])
    pairs = _q_head_pairs()
    tiles[T_Q:T_Q + 8] = _tiles(w_in, [np.concatenate([ar(qb + a * 64, qb + (a + 1) * 64), ar(qb + bb * 64, qb + (bb + 1) * 64)])
                                       for (a, bb) in pairs])
    rp = [ar(0, CW)]
    for (a, bb) in pairs:
        rp.append(ar(CW + a * 64, CW + (a + 1) * 64))
        rp.append(ar(CW + bb * 64, CW + (bb + 1) * 64))
    rp = np.concatenate(rp)
    tiles[T_OUT:T_OUT + 16] = _tiles(w_out, [ar(oc * 128, (oc + 1) * 128) for oc in range(16)], row_perm=rp)
    tg = _tiles(w_gate, [ar(f * 128, (f + 1) * 128) for f in range(FC)])
    tu = _tiles(w_up, [ar(f * 128, (f + 1) * 128) for f in range(FC)])
    for f in range(FC):
        tiles[T_GU + 2 * f] = tg[f]
        tiles[T_GU + 2 * f + 1] = tu[f]
    td = _tiles(w_down, [ar(oc * 128, (oc + 1) * 128) for oc in range(16)])
    for oc in range(16):
        for pi, (k0, kn) in enumerate(DN_PIECES):
            t = np.zeros((128, 16, 128), np.float32)
            t[:, 0:kn, :] = td[oc][:, k0:k0 + kn, :]
            tiles[T_DN + 3 * oc + pi] = t
    return np.ascontiguousarray(np.stack(tiles, 0).astype(np.float32))


_NC_CACHE = {}


def _prepare(x_prompt, x_sample, state_conv, cache_k, cache_v, rel_table, g_pre_mix, w_in, conv_w, attn_sinks,
             w_out, g_post_mix, g_pre_ffn, w_gate, w_up, w_down, g_post_ffn):
    f = np.float32
    x_prompt = np.asarray(x_prompt, f)
    x_sample = np.asarray(x_sample, f)
    wst = _pack_weights(np.asarray(w_in, f)[0], np.asarray(w_out, f)[0], np.asarray(w_gate, f)[0],
                        np.asarray(w_up, f)[0], np.asarray(w_down, f)[0])
    cst0 = np.zeros((128, 112), f)
    for gi, gv in enumerate((g_pre_mix, g_post_mix, g_pre_ffn, g_post_ffn)):
        cst0[:, gi * 16:(gi + 1) * 16] = np.asarray(gv, f)[0].reshape(16, 128).T
    cw = np.asarray(conv_w, f)[0]
    for tap in range(3):
        cst0[:, 64 + tap * 8: 64 + tap * 8 + 8] = cw[tap].reshape(8, 128).T
    cst0[:, 89:105] = np.asarray(attn_sinks, f)[0][None, :]
    tab = np.ascontiguousarray(np.asarray(rel_table, f))
    rel = np.arange(256) - 191
    bk = _t5_bucket_np(rel)
    oh1 = np.zeros((NBUCK, 260), f)
    oh1[bk[:255], np.arange(255)] = 1.0
    oh = np.zeros((128, 256), f)
    for ql in range(4):
        oh[ql * 32:(ql + 1) * 32, :] = oh1[:, 3 - ql: 3 - ql + 256]
    sc = np.asarray(state_conv, f)[0]
    ck = np.asarray(cache_k, f)[0].reshape(16, 128, 256)
    cv = np.asarray(cache_v, f)[0].reshape(16, 128, 256)

    in_maps = []
    for c in range(NCORES):
        bi, qt = c // 4, c % 4
        xT = np.zeros((D, NTOK), f)
        if qt > 0:
            xT[:, 0:HALO] = x_prompt[bi, qt * PTOK - HALO: qt * PTOK].T
        xT[:, HALO:HALO + PTOK] = x_prompt[bi, qt * PTOK:(qt + 1) * PTOK].T
        xT[:, HALO + PTOK:] = x_sample[2 * c:2 * c + 2].reshape(STOK, D).T
        cst = cst0.copy()
        cst[:, 88] = 1.0 if qt > 0 else 0.0
        ckT = ck[2 * c:2 * c + 2].transpose(2, 0, 1).reshape(2, 128, 2, 128).transpose(1, 2, 0, 3)
        scv = sc[2 * c:2 * c + 2].transpose(2, 0, 1).reshape(8, 128, 2, 2).transpose(1, 2, 0, 3)
        in_maps.append({
            "xT": np.ascontiguousarray(xT), "wst": wst, "cst": cst, "tab": tab, "oh": oh,
            "ckT": np.ascontiguousarray(ckT), "ckn": np.ascontiguousarray(ck[2 * c:2 * c + 2]),
            "cvn": np.ascontiguousarray(cv[2 * c:2 * c + 2]), "scv": np.ascontiguousarray(scv),
        })
    return in_maps


def _assemble(R):
    f = np.float32
    y_prompt = np.zeros((2, 8192, D), f)
    y_sample = np.zeros((16, 64, D), f)
    ncp = np.zeros((1, 2, 2, CW), f)
    nkp = np.zeros((1, 2, 128, NKV, HD), f)
    nvp = np.zeros((1, 2, 128, NKV, HD), f)
    ncs = np.zeros((1, 16, 2, CW), f)
    nks = np.zeros((1, 16, 128, NKV, HD), f)
    nvs = np.zeros((1, 16, 128, NKV, HD), f)
    for c in range(NCORES):
        bi, qt = c // 4, c % 4
        r = R[c]
        yT = np.asarray(r["yT"])
        y_prompt[bi, qt * PTOK:(qt + 1) * PTOK] = yT[:, 0:PTOK].T
        y_sample[2 * c:2 * c + 2] = yT[:, PTOK:].T.reshape(2, 64, D)
        go = np.asarray(r["g_out"])
        for s in range(2):
            ncs[0, 2 * c + s] = go[:, :, 1 + s, :].transpose(2, 1, 0).reshape(2, CW)
            kn = np.asarray(r["ko_s"])[:, :, s, :]
            nks[0, 2 * c + s, 0:64] = np.asarray(r["kc_copy"])[s].reshape(64, NKV, HD)
            nks[0, 2 * c + s, 64:128] = kn.transpose(2, 1, 0).reshape(64, NKV, HD)
            nvs[0, 2 * c + s] = np.asarray(r["vs_out"])[s].reshape(128, NKV, HD)
        if qt == 3:
            ncp[0, bi] = go[:, :, 0, :].transpose(2, 1, 0).reshape(2, CW)
            nkp[0, bi] = np.asarray(r["ko_p"]).transpose(2, 1, 0).reshape(128, NKV, HD)
            nvp[0, bi] = np.asarray(r["vo_p"]).transpose(1, 0, 2).reshape(128, NKV, HD)
    return (y_prompt, y_sample, ncp, nkp, nvp, ncs, nks, nvs)


def kernel(**inputs):
    in_maps = _prepare(**inputs)
    if "nc" not in _NC_CACHE:
        _NC_CACHE["nc"] = build_program()
    nc = _NC_CACHE["nc"]
    res = run_bass_kernel_spmd(nc, in_maps, core_ids=list(range(NCORES)))
    return _assemble(res.results)
```

```python
import math
import contextlib
import numpy as np
import concourse.bass as bass
import concourse.mybir as mybir
from concourse.bass_utils import run_bass_kernel_spmd

F32 = mybir.dt.float32
BF16 = mybir.dt.bfloat16
AF = mybir.ActivationFunctionType
ALU = mybir.AluOpType

D = 2048
KC = 16
CW = 1024
NH = 16
NKV = 4
HD = 64
DFF = 5632
FC = 44
IN_COLS = 4608
NCORES = 8
PTOK = 2048
HALO = 128
STOK = 128
NTOK = HALO + PTOK + STOK
NOUT = PTOK + STOK
CH = 64
BLOCKS = [6, 7, 7, 7, 7]
TMAX = 448
EPS = 1e-6
NS = 5
NBUCK = 32

T_K = 0
T_V = 2
T_Q = 4
T_CONV = 12
T_OUT = 36
T_GU = 52
T_DN = 140
NT = 188
DN_PIECES = [(0, 16), (16, 16), (32, 12)]

ENGS = ("pe", "act", "dve", "pool", "sp")
LATE_RSTD = True
SINK_MM = True
VEC2 = "dve"
POOL_H = True
POOL_RES = True
POOL_EB = True
POOL_TAPS = True
ACT_RECIP = True
SCRATCH = False
IOQ = "pool" if SCRATCH else "sp"
PREFETCH = True
import os
DBG = set(os.environ.get("KDBG", "").split(","))


class _Ins:
    __slots__ = ("eng", "idx", "fn", "deps", "dma_key", "signals", "signum")

    def __init__(self, eng, idx, fn, deps, dma_key):
        self.eng = eng
        self.idx = idx
        self.fn = fn
        self.deps = deps
        self.dma_key = dma_key
        self.signals = False
        self.signum = None


class Prog:
    def __init__(self, group_keys=()):
        self.streams = {e: [] for e in ENGS}
        self.last_writer = {}
        self.readers = {}
        self.dma_keys = []
        self.group_keys = set(group_keys)

    def op(self, eng, fn, reads=(), writes=(), dma_key=None):
        deps = set()
        for r in reads:
            lw = self.last_writer.get(r)
            if lw is not None:
                deps.add(lw)
        for w in writes:
            lw = self.last_writer.get(w)
            if lw is not None:
                deps.add(lw)
            for rd in self.readers.get(w, {}).values():
                deps.add(rd)
        idx = len(self.streams[eng])
        me = (eng, idx)
        deps.discard(me)
        if eng == "pe":
            deps = {d for d in deps if d[0] != "pe"}
        ins = _Ins(eng, idx, fn, deps, dma_key)
        self.streams[eng].append(ins)
        if dma_key is not None and dma_key not in self.dma_keys:
            self.dma_keys.append(dma_key)
        for r in reads:
            rk = eng if eng in ("pe", "act", "dve") else me
            self.readers.setdefault(r, {})[rk] = me
        for w in writes:
            self.last_writer[w] = me
            self.readers[w] = {}
        return ins

    def emit(self, nc, final_wait_eng="sp"):
        streams = self.streams
        fin_deps = set()
        for e in ENGS:
            for ins in streams[e]:
                if ins.dma_key is not None:
                    fin_deps.add((e, ins.idx))
            if streams[e] and e != final_wait_eng:
                fin_deps.add((e, len(streams[e]) - 1))
        for e in ENGS:
            for ins in streams[e]:
                for (de, di) in ins.deps:
                    streams[de][di].signals = True
        for (de, di) in fin_deps:
            streams[de][di].signals = True
        dma_count = {}
        for e in ENGS:
            n = 0
            for ins in streams[e]:
                if ins.dma_key is not None:
                    c = dma_count.get(ins.dma_key, 0) + 1
                    dma_count[ins.dma_key] = c
                    ins.signum = 16 * c
                elif ins.signals:
                    n += 1
                    ins.signum = n
        self.sig_counts = {e: max([i.signum or 0 for i in streams[e] if i.dma_key is None] + [0]) for e in ENGS}
        self.sig_counts.update({str(k): 16 * v for k, v in dma_count.items()})
        for e in ENGS:
            for ins in streams[e]:
                if ins.dma_key in self.group_keys:
                    ins.signum = 16 * dma_count[ins.dma_key]
        with contextlib.ExitStack() as st:
            sem_eng = {e: st.enter_context(nc.semaphore("s_" + e)) for e in ENGS}
            sem_dma = {k: st.enter_context(nc.semaphore("d_%d" % i)) for i, k in enumerate(self.dma_keys)}
            block = st.enter_context(nc.Block())

            def sem_of(ins):
                return sem_dma[ins.dma_key] if ins.dma_key is not None else sem_eng[ins.eng]

            def waits_for(deps, known, engobj):
                need = {}
                for (de, di) in deps:
                    d = streams[de][di]
                    s = sem_of(d)
                    key = id(s)
                    if known.get(key, 0) >= d.signum:
                        continue
                    if key not in need or need[key][1] < d.signum:
                        need[key] = (s, d.signum)
                for key, (s, v) in need.items():
                    engobj.wait_ge(s, v)
                    known[key] = v

            def run_stream(e, engobj):
                known = {}
                for ins in streams[e]:
                    waits_for(ins.deps, known, engobj)
                    bi = ins.fn(engobj)
                    if ins.dma_key is not None:
                        bi.then_inc(sem_dma[ins.dma_key], 16)
                    elif ins.signals:
                        bi.then_inc(sem_eng[e], 1)
                if e == final_wait_eng:
                    waits_for(fin_deps, known, engobj)

            @block.tensor
            def _(eng):
                run_stream("pe", eng)

            @block.scalar
            def _(eng):
                run_stream("act", eng)

            @block.vector
            def _(eng):
                run_stream("dve", eng)

            @block.gpsimd
            def _(eng):
                run_stream("pool", eng)

            @block.sync
            def _(eng):
                run_stream("sp", eng)


def build_program(n_blocks=len(BLOCKS)):
    nc = bass.Bass("TRN2", target_bir_lowering=False)

    def din(name, shape, dt=F32):
        return nc.dram_tensor(name, list(shape), dt, kind="ExternalInput").ap()

    def dout(name, shape, dt=F32):
        return nc.dram_tensor(name, list(shape), dt, kind="ExternalOutput").ap()

    xT = din("xT", [D, NTOK])
    wst = din("wst", [NT, 128, 16, 128])
    cst = din("cst", [128, 112])
    tab = din("tab", [NBUCK, NH])
    ohd = din("oh", [128, 256])
    ckT = din("ckT", [128, 2, 2, 128])
    ckn = din("ckn", [2, 128, 256])
    cvn = din("cvn", [2, 128, 256])
    scv = din("scv", [128, 2, 8, 2])
    wbf = nc.dram_tensor("wbf", [NT, 128, 16, 128], BF16, kind="Internal").ap()

    yT = dout("yT", [D, NOUT])
    g_out = dout("g_out", [128, 8, 3, 2])
    ko_p = dout("ko_p", [128, 2, 128])
    vo_p = dout("vo_p", [64, 2, 256])
    ko_s = dout("ko_s", [128, 2, 2, 64])
    kc_copy = dout("kc_copy", [2, 64, 256])
    vs_out = dout("vs_out", [2, 128, 256])

    xT_v = xT.rearrange("(kc p) t -> p kc t", p=128)
    yT_v = yT.rearrange("(kc p) t -> p kc t", p=128)

    P = Prog(group_keys=("consts", "consts_sw", "outs"))

    with contextlib.ExitStack() as st:
        def sb(name, shape, dt):
            return st.enter_context(nc.sbuf_tensor(name, list(shape), dt))

        PS = st.enter_context(nc.psum_tensor("PS", [128, 4096], F32))
        xs = sb("xs", [128, KC, TMAX], F32)
        hT = sb("hT", [128, KC, 512], BF16)
        u = sb("u", [128, KC, TMAX], F32)
        act = sb("act", [128, FC, TMAX], BF16)
        kT = sb("kT", [128, 2, 16 * CH], BF16)
        Vr = sb("Vr", [64, 16, 256], BF16)
        kcb = sb("kcb", [128, 2, 2, 128], BF16)
        vcb = sb("vcb", [64, 2, 2, 256], BF16)
        wsl = [sb("w%d" % i, [128, 16, 128], BF16) for i in range(NS)]
        EB = sb("EB", [64, NKV, 3, 256], F32)
        cs = sb("cs", [128, 112], F32)
        sinkx = sb("sinkx", [64, NH], F32)
        sinkb = sb("sinkb", [1, NH], BF16)
        bd = sb("bd", [128, 64], F32)
        ohs = sb("ohs", [128, 256], F32)
        ones = sb("ones", [128, 128], BF16)
        gst = sb("gst", [128, 8, 2], F32)
        scs = sb("scs", [128, 2, 8, 2], F32)
        NTMP = 6
        tmp = [sb("tmp%d" % i, [128, 520], F32) for i in range(NTMP)]
        NSQ = 3
        sq = [sb("sq%d" % i, [128, 512], BF16) for i in range(NSQ)]
        Ebuf = [sb("E%d" % i, [64, 768], F32) for i in range(2)]
        PT = [sb("PT%d" % i, [64, 768], BF16) for i in range(2)]
        rec = [sb("rec%d" % i, [64, 256], F32) for i in range(2)]
        rstd = [sb("rstd%d" % i, [128, 512], F32) for i in range(2)]
        stg = [sb("stg%d" % i, [128, TMAX], F32) for i in range(4)]
        gout_s = sb("gout_s", [128, 8, 3, 2], F32)
        kop_s = sb("kop_s", [128, 2, 128], F32)
        kos_s = sb("kos_s", [128, 2, 2, 64], F32)
        vop_s = sb("vop_s", [64, 2, 256], F32)
        vos_s = sb("vos_s", [64, 2, 256], F32)

        cnt = {"tmp": 0, "sq": 0, "rstd": 0, "bank": 0}

        def next_tmp():
            i = cnt["tmp"] % NTMP
            cnt["tmp"] += 1
            return tmp[i], ("tmp", i)

        def next_sq():
            i = cnt["sq"] % NSQ
            cnt["sq"] += 1
            return sq[i], ("sq", i)

        def next_rstd():
            i = cnt["rstd"] % 2
            cnt["rstd"] += 1
            return rstd[i], ("rstd", i)

        def psk(c0, c1):
            return [("ps", h) for h in range(c0 // 512, (c1 + 511) // 512)]

        BANK_SSUM = 6
        BANK_V = 7

        def next_bank():
            b = cnt["bank"] % 6
            cnt["bank"] += 1
            return b

        ws = {"issued": 0, "req": 0}
        total_tiles = NT * n_blocks

        def tile_kcn(t):
            if t >= T_DN and (t - T_DN) % 3 == 2:
                return 12
            return 16

        def ws_ensure(upto):
            while ws["issued"] < min(upto, total_tiles):
                g = ws["issued"]
                pas, t = divmod(g, NT)
                slot = g % NS
                kcn = tile_kcn(t)
                dst = wsl[slot][:, 0:kcn, :]
                if pas == 0 or not SCRATCH:
                    first_reads = ([("xs", kc) for kc in range(KC)] + [("u", kc) for kc in range(KC)]) if g < NS else []
                    P.op("pool", lambda e, dst=dst, src=wst[t, :, 0:kcn, :]: e.dma_start(out=dst, in_=src),
                         reads=first_reads, writes=[("w", slot)], dma_key=("wl_sw", slot))
                    if n_blocks > 1 and SCRATCH:
                        P.op("sp", lambda e, dst=wbf[t, :, 0:kcn, :], src=dst: e.dma_start(out=dst, in_=src),
                             reads=[("w", slot)], writes=[("wbf", t)], dma_key=("wo", slot))
                else:
                    P.op("sp", lambda e, dst=dst, src=wbf[t, :, 0:kcn, :]: e.dma_start(out=dst, in_=src),
                         reads=[("wbf", t)], writes=[("w", slot)], dma_key=("wl", slot))
                ws["issued"] += 1

        def ws_next(expect_t):
            g = ws["req"]
            ws["req"] += 1
            assert g % NT == expect_t, (g % NT, expect_t)
            ws_ensure(g + NS)
            return g % NS

        P.op("sp", lambda e: e.dma_start(out=cs[:], in_=cst), writes=["cs"], dma_key="consts")
        P.op("dve", lambda e: e.memset(bd[:], 0.0), writes=[("bd", i) for i in range(4)])
        for ql in range(4):
            P.op("sp", lambda e, ql=ql: e.dma_start(out=bd[ql * 32:(ql + 1) * 32, ql * 16:(ql + 1) * 16], in_=tab),
                 writes=[("bd", ql)], dma_key="consts")
        P.op("sp", lambda e: e.dma_start(out=ohs[:], in_=ohd), writes=["ohs"], dma_key="consts")
        P.op("sp", lambda e: e.dma_start(out=scs[:], in_=scv), writes=["scs"], dma_key="consts")
        P.op("pool", lambda e: e.dma_start(out=kcb[:], in_=ckT), writes=["kcb"], dma_key="consts_sw")
        P.op("pool", lambda e: e.dma_start(out=vcb[:], in_=cvn.rearrange("s (j p) f -> p s j f", p=64)),
             writes=["vcb"], dma_key="consts_sw")
        P.op(IOQ, lambda e: e.dma_start(out=kc_copy, in_=ckn[:, 64:128, :]), dma_key="outs")
        P.op(IOQ, lambda e: e.dma_start(out=vs_out[:, 0:64, :], in_=cvn[:, 64:128, :]), dma_key="outs")
        P.op("dve", lambda e: e.memset(ones[:], 1.0), writes=["ones"])
        for i in range(NTMP):
            P.op("dve", lambda e, i=i: e.memset(tmp[i][:], 0.0), writes=[("tmp", i)])
        P.op("dve", lambda e: e.memset(gst[:], 0.0), writes=["gst"])
        P.op("act", lambda e: e.activation(out=sinkx[:], in_=cs[0:64, 89:105], func=AF.Exp),
             reads=["cs"], writes=["sinkx"])
        if SINK_MM:
            P.op("act", lambda e: e.activation(out=sinkb[0:1, :], in_=sinkx[0:1, :], func=AF.Copy),
                 reads=["sinkx"], writes=["sinkb"])
        for j in range(3):
            for qh in range(2):
                bank = next_bank()
                c0 = bank * 512
                for c2 in range(8):
                    X = j * 64 + 60 - 4 * (qh * 8 + c2)
                    P.op("pe", lambda e, c=c0 + c2 * 64, X=X: e.matmul(
                        PS[0:64, c:c + 64], ohs[:, X:X + 64], bd[:, :], start=True, stop=True),
                        reads=["ohs"] + [("bd", i) for i in range(4)], writes=psk(c0, c0 + 512))
                out_ap = EB[:, :, j, :].rearrange("p kh (g q) -> p kh g q", g=4)[:, :, :, qh * 32:(qh + 1) * 32] \
                    .rearrange("p kh g q -> p q kh g")
                in_ap = PS[0:64, c0:c0 + 512].rearrange("p (q kh g) -> p q kh g", q=32, kh=4)
                P.op("act", lambda e, o=out_ap, i=in_ap: e.activation(out=o, in_=i, func=AF.Exp),
                     reads=psk(c0, c0 + 512), writes=["EB"])

        def gain_ap(gi, kc):
            return cs[:, gi * 16 + kc: gi * 16 + kc + 1]

        def convw_ap(tap, j):
            return cs[:, 64 + tap * 8 + j: 64 + tap * 8 + j + 1]

        def ssum_mm(sq_t, sq_key, n, first, last, bank=BANK_SSUM):
            P.op("pe", lambda e, r=sq_t[:, 0:n]: e.matmul(PS[:, bank * 512: bank * 512 + n], ones[:, :], r,
                                                           start=first, stop=last),
                 reads=["ones", sq_key], writes=psk(bank * 512, bank * 512 + 512))

        def finish_rstd(n, bank=BANK_SSUM):
            r_t, r_key = next_rstd()
            if ACT_RECIP:
                P.op("act", lambda e: e.activation(out=r_t[:, 0:n], in_=PS[:, bank * 512: bank * 512 + n],
                                                   func=AF.Ln, scale=1.0 / D, bias=EPS),
                     reads=psk(bank * 512, bank * 512 + 512), writes=[r_key])
                P.op("act", lambda e: e.activation(out=r_t[:, 0:n], in_=r_t[:, 0:n], func=AF.Exp, scale=-0.5),
                     reads=[r_key], writes=[r_key])
            else:
                P.op("act", lambda e: e.activation(out=r_t[:, 0:n], in_=PS[:, bank * 512: bank * 512 + n],
                                                   func=AF.Sqrt, scale=1.0 / D, bias=EPS),
                     reads=psk(bank * 512, bank * 512 + 512), writes=[r_key])
                P.op("dve", lambda e: e.reciprocal(out=r_t[:, 0:n], in_=r_t[:, 0:n]), reads=[r_key], writes=[r_key])
            return r_t, r_key

        def apply_h(out_ap, out_key, x_ap, x_key, g_ap, r_ap, r_key, n, use_pool):
            if not use_pool:
                P.op("dve", lambda e: e.scalar_tensor_tensor(out=out_ap, in0=x_ap, scalar=g_ap, in1=r_ap,
                                                             op0=ALU.mult, op1=ALU.mult),
                     reads=[x_key, r_key, "cs"], writes=[out_key])
            else:
                t_t, t_key = next_tmp()
                P.op("act", lambda e: e.mul(out=t_t[:, 0:n], in_=x_ap, mul=g_ap), reads=[x_key, "cs"], writes=[t_key])
                P.op(VEC2, lambda e: e.tensor_tensor(out=out_ap, in0=t_t[:, 0:n], in1=r_ap, op=ALU.mult),
                     reads=[t_key, r_key], writes=[out_key])

        def norm_to_h(src, src_key_fn, n, gi, h_off):
            for kc in range(KC):
                s_t, s_key = next_sq()
                P.op("act", lambda e, kc=kc, s_t=s_t: e.activation(out=s_t[:, 0:n], in_=src[:, kc, 0:n], func=AF.Square),
                     reads=[src_key_fn(kc)], writes=[s_key])
                ssum_mm(s_t, s_key, n, kc == 0, kc == KC - 1)
            r_t, r_key = finish_rstd(n)
            for kc in range(KC):
                apply_h(hT[:, kc, h_off:h_off + n], ("hT", kc), src[:, kc, 0:n], src_key_fn(kc), gain_ap(gi, kc),
                        r_t[:, 0:n], r_key, n, POOL_H)

        def mm_group(out_ap, out_keys, slot, rhs_fn, rhs_keys_fn, nk, first=True, last=True, k_off=0):
            for kk in range(nk):
                rhs = rhs_fn(k_off + kk)
                P.op("pe", lambda e, kk=kk, rhs=rhs: e.matmul(out_ap, wsl[slot][:, kk, :], rhs,
                                                              start=(first and kk == 0), stop=(last and kk == nk - 1)),
                     reads=[("w", slot), rhs_keys_fn(k_off + kk)], writes=out_keys)

        def residual_phase(n, gi, use_pool=False):
            r_t, r_key = finish_rstd(n)
            for oc in range(KC):
                t_t, t_key = next_tmp()
                P.op("pool" if (use_pool and oc % 2 == 1) else "dve", lambda e, oc=oc, t_t=t_t: e.tensor_tensor(
                    out=t_t[:, 0:n], in0=u[:, oc, 0:n], in1=r_t[:, 0:n], op=ALU.mult),
                    reads=[("u", oc), r_key], writes=[t_key])
                P.op("dve", lambda e, oc=oc, t_t=t_t: e.tensor_tensor(
                    out=xs[:, oc, 0:n], in0=xs[:, oc, 0:n], in1=t_t[:, 0:n], op=ALU.add),
                    reads=[("xs", oc), t_key], writes=[("xs", oc)])

        def out_proj_phase(n, tile_base, gi, rhs_fn, rhs_keys_fn, pieces, hook=None):
            pending = None
            for oc in range(KC):
                bank = next_bank()
                c0 = bank * 512
                o_ap = PS[:, c0:c0 + n]
                for pi, (k0, kn) in enumerate(pieces):
                    slot = ws_next(tile_base + oc * len(pieces) + pi)
                    mm_group(o_ap, psk(c0, c0 + 512), slot, rhs_fn, rhs_keys_fn, kn,
                             first=(pi == 0), last=(pi == len(pieces) - 1), k_off=k0)
                if pending is not None:
                    ssum_mm(*pending)
                    pending = None
                if hook is not None:
                    hook(oc)
                P.op("act", lambda e, oc=oc, o_ap=o_ap: e.mul(out=u[:, oc, 0:n], in_=o_ap, mul=gain_ap(gi, oc)),
                     reads=psk(c0, c0 + 512) + ["cs"], writes=[("u", oc)])
                s_t, s_key = next_sq()
                P.op("act", lambda e, s_t=s_t, o_ap=o_ap: e.activation(out=s_t[:, 0:n], in_=o_ap, func=AF.Square),
                     reads=psk(c0, c0 + 512), writes=[s_key])
                pending = (s_t, s_key, n, oc == 0, oc == KC - 1)
            ssum_mm(*pending)

        def do_block(b, chunk_base):
            nch = BLOCKS[b]
            T = nch * CH
            pre = HALO if b == 0 else 0
            tok0 = chunk_base * CH
            last_block = (b == len(BLOCKS) - 1) and "nolast" not in DBG
            if b == 0:
                P.op("sp", lambda e: e.dma_start(out=u[:, :, 0:HALO], in_=xT_v[:, :, 0:HALO]),
                     writes=[("u", kc) for kc in range(KC)], dma_key="halo")
            for qd in range(4):
                P.op(IOQ, lambda e, qd=qd: e.dma_start(out=xs[:, qd * 4:(qd + 1) * 4, 0:T],
                                                        in_=xT_v[:, qd * 4:(qd + 1) * 4, HALO + tok0: HALO + tok0 + T]),
                     writes=[("xs", kc) for kc in range(qd * 4, qd * 4 + 4)], dma_key=("xl", qd))
            if b == 0:
                norm_to_h(u, lambda kc: ("u", kc), HALO, 0, 0)
            if not (PREFETCH and b > 0):
                norm_to_h(xs, lambda kc: ("xs", kc), T, 0, pre)

            have_next = PREFETCH and (b + 1 < n_blocks)
            if have_next:
                Tn = BLOCKS[b + 1] * CH
                xoff_n = HALO + tok0 + T
            pf = {"rstd": None}

            NPF = 2 * KC

            def pf_issue(i):
                kc, si = i % KC, i % 4
                P.op(IOQ, lambda e: e.dma_start(out=stg[si][:, 0:Tn], in_=xT_v[:, kc, xoff_n:xoff_n + Tn]),
                     writes=[("stg", si)], dma_key=("stgl", si))

            def pf_process(i):
                kc, si = i % KC, i % 4
                if i < KC:
                    s_t, s_key = next_sq()
                    P.op("act", lambda e, s_t=s_t: e.activation(out=s_t[:, 0:Tn], in_=stg[si][:, 0:Tn], func=AF.Square),
                         reads=[("stg", si)], writes=[s_key])
                    ssum_mm(s_t, s_key, Tn, kc == 0, kc == KC - 1, bank=7)
                    if kc == KC - 1:
                        pf["rstd"] = finish_rstd(Tn, bank=7)
                else:
                    r_t, r_key = pf["rstd"]
                    apply_h(hT[:, kc, 0:Tn], ("hT", kc), stg[si][:, 0:Tn], ("stg", si), gain_ap(0, kc),
                            r_t[:, 0:Tn], r_key, Tn, POOL_H)

            pf_state = {"done": 0}

            def pf_hook(oc):
                if not have_next:
                    return
                if oc == 0:
                    for i in range(4):
                        pf_issue(i)
                    return
                target = min(NPF, -(-oc * NPF // 12))
                while pf_state["done"] < target:
                    i = pf_state["done"]
                    pf_process(i)
                    if i + 4 < NPF:
                        pf_issue(i + 4)
                    pf_state["done"] += 1

            NW = pre + T

            def h_ctx(kc):
                return hT[:, kc, 0:NW]

            def h_main(kc):
                return hT[:, kc, pre:pre + T]

            def h_key(kc):
                return ("hT", kc)

            ctx_chunks = []
            if b == 0:
                ctx_chunks += [(0, 0), (1, 64)]
            for ci in range(nch):
                ctx_chunks.append(((chunk_base + ci + 2) % 16, pre + ci * CH))

            for kp in range(2):
                slot = ws_next(T_K + kp)
                bank = next_bank()
                c0 = bank * 512
                mm_group(PS[:, c0:c0 + NW], psk(c0, c0 + 512), slot, h_ctx, h_key, KC)
                for (cslot, col) in ctx_chunks:
                    P.op("act", lambda e, kp=kp, cslot=cslot, col=col, c0=c0: e.activation(
                        out=kT[:, kp, cslot * CH:(cslot + 1) * CH], in_=PS[:, c0 + col:c0 + col + CH], func=AF.Copy),
                        reads=psk(c0, c0 + 512), writes=[("kT", kp, cslot)])
                if last_block:
                    pc = (nch - 4) * CH
                    P.op("act", lambda e, kp=kp, c0=c0, pc=pc: e.activation(out=kop_s[:, kp, :], in_=PS[:, c0 + pc:c0 + pc + 128], func=AF.Copy),
                         reads=psk(c0, c0 + 512), writes=["kop_s"])
                    sc = (nch - 2) * CH
                    P.op("act", lambda e, kp=kp, c0=c0, sc=sc: e.activation(
                        out=kos_s[:, kp, :, :], in_=PS[:, c0 + sc:c0 + sc + 128].rearrange("p (s t) -> p s t", s=2), func=AF.Copy),
                        reads=psk(c0, c0 + 512), writes=["kos_s"])
            for vt in range(2):
                slot = ws_next(T_V + vt)
                for k_i, (cslot, col) in enumerate(ctx_chunks):
                    vc0 = (6 + k_i % 2) * 512
                    for kc in range(KC):
                        P.op("pe", lambda e, kc=kc, col=col, vc0=vc0, slot=slot: e.matmul(
                            PS[0:64, vc0:vc0 + 128], hT[:, kc, col:col + CH], wsl[slot][:, kc, :],
                            start=(kc == 0), stop=(kc == KC - 1)),
                            reads=[("w", slot), ("hT", kc)], writes=psk(vc0, vc0 + 128))
                    P.op("act", lambda e, cslot=cslot, vt=vt, vc0=vc0: e.activation(
                        out=Vr[:, cslot, vt * 128:(vt + 1) * 128], in_=PS[0:64, vc0:vc0 + 128], func=AF.Copy),
                        reads=psk(vc0, vc0 + 128), writes=[("Vr", cslot)])
                    if last_block:
                        ci = k_i
                        if nch - 4 <= ci < nch - 2:
                            P.op("act", lambda e, ci=ci, vt=vt, vc0=vc0: e.activation(
                                out=vop_s[:, ci - (nch - 4), vt * 128:(vt + 1) * 128], in_=PS[0:64, vc0:vc0 + 128], func=AF.Copy),
                                reads=psk(vc0, vc0 + 128), writes=["vop_s"])
                        if ci >= nch - 2:
                            P.op("act", lambda e, ci=ci, vt=vt, vc0=vc0: e.activation(
                                out=vos_s[:, ci - (nch - 2), vt * 128:(vt + 1) * 128], in_=PS[0:64, vc0:vc0 + 128], func=AF.Copy),
                                reads=psk(vc0, vc0 + 128), writes=["vos_s"])
            for jq in range(8):
                slot = ws_next(T_Q + jq)
                bank = next_bank()
                c0 = bank * 512
                mm_group(PS[:, c0:c0 + T], psk(c0, c0 + 512), slot, h_main, h_key, KC)
                P.op("act", lambda e, jq=jq, c0=c0: e.activation(out=act[:, 16 + jq, 0:T], in_=PS[:, c0:c0 + T], func=AF.Copy),
                     reads=psk(c0, c0 + 512), writes=[("act", 16 + jq)])

            units = []
            for ci in range(nch):
                for kh in range(NKV):
                    units.append((ci, kh))

            def unit_S(n):
                ci, kh = units[n]
                side, kp = kh % 2, kh // 2
                par = n % 2
                sc0 = par * 768
                gci = chunk_base + ci
                is_sample = last_block and ci >= nch - 2 and "nosample" not in DBG
                p0 = side * 64
                q_ap = act[p0:p0 + 64, 16 + 4 * kp:16 + 4 * kp + 4, ci * CH:(ci + 1) * CH]
                for j in range(3):
                    if is_sample and j < 2:
                        s_i = ci - (nch - 2)
                        k_ap = kcb[p0:p0 + 64, s_i, kp, j * 64:(j + 1) * 64]
                        k_key = "kcb"
                    else:
                        cslot = (gci + j) % 16
                        k_ap = kT[p0:p0 + 64, kp, cslot * CH:(cslot + 1) * CH]
                        k_key = ("kT", kp, cslot)
                    P.op("pe", lambda e, j=j, k_ap=k_ap: e.matmul(PS[0:64, sc0 + j * 256: sc0 + (j + 1) * 256], k_ap, q_ap,
                                                                 start=True, stop=True),
                         reads=[k_key] + [("act", 16 + 4 * kp + g) for g in range(4)],
                         writes=psk(sc0, sc0 + 768))
                P.op("act", lambda e: e.activation(out=Ebuf[par][:, :], in_=PS[0:64, sc0:sc0 + 768], func=AF.Exp, scale=HD ** -0.5),
                     reads=psk(sc0, sc0 + 768), writes=[("E", par)])
                P.op(VEC2 if POOL_EB else "dve", lambda e: e.tensor_tensor(out=PT[par][:, :], in0=Ebuf[par][:, :],
                                                      in1=EB[:, kh, :, :].rearrange("p j c -> p (j c)"), op=ALU.mult),
                     reads=[("E", par), "EB"], writes=[("PT", par)])
                if b == 0 and ci < 2:
                    w_m = 512 if ci == 0 else 256
                    P.op("dve", lambda e: e.tensor_scalar(out=PT[par][:, 0:w_m], in0=PT[par][:, 0:w_m],
                                                          scalar1=cs[0:64, 88:89], scalar2=0.0, op0=ALU.mult, op1=ALU.add),
                         reads=[("PT", par), "cs"], writes=[("PT", par)])

            def unit_PV(n):
                ci, kh = units[n]
                side, kp = kh % 2, kh // 2
                par = n % 2
                oc0 = (3 + par) * 512
                gci = chunk_base + ci
                is_sample = last_block and ci >= nch - 2 and "nosample" not in DBG
                p0 = side * 64
                for j in range(3):
                    if is_sample and j < 2:
                        s_i = ci - (nch - 2)
                        v_ap = vcb[:, s_i, j, kh * 64:(kh + 1) * 64]
                        v_key = "vcb"
                    else:
                        cslot = (gci + j) % 16
                        v_ap = Vr[:, cslot, kh * 64:(kh + 1) * 64]
                        v_key = ("Vr", cslot)
                    P.op("pe", lambda e, j=j, v_ap=v_ap: e.matmul(PS[0:64, oc0:oc0 + 256], v_ap, PT[par][:, j * 256:(j + 1) * 256],
                                                                 start=(j == 0), stop=(j == 2)),
                         reads=[v_key, ("PT", par)], writes=psk(oc0, oc0 + 256))
                for j in range(3):
                    P.op("pe", lambda e, j=j: e.matmul(PS[0:64, oc0 + 256:oc0 + 512], ones[0:64, 0:64], PT[par][:, j * 256:(j + 1) * 256],
                                                       start=(j == 0), stop=(j == 2 and not SINK_MM)),
                         reads=["ones", ("PT", par)], writes=psk(oc0 + 256, oc0 + 512))
                if SINK_MM:
                    P.op("pe", lambda e: e.matmul(PS[0:64, oc0 + 256:oc0 + 512], ones[0:1, 0:64], bass.AP(sinkb, kh * 4, [[NH, 1], [1, 4], [0, 64]]),
                                                  start=False, stop=True),
                         reads=["ones", "sinkb"], writes=psk(oc0 + 256, oc0 + 512))
                    P.op("act", lambda e: e.activation(out=rec[par][:, :], in_=PS[0:64, oc0 + 256:oc0 + 512], func=AF.Ln),
                         reads=psk(oc0 + 256, oc0 + 512), writes=[("rec", par)])
                else:
                    sk = bass.AP(sinkx, kh * 4, [[NH, 64], [1, 4], [0, 64]])
                    P.op("dve", lambda e: e.tensor_tensor(out=rec[par][:, :].rearrange("p (g q) -> p g q", g=4),
                                                          in0=PS[0:64, oc0 + 256:oc0 + 512].rearrange("p (g q) -> p g q", g=4),
                                                          in1=sk, op=ALU.add),
                         reads=psk(oc0 + 256, oc0 + 512) + ["sinkx"], writes=[("rec", par)])
                if ACT_RECIP and not SINK_MM:
                    P.op("act", lambda e: e.activation(out=rec[par][:, :], in_=rec[par][:, :], func=AF.Ln),
                         reads=[("rec", par)], writes=[("rec", par)])
                if ACT_RECIP:
                    P.op("act", lambda e: e.activation(out=rec[par][:, :], in_=rec[par][:, :], func=AF.Exp, scale=-1.0),
                         reads=[("rec", par)], writes=[("rec", par)])
                if not ACT_RECIP:
                    P.op("dve", lambda e: e.reciprocal(out=rec[par][:, :], in_=rec[par][:, :]),
                         reads=[("rec", par)], writes=[("rec", par)])
                mc0 = 8 + 4 * kp
                P.op("dve", lambda e: e.tensor_tensor(
                    out=act[p0:p0 + 64, mc0:mc0 + 4, ci * CH:(ci + 1) * CH],
                    in0=rec[par][:, :].rearrange("p (g q) -> p g q", g=4),
                    in1=PS[0:64, oc0:oc0 + 256].rearrange("p (g q) -> p g q", g=4), op=ALU.mult),
                    reads=[("rec", par)] + psk(oc0, oc0 + 256), writes=[("act", mc0 + g) for g in range(4)])

            n_units = len(units)
            n_slots = 24
            s_done = 0
            pv_done = 0

            def attn_step(target_s):
                nonlocal s_done, pv_done
                while pv_done < s_done - 1:
                    unit_PV(pv_done)
                    pv_done += 1
                while s_done < target_s:
                    while pv_done < s_done - 1:
                        unit_PV(pv_done)
                        pv_done += 1
                    unit_S(s_done)
                    s_done += 1

            if last_block and "nosegs" not in DBG:
                segs = [(0, (nch - 2) * CH, "prev"), ((nch - 2) * CH, CH, ("s", 0)), ((nch - 1) * CH, CH, ("s", 1))]
            else:
                segs = [(0, T, "prev")]
            slot_i = 0
            for j in range(8):
                slot = ws_next(T_CONV + 3 * j)
                c_cc = 5 * 512
                mm_group(PS[:, c_cc:c_cc + NW], psk(c_cc, c_cc + 512), slot, h_ctx, h_key, KC)
                attn_step(math.ceil((slot_i + 1) * n_units / n_slots)); slot_i += 1
                ccs_t, ccs_key = next_tmp()
                P.op("act", lambda e, ccs_t=ccs_t: e.activation(out=ccs_t[:, 0:NW], in_=PS[:, c_cc:c_cc + NW], func=AF.Copy),
                     reads=psk(c_cc, c_cc + 512), writes=[ccs_key])
                slot = ws_next(T_CONV + 3 * j + 1)
                c_cu = 6 * 512
                mm_group(PS[:, c_cu:c_cu + NW], psk(c_cu, c_cu + 512), slot, h_ctx, h_key, KC)
                attn_step(math.ceil((slot_i + 1) * n_units / n_slots)); slot_i += 1
                G_t, G_key = next_tmp()
                goff = 0
                seg_info = []
                for (toff, L, src) in segs:
                    Lc = L + (pre if toff == 0 else 0)
                    coff = toff + (pre if toff != 0 else 0)
                    seg_info.append((goff, coff, Lc, toff, L, src))
                    goff += 2 + Lc
                GW = goff
                for (go, coff, Lc, toff, L, src) in seg_info:
                    if src == "prev":
                        if b > 0:
                            P.op("dve", lambda e, G_t=G_t, go=go, j=j: e.tensor_copy(out=G_t[:, go:go + 2], in_=gst[:, j, :]),
                                 reads=["gst"], writes=[G_key])
                    else:
                        P.op("dve", lambda e, G_t=G_t, go=go, j=j, si=src[1]: e.tensor_copy(out=G_t[:, go:go + 2], in_=scs[:, si, j, :]),
                             reads=["scs"], writes=[G_key])
                    P.op("dve", lambda e, G_t=G_t, go=go, coff=coff, Lc=Lc, ccs_t=ccs_t: e.tensor_tensor(
                        out=G_t[:, go + 2:go + 2 + Lc], in0=ccs_t[:, coff:coff + Lc], in1=PS[:, c_cu + coff:c_cu + coff + Lc], op=ALU.mult),
                        reads=[ccs_key] + psk(c_cu, c_cu + 512), writes=[G_key])
                go0, _, Lc0, _, _, _ = seg_info[0]
                if not last_block:
                    P.op("dve", lambda e, G_t=G_t, a=go0 + Lc0, j=j: e.tensor_copy(out=gst[:, j, :], in_=G_t[:, a:a + 2]),
                         reads=[G_key], writes=["gst"])
                else:
                    for si_, (go, coff, Lc, toff, L, src) in enumerate(seg_info):
                        P.op("dve", lambda e, G_t=G_t, a=go + Lc, j=j, si_=si_: e.tensor_copy(out=gout_s[:, j, si_, :], in_=G_t[:, a:a + 2]),
                             reads=[G_key], writes=["gout_s"])
                A_t, A_key = next_tmp()
                WA = GW - 2
                P.op("act", lambda e, A_t=A_t, G_t=G_t, j=j: e.mul(out=A_t[:, 0:WA], in_=G_t[:, 0:WA], mul=convw_ap(0, j)),
                     reads=[G_key, "cs"], writes=[A_key])
                for tap in (1, 2):
                    if POOL_TAPS:
                        B_t, B_key = next_tmp()
                        P.op("act", lambda e, B_t=B_t, G_t=G_t, j=j, tap=tap: e.mul(out=B_t[:, 0:WA], in_=G_t[:, tap:tap + WA],
                                                                                    mul=convw_ap(tap, j)),
                             reads=[G_key, "cs"], writes=[B_key])
                        P.op(VEC2, lambda e, A_t=A_t, B_t=B_t: e.tensor_tensor(out=A_t[:, 0:WA], in0=A_t[:, 0:WA],
                                                                                 in1=B_t[:, 0:WA], op=ALU.add),
                             reads=[A_key, B_key], writes=[A_key])
                    else:
                        P.op("dve", lambda e, A_t=A_t, G_t=G_t, j=j, tap=tap: e.scalar_tensor_tensor(
                            out=A_t[:, 0:WA], in0=G_t[:, tap:tap + WA], scalar=convw_ap(tap, j), in1=A_t[:, 0:WA],
                            op0=ALU.mult, op1=ALU.add),
                            reads=[G_key, A_key, "cs"], writes=[A_key])
                slot = ws_next(T_CONV + 3 * j + 2)
                c_cb = 7 * 512
                mm_group(PS[:, c_cb:c_cb + T], psk(c_cb, c_cb + 512), slot, h_main, h_key, KC)
                attn_step(math.ceil((slot_i + 1) * n_units / n_slots)); slot_i += 1
                for (go, coff, Lc, toff, L, src) in seg_info:
                    a0 = go + (Lc - L)
                    P.op("dve", lambda e, A_t=A_t, a0=a0, toff=toff, L=L, j=j: e.tensor_tensor(
                        out=act[:, j, toff:toff + L], in0=A_t[:, a0:a0 + L], in1=PS[:, c_cb + toff:c_cb + toff + L], op=ALU.mult),
                        reads=[A_key] + psk(c_cb, c_cb + 512), writes=[("act", j)])
            attn_step(n_units)
            while pv_done < s_done:
                unit_PV(pv_done)
                pv_done += 1

            out_proj_phase(T, T_OUT, 1, lambda kc: act[:, kc, 0:T], lambda kc: ("act", kc), [(0, 16)])
            residual_phase(T, 1)
            def h2(kc):
                return hT[:, kc, 0:T]

            h2_aps = [hT[:, kc, 0:T] for kc in range(KC)]
            KMAJ = 2 if (LATE_RSTD and NS >= 5) else 0
            pre_banks = {}
            if KMAJ:
                g0 = ws["req"]
                assert g0 % NT == T_GU and ws["issued"] >= g0 + 2 * KMAJ
                kslots = [(g0 + k) % NS for k in range(2 * KMAJ)]
                kcols = [next_bank() * 512 for k in range(2 * KMAJ)]
                ws["req"] = g0 + 2 * KMAJ
                for f in range(KMAJ):
                    pre_banks[f] = (kcols[2 * f], kcols[2 * f + 1])
            if LATE_RSTD:
                for kc in range(KC):
                    P.op("act", lambda e, kc=kc: e.mul(out=hT[:, kc, 0:T], in_=xs[:, kc, 0:T], mul=gain_ap(2, kc)),
                         reads=[("xs", kc), "cs"], writes=[("hT", kc)])
                    s_t, s_key = next_sq()
                    P.op("act", lambda e, kc=kc, s_t=s_t: e.activation(out=s_t[:, 0:T], in_=xs[:, kc, 0:T], func=AF.Square),
                         reads=[("xs", kc)], writes=[s_key])
                    for k in range(2 * KMAJ):
                        P.op("pe", lambda e, kc=kc, k=k: e.matmul(PS[:, kcols[k]:kcols[k] + T], wsl[kslots[k]][:, kc, :], h2_aps[kc],
                                                                   start=(kc == 0), stop=(kc == KC - 1)),
                             reads=[("w", kslots[k]), ("hT", kc)], writes=psk(kcols[k], kcols[k] + 512))
                    ssum_mm(s_t, s_key, T, kc == 0, kc == KC - 1)
                r2_t, r2_key = finish_rstd(T)
            else:
                norm_to_h(xs, lambda kc: ("xs", kc), T, 2, 0)
            for fc in range(FC):
                if fc in pre_banks:
                    cg, cu_ = pre_banks[fc]
                else:
                    slot = ws_next(T_GU + 2 * fc)
                    bg = next_bank()
                    cg = bg * 512
                    mm_group(PS[:, cg:cg + T], psk(cg, cg + 512), slot, h2, h_key, KC)
                    slot = ws_next(T_GU + 2 * fc + 1)
                    bu = next_bank()
                    cu_ = bu * 512
                    mm_group(PS[:, cu_:cu_ + T], psk(cu_, cu_ + 512), slot, h2, h_key, KC)
                s_t, s_key = next_tmp()
                if LATE_RSTD:
                    w_t, w_key = next_tmp()
                    P.op("dve", lambda e, s_t=s_t, cg=cg: e.tensor_tensor(out=s_t[:, 0:T], in0=r2_t[:, 0:T], in1=PS[:, cg:cg + T], op=ALU.mult),
                         reads=[r2_key] + psk(cg, cg + 512), writes=[s_key])
                    P.op("act", lambda e, s_t=s_t: e.activation(out=s_t[:, 0:T], in_=s_t[:, 0:T], func=AF.Silu),
                         reads=[s_key], writes=[s_key])
                    P.op("dve", lambda e, w_t=w_t, cu_=cu_: e.tensor_tensor(out=w_t[:, 0:T], in0=r2_t[:, 0:T], in1=PS[:, cu_:cu_ + T], op=ALU.mult),
                         reads=[r2_key] + psk(cu_, cu_ + 512), writes=[w_key])
                    P.op("dve", lambda e, s_t=s_t, w_t=w_t, fc=fc: e.tensor_tensor(
                        out=act[:, fc, 0:T], in0=s_t[:, 0:T], in1=w_t[:, 0:T], op=ALU.mult),
                        reads=[s_key, w_key], writes=[("act", fc)])
                else:
                    P.op("act", lambda e, s_t=s_t, cg=cg: e.activation(out=s_t[:, 0:T], in_=PS[:, cg:cg + T], func=AF.Silu),
                         reads=psk(cg, cg + 512), writes=[s_key])
                    P.op("dve", lambda e, s_t=s_t, cu_=cu_, fc=fc: e.tensor_tensor(
                        out=act[:, fc, 0:T], in0=s_t[:, 0:T], in1=PS[:, cu_:cu_ + T], op=ALU.mult),
                        reads=[s_key] + psk(cu_, cu_ + 512), writes=[("act", fc)])
            out_proj_phase(T, T_DN, 3, lambda kc: act[:, kc, 0:T], lambda kc: ("act", kc), DN_PIECES, hook=pf_hook)
            residual_phase(T, 3, use_pool=(b == len(BLOCKS) - 1))
            for qd in range(4):
                P.op(IOQ, lambda e, qd=qd, tok0=tok0, T=T: e.dma_start(out=yT_v[:, qd * 4:(qd + 1) * 4, tok0:tok0 + T],
                                                                     in_=xs[:, qd * 4:(qd + 1) * 4, 0:T]),
                     reads=[("xs", kc) for kc in range(qd * 4, qd * 4 + 4)], dma_key=("ys", qd))
            if last_block and "noouts" not in DBG:
                P.op(IOQ, lambda e: e.dma_start(out=g_out, in_=gout_s[:]), reads=["gout_s"], dma_key="outs")
                P.op(IOQ, lambda e: e.dma_start(out=ko_p, in_=kop_s[:]), reads=["kop_s"], dma_key="outs")
                P.op(IOQ, lambda e: e.dma_start(out=ko_s, in_=kos_s[:]), reads=["kos_s"], dma_key="outs")
                P.op(IOQ, lambda e: e.dma_start(out=vo_p, in_=vop_s[:]), reads=["vop_s"], dma_key="outs")
                P.op(IOQ, lambda e: e.dma_start(out=vs_out[:, 64:128, :].rearrange("s p f -> p s f"), in_=vos_s[:]),
                     reads=["vos_s"], dma_key="outs")

        chunk_base = 0
        for b in range(n_blocks):
            do_block(b, chunk_base)
            chunk_base += BLOCKS[b]

        P.emit(nc)
        _NC_CACHE["sig_counts"] = P.sig_counts
    return nc


def _t5_bucket_np(rel):
    half = NBUCK // 2
    max_exact = half // 2
    try:
        import jax
        import jax.numpy as jnp
        with jax.default_device(jax.devices("cpu")[0]):
            r = jnp.asarray(rel, dtype=jnp.int32)
            ret = jnp.where(r > 0, half, 0)
            n = jnp.abs(r)
            nf = jnp.maximum(n, 1).astype(jnp.float32)
            large = max_exact + (jnp.log(nf / max_exact) / math.log(128 / max_exact) * (half - max_exact)).astype(jnp.int32)
            large = jnp.minimum(large, half - 1)
            return np.asarray(ret + jnp.where(n < max_exact, n, large))
    except Exception:
        rel = np.asarray(rel, dtype=np.int32)
        ret = np.where(rel > 0, half, 0)
        n = np.abs(rel)
        nf = np.maximum(n, 1).astype(np.float32)
        large = max_exact + (np.log(nf / np.float32(max_exact)) / np.float32(math.log(128 / max_exact))
                             * np.float32(half - max_exact)).astype(np.int32)
        large = np.minimum(large, half - 1)
        return ret + np.where(n < max_exact, n, large)


def _tiles(W, col_lists=None, row_perm=None):
    K = W.shape[0]
    if row_perm is not None:
        W = W[row_perm]
    out = []
    for cols in col_lists:
        blk = W[:, cols]
        out.append(blk.reshape(K // 128, 128, 128).transpose(1, 0, 2))
    return out


def _q_head_pairs():
    pairs = []
    for kp in range(2):
        for g in range(4):
            pairs.append((8 * kp + g, 8 * kp + 4 + g))
    return pairs


def _pack_weights(w_in, w_out, w_gate, w_up, w_down):
    tiles = [None] * NT
    ar = np.arange
    for j in range(8):
        cols_cb = ar(j * 128, (j + 1) * 128)
        cols_cc = CW + cols_cb
        cols_cu = 2 * CW + cols_cb
        t = _tiles(w_in, [cols_cc, cols_cu, cols_cb])
        tiles[T_CONV + 3 * j: T_CONV + 3 * j + 3] = t
    qb = 3 * CW
    kb = qb + NH * HD
    vb = kb + NKV * HD
    tiles[T_K:T_K + 2] = _tiles(w_in, [ar(kb + kp * 128, kb + (kp + 1) * 128) for kp in range(2)])
    tiles[T_V:T_V + 2] = _tiles(w_in, [ar(vb + vt * 128, vb + (vt + 1) * 128) for vt in range(2)])
    pairs = _q_head_pairs()
    tiles[T_Q:T_Q + 8] = _tiles(w_in, [np.concatenate([ar(qb + a * 64, qb + (a + 1) * 64), ar(qb + bb * 64, qb + (bb + 1) * 64)])
                                       for (a, bb) in pairs])
    rp = [ar(0, CW)]
    for (a, bb) in pairs:
        rp.append(ar(CW + a * 64, CW + (a + 1) * 64))
        rp.append(ar(CW + bb * 64, CW + (bb + 1) * 64))
    rp = np.concatenate(rp)
    tiles[T_OUT:T_OUT + 16] = _tiles(w_out, [ar(oc * 128, (oc + 1) * 128) for oc in range(16)], row_perm=rp)
    tg = _tiles(w_gate, [ar(f * 128, (f + 1) * 128) for f in range(FC)])
    tu = _tiles(w_up, [ar(f * 128, (f + 1) * 128) for f in range(FC)])
    for f in range(FC):
        tiles[T_GU + 2 * f] = tg[f]
        tiles[T_GU + 2 * f + 1] = tu[f]
    td = _tiles(w_down, [ar(oc * 128, (oc + 1) * 128) for oc in range(16)])
    for oc in range(16):
        for pi, (k0, kn) in enumerate(DN_PIECES):
            t = np.zeros((128, 16, 128), np.float32)
            t[:, 0:kn, :] = td[oc][:, k0:k0 + kn, :]
            tiles[T_DN + 3 * oc + pi] = t
    return np.ascontiguousarray(np.stack(tiles, 0).astype(np.float32))


_NC_CACHE = {}


def _prepare(x_prompt, x_sample, state_conv, cache_k, cache_v, rel_table, g_pre_mix, w_in, conv_w, attn_sinks,
             w_out, g_post_mix, g_pre_ffn, w_gate, w_up, w_down, g_post_ffn):
    f = np.float32
    x_prompt = np.asarray(x_prompt, f)
    x_sample = np.asarray(x_sample, f)
    wst = _pack_weights(np.asarray(w_in, f)[0], np.asarray(w_out, f)[0], np.asarray(w_gate, f)[0],
                        np.asarray(w_up, f)[0], np.asarray(w_down, f)[0])
    cst0 = np.zeros((128, 112), f)
    for gi, gv in enumerate((g_pre_mix, g_post_mix, g_pre_ffn, g_post_ffn)):
        cst0[:, gi * 16:(gi + 1) * 16] = np.asarray(gv, f)[0].reshape(16, 128).T
    cw = np.asarray(conv_w, f)[0]
    for tap in range(3):
        cst0[:, 64 + tap * 8: 64 + tap * 8 + 8] = cw[tap].reshape(8, 128).T
    cst0[:, 89:105] = np.asarray(attn_sinks, f)[0][None, :]
    tab = np.ascontiguousarray(np.asarray(rel_table, f))
    rel = np.arange(256) - 191
    bk = _t5_bucket_np(rel)
    oh1 = np.zeros((NBUCK, 260), f)
    oh1[bk[:255], np.arange(255)] = 1.0
    oh = np.zeros((128, 256), f)
    for ql in range(4):
        oh[ql * 32:(ql + 1) * 32, :] = oh1[:, 3 - ql: 3 - ql + 256]
    sc = np.asarray(state_conv, f)[0]
    ck = np.asarray(cache_k, f)[0].reshape(16, 128, 256)
    cv = np.asarray(cache_v, f)[0].reshape(16, 128, 256)

    in_maps = []
    for c in range(NCORES):
        bi, qt = c // 4, c % 4
        xT = np.zeros((D, NTOK), f)
        if qt > 0:
            xT[:, 0:HALO] = x_prompt[bi, qt * PTOK - HALO: qt * PTOK].T
        xT[:, HALO:HALO + PTOK] = x_prompt[bi, qt * PTOK:(qt + 1) * PTOK].T
        xT[:, HALO + PTOK:] = x_sample[2 * c:2 * c + 2].reshape(STOK, D).T
        cst = cst0.copy()
        cst[:, 88] = 1.0 if qt > 0 else 0.0
        ckT = ck[2 * c:2 * c + 2].transpose(2, 0, 1).reshape(2, 128, 2, 128).transpose(1, 2, 0, 3)
        scv = sc[2 * c:2 * c + 2].transpose(2, 0, 1).reshape(8, 128, 2, 2).transpose(1, 2, 0, 3)
        in_maps.append({
            "xT": np.ascontiguousarray(xT), "wst": wst, "cst": cst, "tab": tab, "oh": oh,
            "ckT": np.ascontiguousarray(ckT), "ckn": np.ascontiguousarray(ck[2 * c:2 * c + 2]),
            "cvn": np.ascontiguousarray(cv[2 * c:2 * c + 2]), "scv": np.ascontiguousarray(scv),
        })
    return in_maps


def _assemble(R):
    f = np.float32
    y_prompt = np.zeros((2, 8192, D), f)
    y_sample = np.zeros((16, 64, D), f)
    ncp = np.zeros((1, 2, 2, CW), f)
    nkp = np.zeros((1, 2, 128, NKV, HD), f)
    nvp = np.zeros((1, 2, 128, NKV, HD), f)
    ncs = np.zeros((1, 16, 2, CW), f)
    nks = np.zeros((1, 16, 128, NKV, HD), f)
    nvs = np.zeros((1, 16, 128, NKV, HD), f)
    for c in range(NCORES):
        bi, qt = c // 4, c % 4
        r = R[c]
        yT = np.asarray(r["yT"])
        y_prompt[bi, qt * PTOK:(qt + 1) * PTOK] = yT[:, 0:PTOK].T
        y_sample[2 * c:2 * c + 2] = yT[:, PTOK:].T.reshape(2, 64, D)
        go = np.asarray(r["g_out"])
        for s in range(2):
            ncs[0, 2 * c + s] = go[:, :, 1 + s, :].transpose(2, 1, 0).reshape(2, CW)
            kn = np.asarray(r["ko_s"])[:, :, s, :]
            nks[0, 2 * c + s, 0:64] = np.asarray(r["kc_copy"])[s].reshape(64, NKV, HD)
            nks[0, 2 * c + s, 64:128] = kn.transpose(2, 1, 0).reshape(64, NKV, HD)
            nvs[0, 2 * c + s] = np.asarray(r["vs_out"])[s].reshape(128, NKV, HD)
        if qt == 3:
            ncp[0, bi] = go[:, :, 0, :].transpose(2, 1, 0).reshape(2, CW)
            nkp[0, bi] = np.asarray(r["ko_p"]).transpose(2, 1, 0).reshape(128, NKV, HD)
            nvp[0, bi] = np.asarray(r["vo_p"]).transpose(1, 0, 2).reshape(128, NKV, HD)
    return (y_prompt, y_sample, ncp, nkp, nvp, ncs, nks, nvs)


def kernel(**inputs):
    in_maps = _prepare(**inputs)
    if "nc" not in _NC_CACHE:
        _NC_CACHE["nc"] = build_program()
    nc = _NC_CACHE["nc"]
    res = run_bass_kernel_spmd(nc, in_maps, core_ids=list(range(NCORES)))
    return _assemble(res.results)
```

```python
import math
import contextlib
import numpy as np
import concourse.bass as bass
import concourse.mybir as mybir
from concourse.bass_utils import run_bass_kernel_spmd

F32 = mybir.dt.float32
BF16 = mybir.dt.bfloat16
AF = mybir.ActivationFunctionType
ALU = mybir.AluOpType

D = 2048
KC = 16
CW = 1024
NH = 16
NKV = 4
HD = 64
DFF = 5632
FC = 44
IN_COLS = 4608
NCORES = 8
PTOK = 2048
HALO = 128
STOK = 128
NTOK = HALO + PTOK + STOK
NOUT = PTOK + STOK
CH = 64
BLOCKS = [6, 7, 7, 7, 7]
TMAX = 448
EPS = 1e-6
NS = 5
NBUCK = 32

T_K = 0
T_V = 2
T_Q = 4
T_CONV = 12
T_OUT = 36
T_GU = 52
T_DN = 140
NT = 188
DN_PIECES = [(0, 16), (16, 16), (32, 12)]

ENGS = ("pe", "act", "dve", "pool", "sp")
LATE_RSTD = True
SINK_MM = True
VEC2 = "dve"
POOL_H = True
POOL_RES = True
POOL_EB = True
POOL_TAPS = True
ACT_RECIP = True
SCRATCH = False
IOQ = "pool" if SCRATCH else "sp"
PREFETCH = True
import os
DBG = set(os.environ.get("KDBG", "").split(","))


class _Ins:
    __slots__ = ("eng", "idx", "fn", "deps", "dma_key", "signals", "signum")

    def __init__(self, eng, idx, fn, deps, dma_key):
        self.eng = eng
        self.idx = idx
        self.fn = fn
        self.deps = deps
        self.dma_key = dma_key
        self.signals = False
        self.signum = None


class Prog:
    def __init__(self, group_keys=()):
        self.streams = {e: [] for e in ENGS}
        self.last_writer = {}
        self.readers = {}
        self.dma_keys = []
        self.group_keys = set(group_keys)

    def op(self, eng, fn, reads=(), writes=(), dma_key=None):
        deps = set()
        for r in reads:
            lw = self.last_writer.get(r)
            if lw is not None:
                deps.add(lw)
        for w in writes:
            lw = self.last_writer.get(w)
            if lw is not None:
                deps.add(lw)
            for rd in self.readers.get(w, {}).values():
                deps.add(rd)
        idx = len(self.streams[eng])
        me = (eng, idx)
        deps.discard(me)
        if eng == "pe":
            deps = {d for d in deps if d[0] != "pe"}
        ins = _Ins(eng, idx, fn, deps, dma_key)
        self.streams[eng].append(ins)
        if dma_key is not None and dma_key not in self.dma_keys:
            self.dma_keys.append(dma_key)
        for r in reads:
            rk = eng if eng in ("pe", "act", "dve") else me
            self.readers.setdefault(r, {})[rk] = me
        for w in writes:
            self.last_writer[w] = me
            self.readers[w] = {}
        return ins

    def emit(self, nc, final_wait_eng="sp"):
        streams = self.streams
        fin_deps = set()
        for e in ENGS:
            for ins in streams[e]:
                if ins.dma_key is not None:
                    fin_deps.add((e, ins.idx))
            if streams[e] and e != final_wait_eng:
                fin_deps.add((e, len(streams[e]) - 1))
        for e in ENGS:
            for ins in streams[e]:
                for (de, di) in ins.deps:
                    streams[de][di].signals = True
        for (de, di) in fin_deps:
            streams[de][di].signals = True
        dma_count = {}
        for e in ENGS:
            n = 0
            for ins in streams[e]:
                if ins.dma_key is not None:
                    c = dma_count.get(ins.dma_key, 0) + 1
                    dma_count[ins.dma_key] = c
                    ins.signum = 16 * c
                elif ins.signals:
                    n += 1
                    ins.signum = n
        self.sig_counts = {e: max([i.signum or 0 for i in streams[e] if i.dma_key is None] + [0]) for e in ENGS}
        self.sig_counts.update({str(k): 16 * v for k, v in dma_count.items()})
        for e in ENGS:
            for ins in streams[e]:
                if ins.dma_key in self.group_keys:
                    ins.signum = 16 * dma_count[ins.dma_key]
        with contextlib.ExitStack() as st:
            sem_eng = {e: st.enter_context(nc.semaphore("s_" + e)) for e in ENGS}
            sem_dma = {k: st.enter_context(nc.semaphore("d_%d" % i)) for i, k in enumerate(self.dma_keys)}
            block = st.enter_context(nc.Block())

            def sem_of(ins):
                return sem_dma[ins.dma_key] if ins.dma_key is not None else sem_eng[ins.eng]

            def waits_for(deps, known, engobj):
                need = {}
                for (de, di) in deps:
                    d = streams[de][di]
                    s = sem_of(d)
                    key = id(s)
                    if known.get(key, 0) >= d.signum:
                        continue
                    if key not in need or need[key][1] < d.signum:
                        need[key] = (s, d.signum)
                for key, (s, v) in need.items():
                    engobj.wait_ge(s, v)
                    known[key] = v

            def run_stream(e, engobj):
                known = {}
                for ins in streams[e]:
                    waits_for(ins.deps, known, engobj)
                    bi = ins.fn(engobj)
                    if ins.dma_key is not None:
                        bi.then_inc(sem_dma[ins.dma_key], 16)
                    elif ins.signals:
                        bi.then_inc(sem_eng[e], 1)
                if e == final_wait_eng:
                    waits_for(fin_deps, known, engobj)

            @block.tensor
            def _(eng):
                run_stream("pe", eng)

            @block.scalar
            def _(eng):
                run_stream("act", eng)

            @block.vector
            def _(eng):
                run_stream("dve", eng)

            @block.gpsimd
            def _(eng):
                run_stream("pool", eng)

            @block.sync
            def _(eng):
                run_stream("sp", eng)


def build_program(n_blocks=len(BLOCKS)):
    nc = bass.Bass("TRN2", target_bir_lowering=False)

    def din(name, shape, dt=F32):
        return nc.dram_tensor(name, list(shape), dt, kind="ExternalInput").ap()

    def dout(name, shape, dt=F32):
        return nc.dram_tensor(name, list(shape), dt, kind="ExternalOutput").ap()

    xT = din("xT", [D, NTOK])
    wst = din("wst", [NT, 128, 16, 128])
    cst = din("cst", [128, 112])
    tab = din("tab", [NBUCK, NH])
    ohd = din("oh", [128, 256])
    ckT = din("ckT", [128, 2, 2, 128])
    ckn = din("ckn", [2, 128, 256])
    cvn = din("cvn", [2, 128, 256])
    scv = din("scv", [128, 2, 8, 2])
    wbf = nc.dram_tensor("wbf", [NT, 128, 16, 128], BF16, kind="Internal").ap()

    yT = dout("yT", [D, NOUT])
    g_out = dout("g_out", [128, 8, 3, 2])
    ko_p = dout("ko_p", [128, 2, 128])
    vo_p = dout("vo_p", [64, 2, 256])
    ko_s = dout("ko_s", [128, 2, 2, 64])
    kc_copy = dout("kc_copy", [2, 64, 256])
    vs_out = dout("vs_out", [2, 128, 256])

    xT_v = xT.rearrange("(kc p) t -> p kc t", p=128)
    yT_v = yT.rearrange("(kc p) t -> p kc t", p=128)

    P = Prog(group_keys=("consts", "consts_sw", "outs"))

    with contextlib.ExitStack() as st:
        def sb(name, shape, dt):
            return st.enter_context(nc.sbuf_tensor(name, list(shape), dt))

        PS = st.enter_context(nc.psum_tensor("PS", [128, 4096], F32))
        xs = sb("xs", [128, KC, TMAX], F32)
        hT = sb("hT", [128, KC, 512], BF16)
        u = sb("u", [128, KC, TMAX], F32)
        act = sb("act", [128, FC, TMAX], BF16)
        kT = sb("kT", [128, 2, 16 * CH], BF16)
        Vr = sb("Vr", [64, 16, 256], BF16)
        kcb = sb("kcb", [128, 2, 2, 128], BF16)
        vcb = sb("vcb", [64, 2, 2, 256], BF16)
        wsl = [sb("w%d" % i, [128, 16, 128], BF16) for i in range(NS)]
        EB = sb("EB", [64, NKV, 3, 256], F32)
        cs = sb("cs", [128, 112], F32)
        sinkx = sb("sinkx", [64, NH], F32)
        sinkb = sb("sinkb", [1, NH], BF16)
        bd = sb("bd", [128, 64], F32)
        ohs = sb("ohs", [128, 256], F32)
        ones = sb("ones", [128, 128], BF16)
        gst = sb("gst", [128, 8, 2], F32)
        scs = sb("scs", [128, 2, 8, 2], F32)
        NTMP = 6
        tmp = [sb("tmp%d" % i, [128, 520], F32) for i in range(NTMP)]
        NSQ = 3
        sq = [sb("sq%d" % i, [128, 512], BF16) for i in range(NSQ)]
        Ebuf = [sb("E%d" % i, [64, 768], F32) for i in range(2)]
        PT = [sb("PT%d" % i, [64, 768], BF16) for i in range(2)]
        rec = [sb("rec%d" % i, [64, 256], F32) for i in range(2)]
        rstd = [sb("rstd%d" % i, [128, 512], F32) for i in range(2)]
        stg = [sb("stg%d" % i, [128, TMAX], F32) for i in range(4)]
        gout_s = sb("gout_s", [128, 8, 3, 2], F32)
        kop_s = sb("kop_s", [128, 2, 128], F32)
        kos_s = sb("kos_s", [128, 2, 2, 64], F32)
        vop_s = sb("vop_s", [64, 2, 256], F32)
        vos_s = sb("vos_s", [64, 2, 256], F32)

        cnt = {"tmp": 0, "sq": 0, "rstd": 0, "bank": 0}

        def next_tmp():
            i = cnt["tmp"] % NTMP
            cnt["tmp"] += 1
            return tmp[i], ("tmp", i)

        def next_sq():
            i = cnt["sq"] % NSQ
            cnt["sq"] += 1
            return sq[i], ("sq", i)

        def next_rstd():
            i = cnt["rstd"] % 2
            cnt["rstd"] += 1
            return rstd[i], ("rstd", i)

        def psk(c0, c1):
            return [("ps", h) for h in range(c0 // 512, (c1 + 511) // 512)]

        BANK_SSUM = 6
        BANK_V = 7

        def next_bank():
            b = cnt["bank"] % 6
            cnt["bank"] += 1
            return b

        ws = {"issued": 0, "req": 0}
        total_tiles = NT * n_blocks

        def tile_kcn(t):
            if t >= T_DN and (t - T_DN) % 3 == 2:
                return 12
            return 16

        def ws_ensure(upto):
            while ws["issued"] < min(upto, total_tiles):
                g = ws["issued"]
                pas, t = divmod(g, NT)
                slot = g % NS
                kcn = tile_kcn(t)
                dst = wsl[slot][:, 0:kcn, :]
                if pas == 0 or not SCRATCH:
                    first_reads = ([("xs", kc) for kc in range(KC)] + [("u", kc) for kc in range(KC)]) if g < NS else []
                    P.op("pool", lambda e, dst=dst, src=wst[t, :, 0:kcn, :]: e.dma_start(out=dst, in_=src),
                         reads=first_reads, writes=[("w", slot)], dma_key=("wl_sw", slot))
                    if n_blocks > 1 and SCRATCH:
                        P.op("sp", lambda e, dst=wbf[t, :, 0:kcn, :], src=dst: e.dma_start(out=dst, in_=src),
                             reads=[("w", slot)], writes=[("wbf", t)], dma_key=("wo", slot))
                else:
                    P.op("sp", lambda e, dst=dst, src=wbf[t, :, 0:kcn, :]: e.dma_start(out=dst, in_=src),
                         reads=[("wbf", t)], writes=[("w", slot)], dma_key=("wl", slot))
                ws["issued"] += 1

        def ws_next(expect_t):
            g = ws["req"]
            ws["req"] += 1
            assert g % NT == expect_t, (g % NT, expect_t)
            ws_ensure(g + NS)
            return g % NS

        P.op("sp", lambda e: e.dma_start(out=cs[:], in_=cst), writes=["cs"], dma_key="consts")
        P.op("dve", lambda e: e.memset(bd[:], 0.0), writes=[("bd", i) for i in range(4)])
        for ql in range(4):
            P.op("sp", lambda e, ql=ql: e.dma_start(out=bd[ql * 32:(ql + 1) * 32, ql * 16:(ql + 1) * 16], in_=tab),
                 writes=[("bd", ql)], dma_key="consts")
        P.op("sp", lambda e: e.dma_start(out=ohs[:], in_=ohd), writes=["ohs"], dma_key="consts")
        P.op("sp", lambda e: e.dma_start(out=scs[:], in_=scv), writes=["scs"], dma_key="consts")
        P.op("pool", lambda e: e.dma_start(out=kcb[:], in_=ckT), writes=["kcb"], dma_key="consts_sw")
        P.op("pool", lambda e: e.dma_start(out=vcb[:], in_=cvn.rearrange("s (j p) f -> p s j f", p=64)),
             writes=["vcb"], dma_key="consts_sw")
        P.op(IOQ, lambda e: e.dma_start(out=kc_copy, in_=ckn[:, 64:128, :]), dma_key="outs")
        P.op(IOQ, lambda e: e.dma_start(out=vs_out[:, 0:64, :], in_=cvn[:, 64:128, :]), dma_key="outs")
        P.op("dve", lambda e: e.memset(ones[:], 1.0), writes=["ones"])
        for i in range(NTMP):
            P.op("dve", lambda e, i=i: e.memset(tmp[i][:], 0.0), writes=[("tmp", i)])
        P.op("dve", lambda e: e.memset(gst[:], 0.0), writes=["gst"])
        P.op("act", lambda e: e.activation(out=sinkx[:], in_=cs[0:64, 89:105], func=AF.Exp),
             reads=["cs"], writes=["sinkx"])
        if SINK_MM:
            P.op("act", lambda e: e.activation(out=sinkb[0:1, :], in_=sinkx[0:1, :], func=AF.Copy),
                 reads=["sinkx"], writes=["sinkb"])
        for j in range(3):
            for qh in range(2):
                bank = next_bank()
                c0 = bank * 512
                for c2 in range(8):
                    X = j * 64 + 60 - 4 * (qh * 8 + c2)
                    P.op("pe", lambda e, c=c0 + c2 * 64, X=X: e.matmul(
                        PS[0:64, c:c + 64], ohs[:, X:X + 64], bd[:, :], start=True, stop=True),
                        reads=["ohs"] + [("bd", i) for i in range(4)], writes=psk(c0, c0 + 512))
                out_ap = EB[:, :, j, :].rearrange("p kh (g q) -> p kh g q", g=4)[:, :, :, qh * 32:(qh + 1) * 32] \
                    .rearrange("p kh g q -> p q kh g")
                in_ap = PS[0:64, c0:c0 + 512].rearrange("p (q kh g) -> p q kh g", q=32, kh=4)
                P.op("act", lambda e, o=out_ap, i=in_ap: e.activation(out=o, in_=i, func=AF.Exp),
                     reads=psk(c0, c0 + 512), writes=["EB"])

        def gain_ap(gi, kc):
            return cs[:, gi * 16 + kc: gi * 16 + kc + 1]

        def convw_ap(tap, j):
            return cs[:, 64 + tap * 8 + j: 64 + tap * 8 + j + 1]

        def ssum_mm(sq_t, sq_key, n, first, last, bank=BANK_SSUM):
            P.op("pe", lambda e, r=sq_t[:, 0:n]: e.matmul(PS[:, bank * 512: bank * 512 + n], ones[:, :], r,
                                                           start=first, stop=last),
                 reads=["ones", sq_key], writes=psk(bank * 512, bank * 512 + 512))

        def finish_rstd(n, bank=BANK_SSUM):
            r_t, r_key = next_rstd()
            if ACT_RECIP:
                P.op("act", lambda e: e.activation(out=r_t[:, 0:n], in_=PS[:, bank * 512: bank * 512 + n],
                                                   func=AF.Ln, scale=1.0 / D, bias=EPS),
                     reads=psk(bank * 512, bank * 512 + 512), writes=[r_key])
                P.op("act", lambda e: e.activation(out=r_t[:, 0:n], in_=r_t[:, 0:n], func=AF.Exp, scale=-0.5),
                     reads=[r_key], writes=[r_key])
            else:
                P.op("act", lambda e: e.activation(out=r_t[:, 0:n], in_=PS[:, bank * 512: bank * 512 + n],
                                                   func=AF.Sqrt, scale=1.0 / D, bias=EPS),
                     reads=psk(bank * 512, bank * 512 + 512), writes=[r_key])
                P.op("dve", lambda e: e.reciprocal(out=r_t[:, 0:n], in_=r_t[:, 0:n]), reads=[r_key], writes=[r_key])
            return r_t, r_key

        def apply_h(out_ap, out_key, x_ap, x_key, g_ap, r_ap, r_key, n, use_pool):
            if not use_pool:
                P.op("dve", lambda e: e.scalar_tensor_tensor(out=out_ap, in0=x_ap, scalar=g_ap, in1=r_ap,
                                                             op0=ALU.mult, op1=ALU.mult),
                     reads=[x_key, r_key, "cs"], writes=[out_key])
            else:
                t_t, t_key = next_tmp()
                P.op("act", lambda e: e.mul(out=t_t[:, 0:n], in_=x_ap, mul=g_ap), reads=[x_key, "cs"], writes=[t_key])
                P.op(VEC2, lambda e: e.tensor_tensor(out=out_ap, in0=t_t[:, 0:n], in1=r_ap, op=ALU.mult),
                     reads=[t_key, r_key], writes=[out_key])

        def norm_to_h(src, src_key_fn, n, gi, h_off):
            for kc in range(KC):
                s_t, s_key = next_sq()
                P.op("act", lambda e, kc=kc, s_t=s_t: e.activation(out=s_t[:, 0:n], in_=src[:, kc, 0:n], func=AF.Square),
                     reads=[src_key_fn(kc)], writes=[s_key])
                ssum_mm(s_t, s_key, n, kc == 0, kc == KC - 1)
            r_t, r_key = finish_rstd(n)
            for kc in range(KC):
                apply_h(hT[:, kc, h_off:h_off + n], ("hT", kc), src[:, kc, 0:n], src_key_fn(kc), gain_ap(gi, kc),
                        r_t[:, 0:n], r_key, n, POOL_H)

        def mm_group(out_ap, out_keys, slot, rhs_fn, rhs_keys_fn, nk, first=True, last=True, k_off=0):
            for kk in range(nk):
                rhs = rhs_fn(k_off + kk)
                P.op("pe", lambda e, kk=kk, rhs=rhs: e.matmul(out_ap, wsl[slot][:, kk, :], rhs,
                                                              start=(first and kk == 0), stop=(last and kk == nk - 1)),
                     reads=[("w", slot), rhs_keys_fn(k_off + kk)], writes=out_keys)

        def residual_phase(n, gi, use_pool=False):
            r_t, r_key = finish_rstd(n)
            for oc in range(KC):
                t_t, t_key = next_tmp()
                P.op("pool" if (use_pool and oc % 2 == 1) else "dve", lambda e, oc=oc, t_t=t_t: e.tensor_tensor(
                    out=t_t[:, 0:n], in0=u[:, oc, 0:n], in1=r_t[:, 0:n], op=ALU.mult),
                    reads=[("u", oc), r_key], writes=[t_key])
                P.op("dve", lambda e, oc=oc, t_t=t_t: e.tensor_tensor(
                    out=xs[:, oc, 0:n], in0=xs[:, oc, 0:n], in1=t_t[:, 0:n], op=ALU.add),
                    reads=[("xs", oc), t_key], writes=[("xs", oc)])

        def out_proj_phase(n, tile_base, gi, rhs_fn, rhs_keys_fn, pieces, hook=None):
            pending = None
            for oc in range(KC):
                bank = next_bank()
                c0 = bank * 512
                o_ap = PS[:, c0:c0 + n]
                for pi, (k0, kn) in enumerate(pieces):
                    slot = ws_next(tile_base + oc * len(pieces) + pi)
                    mm_group(o_ap, psk(c0, c0 + 512), slot, rhs_fn, rhs_keys_fn, kn,
                             first=(pi == 0), last=(pi == len(pieces) - 1), k_off=k0)
                if pending is not None:
                    ssum_mm(*pending)
                    pending = None
                if hook is not None:
                    hook(oc)
                P.op("act", lambda e, oc=oc, o_ap=o_ap: e.mul(out=u[:, oc, 0:n], in_=o_ap, mul=gain_ap(gi, oc)),
                     reads=psk(c0, c0 + 512) + ["cs"], writes=[("u", oc)])
                s_t, s_key = next_sq()
                P.op("act", lambda e, s_t=s_t, o_ap=o_ap: e.activation(out=s_t[:, 0:n], in_=o_ap, func=AF.Square),
                     reads=psk(c0, c0 + 512), writes=[s_key])
                pending = (s_t, s_key, n, oc == 0, oc == KC - 1)
            ssum_mm(*pending)

        def do_block(b, chunk_base):
            nch = BLOCKS[b]
            T = nch * CH
            pre = HALO if b == 0 else 0
            tok0 = chunk_base * CH
            last_block = (b == len(BLOCKS) - 1) and "nolast" not in DBG
            if b == 0:
                P.op("sp", lambda e: e.dma_start(out=u[:, :, 0:HALO], in_=xT_v[:, :, 0:HALO]),
                     writes=[("u", kc) for kc in range(KC)], dma_key="halo")
            for qd in range(4):
                P.op(IOQ, lambda e, qd=qd: e.dma_start(out=xs[:, qd * 4:(qd + 1) * 4, 0:T],
                                                        in_=xT_v[:, qd * 4:(qd + 1) * 4, HALO + tok0: HALO + tok0 + T]),
                     writes=[("xs", kc) for kc in range(qd * 4, qd * 4 + 4)], dma_key=("xl", qd))
            if b == 0:
                norm_to_h(u, lambda kc: ("u", kc), HALO, 0, 0)
            if not (PREFETCH and b > 0):
                norm_to_h(xs, lambda kc: ("xs", kc), T, 0, pre)

            have_next = PREFETCH and (b + 1 < n_blocks)
            if have_next:
                Tn = BLOCKS[b + 1] * CH
                xoff_n = HALO + tok0 + T
            pf = {"rstd": None}

            NPF = 2 * KC

            def pf_issue(i):
                kc, si = i % KC, i % 4
                P.op(IOQ, lambda e: e.dma_start(out=stg[si][:, 0:Tn], in_=xT_v[:, kc, xoff_n:xoff_n + Tn]),
                     writes=[("stg", si)], dma_key=("stgl", si))

            def pf_process(i):
                kc, si = i % KC, i % 4
                if i < KC:
                    s_t, s_key = next_sq()
                    P.op("act", lambda e, s_t=s_t: e.activation(out=s_t[:, 0:Tn], in_=stg[si][:, 0:Tn], func=AF.Square),
                         reads=[("stg", si)], writes=[s_key])
                    ssum_mm(s_t, s_key, Tn, kc == 0, kc == KC - 1, bank=7)
                    if kc == KC - 1:
                        pf["rstd"] = finish_rstd(Tn, bank=7)
                else:
                    r_t, r_key = pf["rstd"]
                    apply_h(hT[:, kc, 0:Tn], ("hT", kc), stg[si][:, 0:Tn], ("stg", si), gain_ap(0, kc),
                            r_t[:, 0:Tn], r_key, Tn, POOL_H)

            pf_state = {"done": 0}

            def pf_hook(oc):
                if not have_next:
                    return
                if oc == 0:
                    for i in range(4):
                        pf_issue(i)
                    return
                target = min(NPF, -(-oc * NPF // 12))
                while pf_state["done"] < target:
                    i = pf_state["done"]
                    pf_process(i)
                    if i + 4 < NPF:
                        pf_issue(i + 4)
                    pf_state["done"] += 1

            NW = pre + T

            def h_ctx(kc):
                return hT[:, kc, 0:NW]

            def h_main(kc):
                return hT[:, kc, pre:pre + T]

            def h_key(kc):
                return ("hT", kc)

            ctx_chunks = []
            if b == 0:
                ctx_chunks += [(0, 0), (1, 64)]
            for ci in range(nch):
                ctx_chunks.append(((chunk_base + ci + 2) % 16, pre + ci * CH))

            for kp in range(2):
                slot = ws_next(T_K + kp)
                bank = next_bank()
                c0 = bank * 512
                mm_group(PS[:, c0:c0 + NW], psk(c0, c0 + 512), slot, h_ctx, h_key, KC)
                for (cslot, col) in ctx_chunks:
                    P.op("act", lambda e, kp=kp, cslot=cslot, col=col, c0=c0: e.activation(
                        out=kT[:, kp, cslot * CH:(cslot + 1) * CH], in_=PS[:, c0 + col:c0 + col + CH], func=AF.Copy),
                        reads=psk(c0, c0 + 512), writes=[("kT", kp, cslot)])
                if last_block:
                    pc = (nch - 4) * CH
                    P.op("act", lambda e, kp=kp, c0=c0, pc=pc: e.activation(out=kop_s[:, kp, :], in_=PS[:, c0 + pc:c0 + pc + 128], func=AF.Copy),
                         reads=psk(c0, c0 + 512), writes=["kop_s"])
                    sc = (nch - 2) * CH
                    P.op("act", lambda e, kp=kp, c0=c0, sc=sc: e.activation(
                        out=kos_s[:, kp, :, :], in_=PS[:, c0 + sc:c0 + sc + 128].rearrange("p (s t) -> p s t", s=2), func=AF.Copy),
                        reads=psk(c0, c0 + 512), writes=["kos_s"])
            for vt in range(2):
                slot = ws_next(T_V + vt)
                for k_i, (cslot, col) in enumerate(ctx_chunks):
                    vc0 = (6 + k_i % 2) * 512
                    for kc in range(KC):
                        P.op("pe", lambda e, kc=kc, col=col, vc0=vc0, slot=slot: e.matmul(
                            PS[0:64, vc0:vc0 + 128], hT[:, kc, col:col + CH], wsl[slot][:, kc, :],
                            start=(kc == 0), stop=(kc == KC - 1)),
                            reads=[("w", slot), ("hT", kc)], writes=psk(vc0, vc0 + 128))
                    P.op("act", lambda e, cslot=cslot, vt=vt, vc0=vc0: e.activation(
                        out=Vr[:, cslot, vt * 128:(vt + 1) * 128], in_=PS[0:64, vc0:vc0 + 128], func=AF.Copy),
                        reads=psk(vc0, vc0 + 128), writes=[("Vr", cslot)])
                    if last_block:
                        ci = k_i
                        if nch - 4 <= ci < nch - 2:
                            P.op("act", lambda e, ci=ci, vt=vt, vc0=vc0: e.activation(
                                out=vop_s[:, ci - (nch - 4), vt * 128:(vt + 1) * 128], in_=PS[0:64, vc0:vc0 + 128], func=AF.Copy),
                                reads=psk(vc0, vc0 + 128), writes=["vop_s"])
                        if ci >= nch - 2:
                            P.op("act", lambda e, ci=ci, vt=vt, vc0=vc0: e.activation(
                                out=vos_s[:, ci - (nch - 2), vt * 128:(vt + 1) * 128], in_=PS[0:64, vc0:vc0 + 128], func=AF.Copy),
                                reads=psk(vc0, vc0 + 128), writes=["vos_s"])
            for jq in range(8):
                slot = ws_next(T_Q + jq)
                bank = next_bank()
                c0 = bank * 512
                mm_group(PS[:, c0:c0 + T], psk(c0, c0 + 512), slot, h_main, h_key, KC)
                P.op("act", lambda e, jq=jq, c0=c0: e.activation(out=act[:, 16 + jq, 0:T], in_=PS[:, c0:c0 + T], func=AF.Copy),
                     reads=psk(c0, c0 + 512), writes=[("act", 16 + jq)])

            units = []
            for ci in range(nch):
                for kh in range(NKV):
                    units.append((ci, kh))

            def unit_S(n):
                ci, kh = units[n]
                side, kp = kh % 2, kh // 2
                par = n % 2
                sc0 = par * 768
                gci = chunk_base + ci
                is_sample = last_block and ci >= nch - 2 and "nosample" not in DBG
                p0 = side * 64
                q_ap = act[p0:p0 + 64, 16 + 4 * kp:16 + 4 * kp + 4, ci * CH:(ci + 1) * CH]
                for j in range(3):
                    if is_sample and j < 2:
                        s_i = ci - (nch - 2)
                        k_ap = kcb[p0:p0 + 64, s_i, kp, j * 64:(j + 1) * 64]
                        k_key = "kcb"
                    else:
                        cslot = (gci + j) % 16
                        k_ap = kT[p0:p0 + 64, kp, cslot * CH:(cslot + 1) * CH]
                        k_key = ("kT", kp, cslot)
                    P.op("pe", lambda e, j=j, k_ap=k_ap: e.matmul(PS[0:64, sc0 + j * 256: sc0 + (j + 1) * 256], k_ap, q_ap,
                                                                 start=True, stop=True),
                         reads=[k_key] + [("act", 16 + 4 * kp + g) for g in range(4)],
                         writes=psk(sc0, sc0 + 768))
                P.op("act", lambda e: e.activation(out=Ebuf[par][:, :], in_=PS[0:64, sc0:sc0 + 768], func=AF.Exp, scale=HD ** -0.5),
                     reads=psk(sc0, sc0 + 768), writes=[("E", par)])
                P.op(VEC2 if POOL_EB else "dve", lambda e: e.tensor_tensor(out=PT[par][:, :], in0=Ebuf[par][:, :],
                                                      in1=EB[:, kh, :, :].rearrange("p j c -> p (j c)"), op=ALU.mult),
                     reads=[("E", par), "EB"], writes=[("PT", par)])
                if b == 0 and ci < 2:
                    w_m = 512 if ci == 0 else 256
                    P.op("dve", lambda e: e.tensor_scalar(out=PT[par][:, 0:w_m], in0=PT[par][:, 0:w_m],
                                                          scalar1=cs[0:64, 88:89], scalar2=0.0, op0=ALU.mult, op1=ALU.add),
                         reads=[("PT", par), "cs"], writes=[("PT", par)])

            def unit_PV(n):
                ci, kh = units[n]
                side, kp = kh % 2, kh // 2
                par = n % 2
                oc0 = (3 + par) * 512
                gci = chunk_base + ci
                is_sample = last_block and ci >= nch - 2 and "nosample" not in DBG
                p0 = side * 64
                for j in range(3):
                    if is_sample and j < 2:
                        s_i = ci - (nch - 2)
                        v_ap = vcb[:, s_i, j, kh * 64:(kh + 1) * 64]
                        v_key = "vcb"
                    else:
                        cslot = (gci + j) % 16
                        v_ap = Vr[:, cslot, kh * 64:(kh + 1) * 64]
                        v_key = ("Vr", cslot)
                    P.op("pe", lambda e, j=j, v_ap=v_ap: e.matmul(PS[0:64, oc0:oc0 + 256], v_ap, PT[par][:, j * 256:(j + 1) * 256],
                                                                 start=(j == 0), stop=(j == 2)),
                         reads=[v_key, ("PT", par)], writes=psk(oc0, oc0 + 256))
                for j in range(3):
                    P.op("pe", lambda e, j=j: e.matmul(PS[0:64, oc0 + 256:oc0 + 512], ones[0:64, 0:64], PT[par][:, j * 256:(j + 1) * 256],
                                                       start=(j == 0), stop=(j == 2 and not SINK_MM)),
                         reads=["ones", ("PT", par)], writes=psk(oc0 + 256, oc0 + 512))
                if SINK_MM:
                    P.op("pe", lambda e: e.matmul(PS[0:64, oc0 + 256:oc0 + 512], ones[0:1, 0:64], bass.AP(sinkb, kh * 4, [[NH, 1], [1, 4], [0, 64]]),
                                                  start=False, stop=True),
                         reads=["ones", "sinkb"], writes=psk(oc0 + 256, oc0 + 512))
                    P.op("act", lambda e: e.activation(out=rec[par][:, :], in_=PS[0:64, oc0 + 256:oc0 + 512], func=AF.Ln),
                         reads=psk(oc0 + 256, oc0 + 512), writes=[("rec", par)])
                else:
                    sk = bass.AP(sinkx, kh * 4, [[NH, 64], [1, 4], [0, 64]])
                    P.op("dve", lambda e: e.tensor_tensor(out=rec[par][:, :].rearrange("p (g q) -> p g q", g=4),
                                                          in0=PS[0:64, oc0 + 256:oc0 + 512].rearrange("p (g q) -> p g q", g=4),
                                                          in1=sk, op=ALU.add),
                         reads=psk(oc0 + 256, oc0 + 512) + ["sinkx"], writes=[("rec", par)])
                if ACT_RECIP and not SINK_MM:
                    P.op("act", lambda e: e.activation(out=rec[par][:, :], in_=rec[par][:, :], func=AF.Ln),
                         reads=[("rec", par)], writes=[("rec", par)])
                if ACT_RECIP:
                    P.op("act", lambda e: e.activation(out=rec[par][:, :], in_=rec[par][:, :], func=AF.Exp, scale=-1.0),
                         reads=[("rec", par)], writes=[("rec", par)])
                if not ACT_RECIP:
                    P.op("dve", lambda e: e.reciprocal(out=rec[par][:, :], in_=rec[par][:, :]),
                         reads=[("rec", par)], writes=[("rec", par)])
                mc0 = 8 + 4 * kp
                P.op("dve", lambda e: e.tensor_tensor(
                    out=act[p0:p0 + 64, mc0:mc0 + 4, ci * CH:(ci + 1) * CH],
                    in0=rec[par][:, :].rearrange("p (g q) -> p g q", g=4),
                    in1=PS[0:64, oc0:oc0 + 256].rearrange("p (g q) -> p g q", g=4), op=ALU.mult),
                    reads=[("rec", par)] + psk(oc0, oc0 + 256), writes=[("act", mc0 + g) for g in range(4)])

            n_units = len(units)
            n_slots = 22
            s_done = 0
            pv_done = 0

            def attn_step(target_s):
                nonlocal s_done, pv_done
                while pv_done < s_done - 1:
                    unit_PV(pv_done)
                    pv_done += 1
                while s_done < target_s:
                    while pv_done < s_done - 1:
                        unit_PV(pv_done)
                        pv_done += 1
                    unit_S(s_done)
                    s_done += 1

            if last_block and "nosegs" not in DBG:
                segs = [(0, (nch - 2) * CH, "prev"), ((nch - 2) * CH, CH, ("s", 0)), ((nch - 1) * CH, CH, ("s", 1))]
            else:
                segs = [(0, T, "prev")]
            slot_i = 0
            for j in range(8):
                slot = ws_next(T_CONV + 3 * j)
                c_cc = 5 * 512
                mm_group(PS[:, c_cc:c_cc + NW], psk(c_cc, c_cc + 512), slot, h_ctx, h_key, KC)
                attn_step(min(n_units, math.ceil((slot_i + 1) * n_units / n_slots))); slot_i += 1
                ccs_t, ccs_key = next_tmp()
                P.op("act", lambda e, ccs_t=ccs_t: e.activation(out=ccs_t[:, 0:NW], in_=PS[:, c_cc:c_cc + NW], func=AF.Copy),
                     reads=psk(c_cc, c_cc + 512), writes=[ccs_key])
                slot = ws_next(T_CONV + 3 * j + 1)
                c_cu = 6 * 512
                mm_group(PS[:, c_cu:c_cu + NW], psk(c_cu, c_cu + 512), slot, h_ctx, h_key, KC)
                attn_step(min(n_units, math.ceil((slot_i + 1) * n_units / n_slots))); slot_i += 1
                G_t, G_key = next_tmp()
                goff = 0
                seg_info = []
                for (toff, L, src) in segs:
                    Lc = L + (pre if toff == 0 else 0)
                    coff = toff + (pre if toff != 0 else 0)
                    seg_info.append((goff, coff, Lc, toff, L, src))
                    goff += 2 + Lc
                GW = goff
                for (go, coff, Lc, toff, L, src) in seg_info:
                    if src == "prev":
                        if b > 0:
                            P.op("dve", lambda e, G_t=G_t, go=go, j=j: e.tensor_copy(out=G_t[:, go:go + 2], in_=gst[:, j, :]),
                                 reads=["gst"], writes=[G_key])
                    else:
                        P.op("dve", lambda e, G_t=G_t, go=go, j=j, si=src[1]: e.tensor_copy(out=G_t[:, go:go + 2], in_=scs[:, si, j, :]),
                             reads=["scs"], writes=[G_key])
                    P.op("dve", lambda e, G_t=G_t, go=go, coff=coff, Lc=Lc, ccs_t=ccs_t: e.tensor_tensor(
                        out=G_t[:, go + 2:go + 2 + Lc], in0=ccs_t[:, coff:coff + Lc], in1=PS[:, c_cu + coff:c_cu + coff + Lc], op=ALU.mult),
                        reads=[ccs_key] + psk(c_cu, c_cu + 512), writes=[G_key])
                go0, _, Lc0, _, _, _ = seg_info[0]
                if not last_block:
                    P.op("dve", lambda e, G_t=G_t, a=go0 + Lc0, j=j: e.tensor_copy(out=gst[:, j, :], in_=G_t[:, a:a + 2]),
                         reads=[G_key], writes=["gst"])
                else:
                    for si_, (go, coff, Lc, toff, L, src) in enumerate(seg_info):
                        P.op("dve", lambda e, G_t=G_t, a=go + Lc, j=j, si_=si_: e.tensor_copy(out=gout_s[:, j, si_, :], in_=G_t[:, a:a + 2]),
                             reads=[G_key], writes=["gout_s"])
                A_t, A_key = next_tmp()
                WA = GW - 2
                P.op("act", lambda e, A_t=A_t, G_t=G_t, j=j: e.mul(out=A_t[:, 0:WA], in_=G_t[:, 0:WA], mul=convw_ap(0, j)),
                     reads=[G_key, "cs"], writes=[A_key])
                for tap in (1, 2):
                    if POOL_TAPS:
                        B_t, B_key = next_tmp()
                        P.op("act", lambda e, B_t=B_t, G_t=G_t, j=j, tap=tap: e.mul(out=B_t[:, 0:WA], in_=G_t[:, tap:tap + WA],
                                                                                    mul=convw_ap(tap, j)),
                             reads=[G_key, "cs"], writes=[B_key])
                        P.op(VEC2, lambda e, A_t=A_t, B_t=B_t: e.tensor_tensor(out=A_t[:, 0:WA], in0=A_t[:, 0:WA],
                                                                                 in1=B_t[:, 0:WA], op=ALU.add),
                             reads=[A_key, B_key], writes=[A_key])
                    else:
                        P.op("dve", lambda e, A_t=A_t, G_t=G_t, j=j, tap=tap: e.scalar_tensor_tensor(
                            out=A_t[:, 0:WA], in0=G_t[:, tap:tap + WA], scalar=convw_ap(tap, j), in1=A_t[:, 0:WA],
                            op0=ALU.mult, op1=ALU.add),
                            reads=[G_key, A_key, "cs"], writes=[A_key])
                slot = ws_next(T_CONV + 3 * j + 2)
                c_cb = 7 * 512
                mm_group(PS[:, c_cb:c_cb + T], psk(c_cb, c_cb + 512), slot, h_main, h_key, KC)
                attn_step(min(n_units, math.ceil((slot_i + 1) * n_units / n_slots))); slot_i += 1
                for (go, coff, Lc, toff, L, src) in seg_info:
                    a0 = go + (Lc - L)
                    P.op("dve", lambda e, A_t=A_t, a0=a0, toff=toff, L=L, j=j: e.tensor_tensor(
                        out=act[:, j, toff:toff + L], in0=A_t[:, a0:a0 + L], in1=PS[:, c_cb + toff:c_cb + toff + L], op=ALU.mult),
                        reads=[A_key] + psk(c_cb, c_cb + 512), writes=[("act", j)])
            attn_step(n_units)
            while pv_done < s_done:
                unit_PV(pv_done)
                pv_done += 1

            out_proj_phase(T, T_OUT, 1, lambda kc: act[:, kc, 0:T], lambda kc: ("act", kc), [(0, 16)])
            residual_phase(T, 1)
            def h2(kc):
                return hT[:, kc, 0:T]

            h2_aps = [hT[:, kc, 0:T] for kc in range(KC)]
            KMAJ = 2 if (LATE_RSTD and NS >= 5) else 0
            pre_banks = {}
            if KMAJ:
                g0 = ws["req"]
                assert g0 % NT == T_GU and ws["issued"] >= g0 + 2 * KMAJ
                kslots = [(g0 + k) % NS for k in range(2 * KMAJ)]
                kcols = [next_bank() * 512 for k in range(2 * KMAJ)]
                ws["req"] = g0 + 2 * KMAJ
                for f in range(KMAJ):
                    pre_banks[f] = (kcols[2 * f], kcols[2 * f + 1])
            if LATE_RSTD:
                for kc in range(KC):
                    P.op("act", lambda e, kc=kc: e.mul(out=hT[:, kc, 0:T], in_=xs[:, kc, 0:T], mul=gain_ap(2, kc)),
                         reads=[("xs", kc), "cs"], writes=[("hT", kc)])
                    s_t, s_key = next_sq()
                    P.op("act", lambda e, kc=kc, s_t=s_t: e.activation(out=s_t[:, 0:T], in_=xs[:, kc, 0:T], func=AF.Square),
                         reads=[("xs", kc)], writes=[s_key])
                    for k in range(2 * KMAJ):
                        P.op("pe", lambda e, kc=kc, k=k: e.matmul(PS[:, kcols[k]:kcols[k] + T], wsl[kslots[k]][:, kc, :], h2_aps[kc],
                                                                   start=(kc == 0), stop=(kc == KC - 1)),
                             reads=[("w", kslots[k]), ("hT", kc)], writes=psk(kcols[k], kcols[k] + 512))
                    ssum_mm(s_t, s_key, T, kc == 0, kc == KC - 1)
                r2_t, r2_key = finish_rstd(T)
            else:
                norm_to_h(xs, lambda kc: ("xs", kc), T, 2, 0)
            for fc in range(FC):
                if fc in pre_banks:
                    cg, cu_ = pre_banks[fc]
                else:
                    slot = ws_next(T_GU + 2 * fc)
                    bg = next_bank()
                    cg = bg * 512
                    mm_group(PS[:, cg:cg + T], psk(cg, cg + 512), slot, h2, h_key, KC)
                    slot = ws_next(T_GU + 2 * fc + 1)
                    bu = next_bank()
                    cu_ = bu * 512
                    mm_group(PS[:, cu_:cu_ + T], psk(cu_, cu_ + 512), slot, h2, h_key, KC)
                s_t, s_key = next_tmp()
                if LATE_RSTD:
                    w_t, w_key = next_tmp()
                    P.op("dve", lambda e, s_t=s_t, cg=cg: e.tensor_tensor(out=s_t[:, 0:T], in0=r2_t[:, 0:T], in1=PS[:, cg:cg + T], op=ALU.mult),
                         reads=[r2_key] + psk(cg, cg + 512), writes=[s_key])
                    P.op("act", lambda e, s_t=s_t: e.activation(out=s_t[:, 0:T], in_=s_t[:, 0:T], func=AF.Silu),
                         reads=[s_key], writes=[s_key])
                    P.op("dve", lambda e, w_t=w_t, cu_=cu_: e.tensor_tensor(out=w_t[:, 0:T], in0=r2_t[:, 0:T], in1=PS[:, cu_:cu_ + T], op=ALU.mult),
                         reads=[r2_key] + psk(cu_, cu_ + 512), writes=[w_key])
                    P.op("dve", lambda e, s_t=s_t, w_t=w_t, fc=fc: e.tensor_tensor(
                        out=act[:, fc, 0:T], in0=s_t[:, 0:T], in1=w_t[:, 0:T], op=ALU.mult),
                        reads=[s_key, w_key], writes=[("act", fc)])
                else:
                    P.op("act", lambda e, s_t=s_t, cg=cg: e.activation(out=s_t[:, 0:T], in_=PS[:, cg:cg + T], func=AF.Silu),
                         reads=psk(cg, cg + 512), writes=[s_key])
                    P.op("dve", lambda e, s_t=s_t, cu_=cu_, fc=fc: e.tensor_tensor(
                        out=act[:, fc, 0:T], in0=s_t[:, 0:T], in1=PS[:, cu_:cu_ + T], op=ALU.mult),
                        reads=[s_key] + psk(cu_, cu_ + 512), writes=[("act", fc)])
            out_proj_phase(T, T_DN, 3, lambda kc: act[:, kc, 0:T], lambda kc: ("act", kc), DN_PIECES, hook=pf_hook)
            residual_phase(T, 3, use_pool=(b == len(BLOCKS) - 1))
            for qd in range(4):
                P.op(IOQ, lambda e, qd=qd, tok0=tok0, T=T: e.dma_start(out=yT_v[:, qd * 4:(qd + 1) * 4, tok0:tok0 + T],
                                                                     in_=xs[:, qd * 4:(qd + 1) * 4, 0:T]),
                     reads=[("xs", kc) for kc in range(qd * 4, qd * 4 + 4)], dma_key=("ys", qd))
            if last_block and "noouts" not in DBG:
                P.op(IOQ, lambda e: e.dma_start(out=g_out, in_=gout_s[:]), reads=["gout_s"], dma_key="outs")
                P.op(IOQ, lambda e: e.dma_start(out=ko_p, in_=kop_s[:]), reads=["kop_s"], dma_key="outs")
                P.op(IOQ, lambda e: e.dma_start(out=ko_s, in_=kos_s[:]), reads=["kos_s"], dma_key="outs")
                P.op(IOQ, lambda e: e.dma_start(out=vo_p, in_=vop_s[:]), reads=["vop_s"], dma_key="outs")
                P.op(IOQ, lambda e: e.dma_start(out=vs_out[:, 64:128, :].rearrange("s p f -> p s f"), in_=vos_s[:]),
                     reads=["vos_s"], dma_key="outs")

        chunk_base = 0
        for b in range(n_blocks):
            do_block(b, chunk_base)
            chunk_base += BLOCKS[b]

        P.emit(nc)
        _NC_CACHE["sig_counts"] = P.sig_counts
    return nc


def _t5_bucket_np(rel):
    half = NBUCK // 2
    max_exact = half // 2
    try:
        import jax
        import jax.numpy as jnp
        with jax.default_device(jax.devices("cpu")[0]):
            r = jnp.asarray(rel, dtype=jnp.int32)
            ret = jnp.where(r > 0, half, 0)
            n = jnp.abs(r)
            nf = jnp.maximum(n, 1).astype(jnp.float32)
            large = max_exact + (jnp.log(nf / max_exact) / math.log(128 / max_exact) * (half - max_exact)).astype(jnp.int32)
            large = jnp.minimum(large, half - 1)
            return np.asarray(ret + jnp.where(n < max_exact, n, large))
    except Exception:
        rel = np.asarray(rel, dtype=np.int32)
        ret = np.where(rel > 0, half, 0)
        n = np.abs(rel)
        nf = np.maximum(n, 1).astype(np.float32)
        large = max_exact + (np.log(nf / np.float32(max_exact)) / np.float32(math.log(128 / max_exact))
                             * np.float32(half - max_exact)).astype(np.int32)
        large = np.minimum(large, half - 1)
        return ret + np.where(n < max_exact, n, large)


def _tiles(W, col_lists=None, row_perm=None):
    K = W.shape[0]
    if row_perm is not None:
        W = W[row_perm]
    out = []
    for cols in col_lists:
        blk = W[:, cols]
        out.append(blk.reshape(K // 128, 128, 128).transpose(1, 0, 2))
    return out


def _q_head_pairs():
    pairs = []
    for kp in range(2):
        for g in range(4):
            pairs.append((8 * kp + g, 8 * kp + 4 + g))
    return pairs


def _pack_weights(w_in, w_out, w_gate, w_up, w_down):
    tiles = [None] * NT
    ar = np.arange
    for j in range(8):
        cols_cb = ar(j * 128, (j + 1) * 128)
        cols_cc = CW + cols_cb
        cols_cu = 2 * CW + cols_cb
        t = _tiles(w_in, [cols_cc, cols_cu, cols_cb])
        tiles[T_CONV + 3 * j: T_CONV + 3 * j + 3] = t
    qb = 3 * CW
    kb = qb + NH * HD
    vb = kb + NKV * HD
    tiles[T_K:T_K + 2] = _tiles(w_in, [ar(kb + kp * 128, kb + (kp + 1) * 128) for kp in range(2)])
    tiles[T_V:T_V + 2] = _tiles(w_in, [ar(vb + vt * 128, vb + (vt + 1) * 128) for vt in range(2)])
    pairs = _q_head_pairs()
    tiles[T_Q:T_Q + 8] = _tiles(w_in, [np.concatenate([ar(qb + a * 64, qb + (a + 1) * 64), ar(qb + bb * 64, qb + (bb + 1) * 64)])
                                       for (a, bb) in pairs])
    rp = [ar(0, CW)]
    for (a, bb) in pairs:
        rp.append(ar(CW + a * 64, CW + (a + 1) * 64))
        rp.append(ar(CW + bb * 64, CW + (bb + 1) * 64))
    rp = np.concatenate(rp)
    tiles[T_OUT:T_OUT + 16] = _tiles(w_out, [ar(oc * 128, (oc + 1) * 128) for oc in range(16)], row_perm=rp)
    tg = _tiles(w_gate, [ar(f * 128, (f + 1) * 128) for f in range(FC)])
    tu = _tiles(w_up, [ar(f * 128, (f + 1) * 128) for f in range(FC)])
    for f in range(FC):
        tiles[T_GU + 2 * f] = tg[f]
        tiles[T_GU + 2 * f + 1] = tu[f]
    td = _tiles(w_down, [ar(oc * 128, (oc + 1) * 128) for oc in range(16)])
    for oc in range(16):
        for pi, (k0, kn) in enumerate(DN_PIECES):
            t = np.zeros((128, 16, 128), np.float32)
            t[:, 0:kn, :] = td[oc][:, k0:k0 + kn, :]
            tiles[T_DN + 3 * oc + pi] = t
    return np.ascontiguousarray(np.stack(tiles, 0).astype(np.float32))


_NC_CACHE = {}


def _prepare(x_prompt, x_sample, state_conv, cache_k, cache_v, rel_table, g_pre_mix, w_in, conv_w, attn_sinks,
             w_out, g_post_mix, g_pre_ffn, w_gate, w_up, w_down, g_post_ffn):
    f = np.float32
    x_prompt = np.asarray(x_prompt, f)
    x_sample = np.asarray(x_sample, f)
    wst = _pack_weights(np.asarray(w_in, f)[0], np.asarray(w_out, f)[0], np.asarray(w_gate, f)[0],
                        np.asarray(w_up, f)[0], np.asarray(w_down, f)[0])
    cst0 = np.zeros((128, 112), f)
    for gi, gv in enumerate((g_pre_mix, g_post_mix, g_pre_ffn, g_post_ffn)):
        cst0[:, gi * 16:(gi + 1) * 16] = np.asarray(gv, f)[0].reshape(16, 128).T
    cw = np.asarray(conv_w, f)[0]
    for tap in range(3):
        cst0[:, 64 + tap * 8: 64 + tap * 8 + 8] = cw[tap].reshape(8, 128).T
    cst0[:, 89:105] = np.asarray(attn_sinks, f)[0][None, :]
    tab = np.ascontiguousarray(np.asarray(rel_table, f))
    rel = np.arange(256) - 191
    bk = _t5_bucket_np(rel)
    oh1 = np.zeros((NBUCK, 260), f)
    oh1[bk[:255], np.arange(255)] = 1.0
    oh = np.zeros((128, 256), f)
    for ql in range(4):
        oh[ql * 32:(ql + 1) * 32, :] = oh1[:, 3 - ql: 3 - ql + 256]
    sc = np.asarray(state_conv, f)[0]
    ck = np.asarray(cache_k, f)[0].reshape(16, 128, 256)
    cv = np.asarray(cache_v, f)[0].reshape(16, 128, 256)

    in_maps = []
    for c in range(NCORES):
        bi, qt = c // 4, c % 4
        xT = np.zeros((D, NTOK), f)
        if qt > 0:
            xT[:, 0:HALO] = x_prompt[bi, qt * PTOK - HALO: qt * PTOK].T
        xT[:, HALO:HALO + PTOK] = x_prompt[bi, qt * PTOK:(qt + 1) * PTOK].T
        xT[:, HALO + PTOK:] = x_sample[2 * c:2 * c + 2].reshape(STOK, D).T
        cst = cst0.copy()
        cst[:, 88] = 1.0 if qt > 0 else 0.0
        ckT = ck[2 * c:2 * c + 2].transpose(2, 0, 1).reshape(2, 128, 2, 128).transpose(1, 2, 0, 3)
        scv = sc[2 * c:2 * c + 2].transpose(2, 0, 1).reshape(8, 128, 2, 2).transpose(1, 2, 0, 3)
        in_maps.append({
            "xT": np.ascontiguousarray(xT), "wst": wst, "cst": cst, "tab": tab, "oh": oh,
            "ckT": np.ascontiguousarray(ckT), "ckn": np.ascontiguousarray(ck[2 * c:2 * c + 2]),
            "cvn": np.ascontiguousarray(cv[2 * c:2 * c + 2]), "scv": np.ascontiguousarray(scv),
        })
    return in_maps


def _assemble(R):
    f = np.float32
    y_prompt = np.zeros((2, 8192, D), f)
    y_sample = np.zeros((16, 64, D), f)
    ncp = np.zeros((1, 2, 2, CW), f)
    nkp = np.zeros((1, 2, 128, NKV, HD), f)
    nvp = np.zeros((1, 2, 128, NKV, HD), f)
    ncs = np.zeros((1, 16, 2, CW), f)
    nks = np.zeros((1, 16, 128, NKV, HD), f)
    nvs = np.zeros((1, 16, 128, NKV, HD), f)
    for c in range(NCORES):
        bi, qt = c // 4, c % 4
        r = R[c]
        yT = np.asarray(r["yT"])
        y_prompt[bi, qt * PTOK:(qt + 1) * PTOK] = yT[:, 0:PTOK].T
        y_sample[2 * c:2 * c + 2] = yT[:, PTOK:].T.reshape(2, 64, D)
        go = np.asarray(r["g_out"])
        for s in range(2):
            ncs[0, 2 * c + s] = go[:, :, 1 + s, :].transpose(2, 1, 0).reshape(2, CW)
            kn = np.asarray(r["ko_s"])[:, :, s, :]
            nks[0, 2 * c + s, 0:64] = np.asarray(r["kc_copy"])[s].reshape(64, NKV, HD)
            nks[0, 2 * c + s, 64:128] = kn.transpose(2, 1, 0).reshape(64, NKV, HD)
            nvs[0, 2 * c + s] = np.asarray(r["vs_out"])[s].reshape(128, NKV, HD)
        if qt == 3:
            ncp[0, bi] = go[:, :, 0, :].transpose(2, 1, 0).reshape(2, CW)
            nkp[0, bi] = np.asarray(r["ko_p"]).transpose(2, 1, 0).reshape(128, NKV, HD)
            nvp[0, bi] = np.asarray(r["vo_p"]).transpose(1, 0, 2).reshape(128, NKV, HD)
    return (y_prompt, y_sample, ncp, nkp, nvp, ncs, nks, nvs)


def kernel(**inputs):
    in_maps = _prepare(**inputs)
    if "nc" not in _NC_CACHE:
        _NC_CACHE["nc"] = build_program()
    nc = _NC_CACHE["nc"]
    res = run_bass_kernel_spmd(nc, in_maps, core_ids=list(range(NCORES)))
    return _assemble(res.results)
```
